# Optimizing a Trainium2 kernel written in Bass

```python
import math
import jax, jax.numpy as jnp
from jax import lax
import numpy as np

D_MODEL = 2048
BATCH = 4
SEQ = 8192
DEPTH = 1
DEC_BATCH = 8
DEC_SEQ = 64
PAST_LEN = 2048

CHUNK = 64
NORM_EPS = 1e-6
MLA_HEADS = 16
Q_LORA = 512
KV_LORA = 512
QK_NOPE = 128
QK_ROPE = 64
V_HEAD = 128
ROPE_THETA = 10000.0
Q_BLOCK = 128
SSD_INNER = 2 * D_MODEL
SSD_HEADDIM = 64
SSD_HEADS = SSD_INNER // SSD_HEADDIM
SSD_GROUPS = 8
SSD_STATE = 128
SSD_CONV = 4
SSD_CHUNK = CHUNK
CONV_DIM = SSD_INNER + 2 * SSD_GROUPS * SSD_STATE
D_FF = 5632
FFN_CONV = 3
OFF_Q = 0
OFF_KV = OFF_Q + Q_LORA
OFF_Z = OFF_KV + KV_LORA + QK_ROPE
OFF_XBC = OFF_Z + SSD_INNER
OFF_DT = OFF_XBC + CONV_DIM
OFF_GATE = OFF_DT + SSD_HEADS
IN_DIM = OFF_GATE + 2 * D_MODEL

kernel_name = 'hybrid_mla_ssd_convffn_stream_step'


def rmsnorm(x, g):
    xf = x.astype(jnp.float32)
    xf = xf * lax.rsqrt(jnp.mean(xf * xf, axis=-1, keepdims=True) + NORM_EPS)
    return (xf * g.astype(jnp.float32)).astype(x.dtype)


def rope(x, pos):
    half = x.shape[-1] // 2
    inv = ROPE_THETA ** (-jnp.arange(half, dtype=jnp.float32) / half)
    ang = pos.astype(jnp.float32)[:, None] * inv[None, :]
    shape = (1, pos.shape[0]) + (1,) * (x.ndim - 3) + (half,)
    cos = jnp.cos(ang).reshape(shape)
    sin = jnp.sin(ang).reshape(shape)
    xf = x.astype(jnp.float32)
    x1, x2 = xf[..., :half], xf[..., half:]
    return jnp.concatenate([x1 * cos - x2 * sin, x1 * sin + x2 * cos], axis=-1).astype(x.dtype)


def causal_dwconv(x, hist, w, b):
    K = w.shape[0]
    L = x.shape[1]
    xp = jnp.concatenate([hist.astype(x.dtype), x], axis=1)
    y = b + xp[:, 0:L] * w[0]
    for k in range(1, K):
        y = y + xp[:, k:k + L] * w[k]
    return y, xp[:, -(K - 1):]


def mla_queries(cq, q_norm_g, w_uq, pos):
    q = jnp.einsum('blr,rhd->blhd', rmsnorm(cq, q_norm_g), w_uq)
    return q[..., :QK_NOPE], rope(q[..., QK_NOPE:], pos)


def mla_latents(kv, kv_norm_g, pos):
    return rmsnorm(kv[..., :KV_LORA], kv_norm_g), rope(kv[..., KV_LORA:], pos)


def mla_expand(ckv, w_ukv):
    kvh = jnp.einsum('bsr,rhd->bshd', ckv, w_ukv)
    return kvh[..., :QK_NOPE], kvh[..., QK_NOPE:]


def mla_attend(qn, qp, kn, kp, v, mask):
    scale = (QK_NOPE + QK_ROPE) ** -0.5
    s = (jnp.einsum('bqhd,bkhd->bhqk', qn, kn, preferred_element_type=jnp.float32)
         + jnp.einsum('bqhr,bkr->bhqk', qp, kp, preferred_element_type=jnp.float32)) * scale
    if mask is not None:
        s = jnp.where(mask, s, -jnp.inf)
    p = jax.nn.softmax(s, axis=-1).astype(v.dtype)
    return jnp.einsum('bhqk,bkhd->bqhd', p, v)


def mla_prompt_attention(qn, qp, kn, kp, v):
    b, S = qn.shape[:2]
    nb = S // Q_BLOCK
    qn_b = jnp.moveaxis(qn.reshape(b, nb, Q_BLOCK, MLA_HEADS, QK_NOPE), 1, 0)
    qp_b = jnp.moveaxis(qp.reshape(b, nb, Q_BLOCK, MLA_HEADS, QK_ROPE), 1, 0)
    kchunk = jnp.arange(S) // CHUNK

    def blk(args):
        i, qn_i, qp_i = args
        qchunk = (i * Q_BLOCK + jnp.arange(Q_BLOCK)) // CHUNK
        mask = kchunk[None, :] <= qchunk[:, None]
        return mla_attend(qn_i, qp_i, kn, kp, v, mask)

    out = lax.map(blk, (jnp.arange(nb), qn_b, qp_b))
    return jnp.moveaxis(out, 0, 1).reshape(b, S, MLA_HEADS, V_HEAD)


def ssd_scan(x, dt, A, Bm, Cm, h0):
    f32 = jnp.float32
    b, L = x.shape[:2]
    Q = SSD_CHUNK if L % SSD_CHUNK == 0 else L
    nc = L // Q
    Hg = SSD_HEADS // SSD_GROUPS
    xdt = (x.astype(f32) * dt[..., None]).reshape(b, nc, Q, SSD_GROUPS, Hg, SSD_HEADDIM)
    a = (dt * A).reshape(b, nc, Q, SSD_GROUPS, Hg)
    Bc = Bm.astype(f32).reshape(b, nc, Q, SSD_GROUPS, SSD_STATE)
    Cc = Cm.astype(f32).reshape(b, nc, Q, SSD_GROUPS, SSD_STATE)
    xs = (jnp.moveaxis(xdt, 1, 0), jnp.moveaxis(a, 1, 0), jnp.moveaxis(Bc, 1, 0), jnp.moveaxis(Cc, 1, 0))
    causal = jnp.tril(jnp.ones((Q, Q), dtype=bool))[None, :, :, None, None]

    def step(h, inp):
        xc, ac, bc, cc = inp
        acum = jnp.cumsum(ac, axis=1)
        seg = acum[:, :, None] - acum[:, None, :]
        Lm = jnp.exp(jnp.where(causal, seg, -jnp.inf))
        cb = jnp.einsum('bign,bjgn->bijg', cc, bc)
        y = jnp.einsum('bijg,bijgh,bjghp->bighp', cb, Lm, xc)
        y = y + jnp.einsum('bign,bghpn->bighp', cc, h) * jnp.exp(acum)[..., None]
        decay = jnp.exp(acum[:, -1:] - acum)
        h = h * jnp.exp(acum[:, -1])[..., None, None] + jnp.einsum('bjgn,bjgh,bjghp->bghpn', bc, decay, xc)
        return h, y

    h_init = h0.astype(f32).reshape(b, SSD_GROUPS, Hg, SSD_HEADDIM, SSD_STATE)
    h, ys = lax.scan(step, h_init, xs)
    y = jnp.moveaxis(ys, 0, 1).reshape(b, L, SSD_HEADS, SSD_HEADDIM)
    return y, h.reshape(b, SSD_HEADS, SSD_HEADDIM, SSD_STATE)


def encoder_layer(x, pos, ckv_past, kpe_past, sconv_hist, ssd_h0, fconv_hist, p):
    b, L, _ = x.shape
    f32 = jnp.float32
    u = rmsnorm(x, p['pre_mix_g'])
    proj = jnp.einsum('bld,de->ble', u, p['w_in'])

    qn, qp = mla_queries(proj[..., OFF_Q:OFF_KV], p['q_norm_g'], p['w_uq'], pos)
    ckv, kpe = mla_latents(proj[..., OFF_KV:OFF_Z], p['kv_norm_g'], pos)
    if ckv_past is None:
        kn, v = mla_expand(ckv, p['w_ukv'])
        att = mla_prompt_attention(qn, qp, kn, kpe, v)
    else:
        ckv_all = jnp.concatenate([ckv_past.astype(ckv.dtype), ckv], axis=1)
        kpe_all = jnp.concatenate([kpe_past.astype(kpe.dtype), kpe], axis=1)
        kn, v = mla_expand(ckv_all, p['w_ukv'])
        att = mla_attend(qn, qp, kn, kpe_all, v, None)
    o_mla = att.reshape(b, L, MLA_HEADS * V_HEAD) @ p['w_o_mla']

    z = proj[..., OFF_Z:OFF_XBC]
    xbc, sconv_tail = causal_dwconv(proj[..., OFF_XBC:OFF_DT], sconv_hist, p['ssd_conv_w'], p['ssd_conv_b'])
    xbc = jax.nn.silu(xbc)
    xs = xbc[..., :SSD_INNER].reshape(b, L, SSD_HEADS, SSD_HEADDIM)
    Bm = xbc[..., SSD_INNER:SSD_INNER + SSD_GROUPS * SSD_STATE].reshape(b, L, SSD_GROUPS, SSD_STATE)
    Cm = xbc[..., SSD_INNER + SSD_GROUPS * SSD_STATE:].reshape(b, L, SSD_GROUPS, SSD_STATE)
    dt = jax.nn.softplus(proj[..., OFF_DT:OFF_GATE].astype(f32) + p['ssd_dt_bias'].astype(f32))
    A = -jnp.exp(p['ssd_A_log'].astype(f32))
    y, h_new = ssd_scan(xs, dt, A, Bm, Cm, ssd_h0)
    y = y + xs.astype(f32) * p['ssd_D'].astype(f32)[:, None]
    y = y.reshape(b, L, SSD_INNER) * jax.nn.silu(z.astype(f32))
    yg = y.reshape(b, L, SSD_GROUPS, SSD_INNER // SSD_GROUPS)
    yg = yg * lax.rsqrt(jnp.mean(yg * yg, axis=-1, keepdims=True) + NORM_EPS)
    y = (yg.reshape(b, L, SSD_INNER) * p['ssd_norm_g'].astype(f32)).astype(x.dtype)
    o_ssd = y @ p['w_o_ssd']

    g_a = jax.nn.sigmoid(proj[..., OFF_GATE:OFF_GATE + D_MODEL])
    g_b = jax.nn.sigmoid(proj[..., OFF_GATE + D_MODEL:])
    mix = (g_a * o_mla + g_b * o_ssd) @ p['w_out']
    x = x + rmsnorm(mix, p['post_mix_g'])

    up = rmsnorm(x, p['pre_ffn_g']) @ p['w_up']
    upc, fconv_tail = causal_dwconv(up, fconv_hist, p['ffn_conv_w'], p['ffn_conv_b'])
    hdn = jax.nn.silu(upc[..., :D_FF]) * upc[..., D_FF:]
    x = x + rmsnorm(hdn @ p['w_down'], p['post_ffn_g'])
    return x, ckv, kpe, sconv_tail, h_new.astype(x.dtype), fconv_tail


def setup_inputs(seed: int = 0) -> dict:
    key = jax.random.key(seed)
    ks = iter(jax.random.split(key, 40))
    f32 = jnp.float32

    def nrm(shape, scale=1.0):
        return jax.random.normal(next(ks), shape, f32) * scale

    def gain(n):
        return 1.0 + 0.02 * nrm((DEPTH, n))

    dt0 = jnp.exp(jax.random.uniform(next(ks), (DEPTH, SSD_HEADS), f32, math.log(1e-3), math.log(1e-1)))
    dt_bias = dt0 + jnp.log(-jnp.expm1(-dt0))
    a_log = jnp.log(jax.random.uniform(next(ks), (DEPTH, SSD_HEADS), f32, 1.0, 16.0))
    return {
        'x_prompt': nrm((BATCH, SEQ, D_MODEL)),
        'x_sample': nrm((DEC_BATCH, DEC_SEQ, D_MODEL)),
        'cache_mla_ckv': nrm((DEPTH, DEC_BATCH, PAST_LEN, KV_LORA)),
        'cache_mla_kpe': nrm((DEPTH, DEC_BATCH, PAST_LEN, QK_ROPE)),
        'state_ssd_conv': nrm((DEPTH, DEC_BATCH, SSD_CONV - 1, CONV_DIM)),
        'state_ssd': nrm((DEPTH, DEC_BATCH, SSD_HEADS, SSD_HEADDIM, SSD_STATE), 0.1),
        'state_ffn_conv': nrm((DEPTH, DEC_BATCH, FFN_CONV - 1, 2 * D_FF)),
        'pre_mix_g': gain(D_MODEL),
        'w_in': nrm((DEPTH, D_MODEL, IN_DIM), D_MODEL ** -0.5),
        'q_norm_g': gain(Q_LORA),
        'w_uq': nrm((DEPTH, Q_LORA, MLA_HEADS, QK_NOPE + QK_ROPE), Q_LORA ** -0.5),
        'kv_norm_g': gain(KV_LORA),
        'w_ukv': nrm((DEPTH, KV_LORA, MLA_HEADS, QK_NOPE + V_HEAD), KV_LORA ** -0.5),
        'ssd_conv_w': nrm((DEPTH, SSD_CONV, CONV_DIM), SSD_CONV ** -0.5),
        'ssd_conv_b': nrm((DEPTH, CONV_DIM), 0.02),
        'ssd_dt_bias': dt_bias,
        'ssd_A_log': a_log,
        'ssd_D': 1.0 + 0.1 * nrm((DEPTH, SSD_HEADS)),
        'ssd_norm_g': gain(SSD_INNER),
        'w_o_mla': nrm((DEPTH, MLA_HEADS * V_HEAD, D_MODEL), (MLA_HEADS * V_HEAD) ** -0.5),
        'w_o_ssd': nrm((DEPTH, SSD_INNER, D_MODEL), SSD_INNER ** -0.5),
        'w_out': nrm((DEPTH, D_MODEL, D_MODEL), D_MODEL ** -0.5),
        'post_mix_g': gain(D_MODEL),
        'pre_ffn_g': gain(D_MODEL),
        'w_up': nrm((DEPTH, D_MODEL, 2 * D_FF), D_MODEL ** -0.5),
        'ffn_conv_w': nrm((DEPTH, FFN_CONV, 2 * D_FF), FFN_CONV ** -0.5),
        'ffn_conv_b': nrm((DEPTH, 2 * D_FF), 0.02),
        'w_down': nrm((DEPTH, D_FF, D_MODEL), D_FF ** -0.5),
        'post_ffn_g': gain(D_MODEL),
    }


def reference(x_prompt, x_sample, cache_mla_ckv, cache_mla_kpe, state_ssd_conv, state_ssd, state_ffn_conv,
              pre_mix_g, w_in, q_norm_g, w_uq, kv_norm_g, w_ukv, ssd_conv_w, ssd_conv_b, ssd_dt_bias,
              ssd_A_log, ssd_D, ssd_norm_g, w_o_mla, w_o_ssd, w_out, post_mix_g, pre_ffn_g, w_up,
              ffn_conv_w, ffn_conv_b, w_down, post_ffn_g):
    bp, S, _ = x_prompt.shape
    bs, Ls, _ = x_sample.shape
    past = cache_mla_ckv.shape[2]
    pos_p = jnp.arange(S, dtype=jnp.int32)
    pos_s = past + jnp.arange(Ls, dtype=jnp.int32)
    hp, hs = x_prompt, x_sample
    ckv_p, kpe_p, sc_p, ss_p, fc_p = [], [], [], [], []
    ckv_s, kpe_s, sc_s, ss_s, fc_s = [], [], [], [], []
    for l in range(DEPTH):
        p = {
            'pre_mix_g': pre_mix_g[l], 'w_in': w_in[l], 'q_norm_g': q_norm_g[l], 'w_uq': w_uq[l],
            'kv_norm_g': kv_norm_g[l], 'w_ukv': w_ukv[l], 'ssd_conv_w': ssd_conv_w[l],
            'ssd_conv_b': ssd_conv_b[l], 'ssd_dt_bias': ssd_dt_bias[l], 'ssd_A_log': ssd_A_log[l],
            'ssd_D': ssd_D[l], 'ssd_norm_g': ssd_norm_g[l], 'w_o_mla': w_o_mla[l], 'w_o_ssd': w_o_ssd[l],
            'w_out': w_out[l], 'post_mix_g': post_mix_g[l], 'pre_ffn_g': pre_ffn_g[l], 'w_up': w_up[l],
            'ffn_conv_w': ffn_conv_w[l], 'ffn_conv_b': ffn_conv_b[l], 'w_down': w_down[l],
            'post_ffn_g': post_ffn_g[l],
        }
        hp, a, b_, c, d, e = encoder_layer(
            hp, pos_p, None, None,
            jnp.zeros((bp, SSD_CONV - 1, CONV_DIM), hp.dtype),
            jnp.zeros((bp, SSD_HEADS, SSD_HEADDIM, SSD_STATE), jnp.float32),
            jnp.zeros((bp, FFN_CONV - 1, 2 * D_FF), hp.dtype), p)
        ckv_p.append(a); kpe_p.append(b_); sc_p.append(c); ss_p.append(d); fc_p.append(e)
        hs, a, b_, c, d, e = encoder_layer(
            hs, pos_s, cache_mla_ckv[l], cache_mla_kpe[l], state_ssd_conv[l], state_ssd[l],
            state_ffn_conv[l], p)
        ckv_s.append(a); kpe_s.append(b_); sc_s.append(c); ss_s.append(d); fc_s.append(e)
    return (hp, hs,
            jnp.stack(ckv_p), jnp.stack(kpe_p), jnp.stack(sc_p), jnp.stack(ss_p), jnp.stack(fc_p),
            jnp.stack(ckv_s), jnp.stack(kpe_s), jnp.stack(sc_s), jnp.stack(ss_s), jnp.stack(fc_s))
```

```python
from contextlib import ExitStack

import numpy as np
import concourse.bass as bass
import concourse.mybir as mybir
from concourse.bass_utils import run_bass_kernel_spmd

F32 = mybir.dt.float32
BF16 = mybir.dt.bfloat16
AF = mybir.ActivationFunctionType
ALU = mybir.AluOpType

D = 2048
SEQ = 8192
DEC_SEQ = 64
PAST = 2048
H = 16
QL = 512
KVL = 512
ROPE = 64
SI = 4096
SH = 64
SP_ = 64
SG = 8
SN = 128
CONV_DIM = 6144
DFF = 5632
EPS = 1e-6
OFF_Q, OFF_KV, OFF_Z, OFF_XBC, OFF_DT, OFF_GATE = 0, 512, 1088, 5184, 11328, 11392
IN_DIM = 15488
SCALE = 192 ** -0.5
NEG = -30000.0
PROMPT_CORES = (0, 1, 4, 5)


class Op:
    __slots__ = ("eng", "fn", "deps", "sem", "val", "needed", "idx")


class Prog:
    ENGS = ("pe", "act", "dve", "pool", "sp")

    def __init__(self, nc):
        self.nc = nc
        self.ops = {e: [] for e in self.ENGS}
        self.res = {}
        self.dma_cnt = {}

    def add(self, eng, fn, reads=(), writes=(), dma=None):
        op = Op()
        op.eng, op.fn, op.sem, op.val, op.needed, op.idx = eng, fn, None, 0, False, 0
        if dma is not None:
            self.dma_cnt[dma] = self.dma_cnt.get(dma, 0) + 1
            op.sem, op.val = dma, 16 * self.dma_cnt[dma]
        stream = ("dma:" + dma) if dma is not None else eng
        deps = []
        for k in reads:
            st = self.res.get(k)
            if st is not None and st[0] is not None:
                deps.append(st[0])
            if st is not None and k.startswith("ps"):
                deps.extend(v for s_, v in st[1].items() if s_ != stream)
        for k in writes:
            st = self.res.get(k)
            if st is not None:
                same = lambda d: dma is None and d.sem is None and d.eng == eng
                if st[0] is not None and not same(st[0]):
                    deps.append(st[0])
                deps.extend(v for v in st[1].values() if not same(v))
        out = []
        for d in deps:
            if d is op or d in out:
                continue
            if d.sem is None and d.eng == eng == "pe":
                continue
            d.needed = True
            out.append(d)
        op.deps = out
        for k in reads:
            st = self.res.get(k)
            if st is None:
                st = self.res[k] = [None, {}]
            st[1][stream] = op
        for k in writes:
            self.res[k] = [op, {}]
        self.ops[eng].append(op)
        return op

    def emit(self, es):
        nc = self.nc
        esem = {e: es.enter_context(nc.semaphore("sem_" + e)) for e in self.ENGS}
        dsem = {k: es.enter_context(nc.semaphore("dsem_" + k)) for k in self.dma_cnt}
        for e in self.ENGS:
            c = 0
            for op in self.ops[e]:
                if op.sem is None and op.needed:
                    c += 1
                    op.idx = c
        final = [(dsem[k], 16 * n) for k, n in self.dma_cnt.items()]

        def run(e, eng):
            waited = {}
            for op in self.ops[e]:
                need = {}
                for d in op.deps:
                    if d.sem is not None:
                        s, v, key = dsem[d.sem], d.val, "d" + d.sem
                    else:
                        s, v, key = esem[d.eng], d.idx, d.eng
                    if key not in need or need[key][1] < v:
                        need[key] = (s, v)
                for key, (s, v) in need.items():
                    if waited.get(key, 0) < v:
                        eng.wait_ge(s, v)
                        waited[key] = v
                ins = op.fn(eng)
                if op.sem is not None:
                    ins.then_inc(dsem[op.sem], 16)
                elif op.needed:
                    ins.then_inc(esem[e], 1)
            if e == "sp":
                for s, v in final:
                    eng.wait_ge(s, v)

        block = es.enter_context(nc.Block())

        @block.tensor
        def _(eng):
            run("pe", eng)

        @block.scalar
        def _(eng):
            run("act", eng)

        @block.vector
        def _(eng):
            run("dve", eng)

        @block.gpsimd
        def _(eng):
            run("pool", eng)

        @block.sync
        def _(eng):
            run("sp", eng)


class Arena:
    def __init__(self, B, name, nbytes):
        self.t = B.sb(name, [128, nbytes // 4], F32)
        self.nbytes = nbytes
        self.off = 0

    def reset(self, off=0):
        self.off = off

    def take(self, shape, dt):
        n = 1
        for s in shape[1:]:
            n *= s
        nb = n * (4 if dt == F32 else 2)
        nb = (nb + 31) // 32 * 32
        assert self.off + nb <= self.nbytes, ("arena overflow", shape, self.off, nb, self.nbytes)
        o4 = self.off // 4
        self.off += nb
        ap = self.t[0:shape[0], o4:o4 + nb // 4]
        if dt == BF16:
            ap = ap.bitcast(BF16)
        ap = ap[:, 0:n]
        if len(shape) == 3:
            ap = ap.rearrange("p (a b) -> p a b", a=shape[1])
        elif len(shape) == 4:
            ap = ap.rearrange("p (a b c) -> p a b c", a=shape[1], b=shape[2])
        return ap


class Builder:
    def __init__(self, nslot=16, do_sample=True, stage=99, dbg=()):
        self.nslot = nslot
        self.do_sample = do_sample
        self.stage = stage
        self.dbg = dbg
        self.nc = bass.Bass("TRN2", target_bir_lowering=False)
        self.P = Prog(self.nc)
        self.es = ExitStack()
        self.psrr = 0
        self.wrr = 0
        self.rr = {}

    def din(self, name, shape, dt=F32):
        return self.nc.dram_tensor(name, list(shape), dt, kind="ExternalInput").ap()

    def dout(self, name, shape, dt=F32):
        return self.nc.dram_tensor(name, list(shape), dt, kind="ExternalOutput").ap()

    def dscr(self, name, shape, dt=BF16):
        return self.nc.dram_tensor(name, list(shape), dt, kind="Internal").ap()

    def sb(self, name, shape, dt=F32):
        return self.es.enter_context(self.nc.sbuf_tensor(name, list(shape), dt))

    def mm(self, out, lhsT, rhs, start, stop, r, w):
        self.P.add("pe", lambda e: e.matmul(out, lhsT=lhsT, rhs=rhs, start=start, stop=stop), r, w)

    def tr(self, out, in_, ident, r, w):
        self.P.add("pe", lambda e: e.transpose(out, in_, ident), r, w)

    def act(self, out, in_, func, r, w, bias=None, scale=None, accum=None):
        kw = {}
        if bias is not None:
            kw["bias"] = bias
        if scale is not None:
            kw["scale"] = scale
        if accum is not None:
            kw["accum_out"] = accum
        self.P.add("act", lambda e: e.activation(out, in_, func, **kw), r, w)

    def tt(self, eng, out, in0, in1, op, r, w):
        self.P.add(eng, lambda e: e.tensor_tensor(out, in0, in1, op), r, w)

    def ts(self, eng, out, in0, s1, s2, op0, op1, r, w):
        if op1 is None:
            self.P.add(eng, lambda e: e.tensor_scalar(out, in0, s1, None, op0), r, w)
        else:
            self.P.add(eng, lambda e: e.tensor_scalar(out, in0, s1, s2, op0, op1), r, w)

    def stt(self, eng, out, in0, scalar, in1, op0, op1, r, w):
        self.P.add(eng, lambda e: e.scalar_tensor_tensor(out, in0, scalar, in1, op0, op1), r, w)

    def cp(self, eng, out, in_, r, w):
        if eng == "act":
            self.P.add("act", lambda e: e.copy(out, in_), r, w)
        else:
            self.P.add(eng, lambda e: e.tensor_copy(out, in_), r, w)

    def recip(self, out, in_, r, w):
        self.P.add("dve", lambda e: e.reciprocal(out, in_), r, w)

    def memset(self, eng, ap, val, w):
        self.P.add(eng, lambda e: e.memset(ap, val), (), w)

    def dma(self, q, out, in_, sem, r, w, **kw):
        self.P.add(q, lambda e: e.dma_start(out=out, in_=in_, **kw), r, w, dma=sem)

    def barrier(self, rkeys=(), wkeys=()):
        t = self.bar_t
        self.P.add("pool", lambda e: e.memset(t[:, 0:1], 0.0), (), tuple(self.KALL) + tuple(rkeys) + tuple(wkeys) + ("bar_t",))

    def bank(self):
        i = self.psrr
        self.psrr = (self.psrr + 1) % 6
        return i

    def rot(self, name, n):
        i = self.rr.get(name, 0)
        self.rr[name] = (i + 1) % n
        return i

    def psb(self, i):
        return self.ps[i][:].bitcast(BF16)

    def build(self):
        with self.es:
            self._declare()
            self._consts()
            self._layout()
            self._convert_weights()
            self._prompt()
            if self.do_sample:
                self._sample()
            self.P.emit(self.es)
        return self.nc

    def _declare(self):
        B = self
        self.xp = B.din("xp", [SEQ, D])
        self.xs = B.din("xs", [DEC_SEQ, D])
        self.c_ckv = B.din("c_ckv", [PAST, KVL])
        self.c_kpe = B.din("c_kpe", [PAST, ROPE])
        self.c_sconv = B.din("c_sconv", [3, CONV_DIM])
        self.c_ssd = B.din("c_ssd", [SH * SP_, SN])
        self.c_fconv = B.din("c_fconv", [2, 2 * DFF])
        self.w = {}
        for n, shp in [("pre_mix_g", [D]), ("w_in", [D, IN_DIM]), ("q_norm_g", [QL]), ("w_uq", [QL, H, 192]),
                       ("kv_norm_g", [KVL]), ("w_ukv", [KVL, H, 256]), ("ssd_conv_w", [4, CONV_DIM]),
                       ("ssd_conv_b", [CONV_DIM]), ("ssd_dt_bias", [SH]), ("ssd_A_log", [SH]), ("ssd_D", [SH]),
                       ("ssd_norm_g", [SI]), ("w_o_mla", [D, D]), ("w_o_ssd", [SI, D]), ("w_out", [D, D]),
                       ("post_mix_g", [D]), ("pre_ffn_g", [D]), ("w_up", [D, 2 * DFF]), ("ffn_conv_w", [3, 2 * DFF]),
                       ("ffn_conv_b", [2 * DFF]), ("w_down", [DFF, D]), ("post_ffn_g", [D])]:
            self.w[n] = B.din(n, shp)
        self.t_cos = B.din("t_cos", [128, 65, 32])
        self.t_sin = B.din("t_sin", [128, 65, 32])
        self.t_amask = B.din("t_amask", [128, 4, 512])
        self.t_misc = B.din("t_misc", [128, 1024])
        self.o_yp = B.dout("o_yp", [SEQ, D])
        self.o_ckvp = B.dout("o_ckvp", [SEQ, KVL])
        self.o_kpep = B.dout("o_kpep", [SEQ, ROPE])
        self.o_sconvp = B.dout("o_sconvp", [3, CONV_DIM])
        self.o_ssdp = B.dout("o_ssdp", [SH * SP_, SN])
        self.o_fconvp = B.dout("o_fconvp", [2, 2 * DFF])
        self.o_ys = B.dout("o_ys", [DEC_SEQ, D])
        self.o_ckvs = B.dout("o_ckvs", [DEC_SEQ, KVL])
        self.o_kpes = B.dout("o_kpes", [DEC_SEQ, ROPE])
        self.o_sconvs = B.dout("o_sconvs", [3, CONV_DIM])
        self.o_ssds = B.dout("o_ssds", [SH * SP_, SN])
        self.o_fconvs = B.dout("o_fconvs", [2, 2 * DFF])
        self.dbg_out = {}
        for name, shp in self.dbg:
            self.dbg_out[name] = B.dout("dbg_" + name, shp)
        self.ps = [self.es.enter_context(self.nc.psum_tensor("ps%d" % i, [128, 512], F32)) for i in range(8)]

    def _wdecl(self, name, src2d, K, N, nb, kpart=None):
        kc = K // 128
        kpart = kpart or kc
        scr = self.dscr("wb_" + name, [N // nb, kc // kpart, 128, kpart, nb])
        self.wt[name] = (scr, src2d, kc, kpart, nb, N // nb)

    def _convert_weights(self):
        B = self
        w = self.w
        self.wt = {}
        self.convkeys = {}
        win = w["w_in"]
        small = {}
        for name, src, lo, hi in (("uqn", w["w_uq"], 0, 128), ("uqr", w["w_uq"], 128, 192),
                                  ("ukn", w["w_ukv"], 0, 128), ("ukv", w["w_ukv"], 128, 256)):
            small[name] = (src, lo, hi)
        order = [("ckv", win[:, OFF_KV:OFF_KV + 512], D, 512, 512, None), ("kpe", win[:, OFF_KV + 512:OFF_Z], D, 64, 64, None),
                 ("q", win[:, OFF_Q:OFF_Q + 512], D, 512, 512, None), "uqn", "uqr", "ukn", "ukv",
                 ("dt", win[:, OFF_DT:OFF_GATE], D, 64, 64, None),
                 ("xs", win[:, OFF_XBC:OFF_XBC + SI], D, SI, 512, None),
                 ("bm", win[:, OFF_XBC + SI:OFF_XBC + SI + 1024], D, 1024, 128, None),
                 ("cm", win[:, OFF_XBC + SI + 1024:OFF_DT], D, 1024, 128, None),
                 ("z", win[:, OFF_Z:OFF_XBC], D, SI, 512, None),
                 ("gate", win[:, OFF_GATE:IN_DIM], D, 2 * D, 512, None),
                 ("omla", w["w_o_mla"], D, D, 512, None), ("ossd", w["w_o_ssd"], SI, D, 512, 16),
                 ("out", w["w_out"], D, D, 512, None), ("up", w["w_up"], D, 2 * DFF, 512, None),
                 ("down", w["w_down"], DFF, D, 512, 11)]
        for item in order:
            if not isinstance(item, str):
                name, src, K, N, nb, kpart = item
                B._wdecl(name, src, K, N, nb, kpart)
        self._conv_items = order
        self._conv_small = small
        self._emit_conversions(0, 12)

    def _emit_conversions(self, lo, hi):
        B = self
        small = self._conv_small
        for item in self._conv_items[lo:hi]:
            if isinstance(item, str):
                name = item
                src, lo, hi = small[name]
                wd = hi - lo
                scr = self.dscr("wb_" + name, [1, 1, 128, 4, H * wd])
                self.wt[name] = (scr, None, 4, 4, H * wd, 1)
                self.convkeys[name] = []
                for rc in range(4):
                    ck = "wc_%s_%d" % (name, rc)
                    self.convkeys[name].append(ck)
                    B.dma("pool", scr[0, 0, :, rc, :].rearrange("p (h d) -> p h d", h=H),
                          src[rc * 128:(rc + 1) * 128, :, lo:hi], "wc_" + name, (), (ck,))
                continue
            name = item[0]
            scr, src, kc, kpart, nb, ncb = self.wt[name]
            self.convkeys[name] = []
            for cb in range(ncb):
                for kp in range(kc // kpart):
                    s_ = src[kp * kpart * 128:(kp + 1) * kpart * 128, cb * nb:(cb + 1) * nb]
                    s_ = s_.rearrange("(kc p) n -> p kc n", p=128)
                    ck = "wc_%s_%d_%d" % (name, cb, kp)
                    self.convkeys[name].append(ck)
                    B.dma("pool", scr[cb, kp], s_, "wc_" + name, (), (ck,))

    def wload(self, name, cb=0, kp=0):
        scr, _, kc, kpart, nb, ncb = self.wt[name]
        i = self.wrr
        self.wrr = (self.wrr + 1) % len(self.wbuf)
        buf = self.wbuf[i]
        key = "wbuf%d" % i
        ap = buf[:, 0:kpart * nb].rearrange("p (k n) -> p k n", k=kpart)
        assert self.convkeys.get(name), ("weight used before its conversion was issued", name)
        self.dma("sp", ap, scr[cb, kp], key, tuple(self.convkeys[name]), (key,))
        return ap, key

    def _consts(self):
        B = self
        w = self.w
        sb = B.sb
        self.bar_t = sb("bar_t", [128, 8], F32)
        self.wbuf = [sb("wbuf%d" % i, [128, 8192], BF16) for i in range(3)]
        self.misc = sb("misc", [128, 1024], F32)
        B.dma("sp", self.misc[:], self.t_misc[:, :], "cst", (), ("cst",))
        self.identf = self.misc[:, 0:128]
        self.tri = self.misc[0:64, 128:192]
        self.amask = sb("amask", [128, 4, 512], BF16)
        B.dma("pool", self.amask[:], self.t_amask[:, :, :], "cstp", (), ("cst",))
        self.identb = sb("identb", [128, 128], BF16)
        B.cp("dve", self.identb[:], self.identf, ("cst",), ("cst2",))
        self.onesb = sb("onesb", [128, 128], BF16)
        B.memset("dve", self.onesb[:], 1.0, ("cst2",))
        self.onesf = sb("onesf", [128, 128], F32)
        B.memset("dve", self.onesf[:], 1.0, ("cst2",))
        self.epsc = sb("epsc", [128, 1], F32)
        B.memset("dve", self.epsc[:], EPS, ("cst2",))
        self.st = sb("st", [128, 16], F32)

        def col(name, src, n):
            t = sb(name, [128, n // 128], F32)
            with self.nc.allow_non_contiguous_dma(reason="tiny one-time gain/bias column loads"):
                B.dma("sp", t[:], src.rearrange("(c p) -> p c", p=128), "cst", (), ("cst",), allow_slow_non_contiguous=True)
            return t
        self.g_premix = col("g_premix", w["pre_mix_g"], D)
        self.g_preffn = col("g_preffn", w["pre_ffn_g"], D)
        self.g_qn = col("g_qn", w["q_norm_g"], QL)
        self.g_ssdn = col("g_ssdn", w["ssd_norm_g"], SI)
        self.sconv_b = col("sconv_b", w["ssd_conv_b"], CONV_DIM)
        self.fconv_b = col("fconv_b", w["ffn_conv_b"], 2 * DFF)
        self.sconv_w = sb("sconv_w", [128, 4, CONV_DIM // 128], F32)
        self.fconv_w = sb("fconv_w", [128, 3, 2 * DFF // 128], F32)
        self.dcol = sb("dcol", [128, 32], F32)
        with self.nc.allow_non_contiguous_dma(reason="tiny one-time conv tap loads"):
            for k in range(4):
                B.dma("sp", self.sconv_w[:, k, :], w["ssd_conv_w"][k].rearrange("(c p) -> p c", p=128), "cst", (), ("cst",), allow_slow_non_contiguous=True)
            for k in range(3):
                B.dma("sp", self.fconv_w[:, k, :], w["ffn_conv_w"][k].rearrange("(c p) -> p c", p=128), "cst", (), ("cst",), allow_slow_non_contiguous=True)
            dsrc = w["ssd_D"].rearrange("(c two) -> two c", two=2)
            for half in range(2):
                B.dma("sp", self.dcol[half * 64:(half + 1) * 64, :], dsrc[half:half + 1, :].broadcast_to([64, 32]),
                      "cst", (), ("cst",), allow_slow_non_contiguous=True)

        def row(name, src, n, parts=128):
            t = sb(name, [parts, n], F32)
            B.dma("sp", t[:], src.rearrange("(o n) -> o n", o=1).broadcast_to([parts, n]), "cst", (), ("cst",))
            return t
        self.g_kvn_r = row("g_kvn_r", w["kv_norm_g"], KVL)
        self.dtb_r = row("dtb_r", w["ssd_dt_bias"], SH, 64)
        self.alog_r = row("alog_r", w["ssd_A_log"], SH, 64)
        self.a_r = sb("a_r", [64, SH], F32)
        B.act(self.a_r[:], self.alog_r[:], AF.Exp, ("cst",), ("cst2",))
        B.ts("dve", self.a_r[:], self.a_r[:], -1.0, None, ALU.mult, None, ("cst2",), ("cst2",))
        self.hst = sb("hst", [128, SI], F32)
        self.hbf = sb("hbf", [128, SI], BF16)
        self.shist = sb("shist", [128, 48, 3], F32)
        self.fhist = sb("fhist", [128, 88, 2], F32)
        self.uT = sb("uT", [128, 16, 512], BF16)

    def _layout(self):
        X = self.arX = Arena(self, "arenaX", 48 * 1024)
        Y = self.arY = Arena(self, "arenaY", 56 * 1024)
        X.reset()
        self.attT = X.take([128, H, 512], BF16)
        o = X.off
        self.qnT = X.take([128, H, 512], BF16)
        self.qpeT = X.take([128, 8, 512], BF16)
        self.qpeb = X.take([128, 4, H * ROPE], BF16)
        X.reset(o)
        self.ynT = X.take([128, 32, 512], BF16)
        X.reset()
        self.hdnT = X.take([128, 44, 512], BF16)
        Y.reset()
        self.xin = [Y.take([128, D], F32) for _ in range(2)]
        self.xnb = Y.take([128, 4, D], BF16)
        self.junk = Y.take([128, D], BF16)
        self.KA1 = ("xin0", "xin1", "xnb", "junk")
        Y.reset()
        self.ckvo = [Y.take([128, KVL], F32) for _ in range(2)]
        self.ckvb = Y.take([128, 4, KVL], BF16)
        self.ckvT = Y.take([128, 4, 512], BF16)
        self.kpeo = Y.take([128, 4, ROPE], F32)
        self.kpeb = Y.take([128, 4, 128], BF16)
        self.kpeT = Y.take([128, 512], BF16)
        self.rtmp = Y.take([128, 4, 256], F32)
        self.cqnb = Y.take([128, 4, QL], BF16)
        self.cqnT = Y.take([128, 4, 512], BF16)
        self.kst = [Y.take([128, 512], BF16) for _ in range(4)]
        self.vst = [Y.take([128, 512], BF16) for _ in range(4)]
        self.junk2 = Y.take([128, 512], BF16)
        self.cs = Y.take([128, 2, 4, 32], F32)
        self.KA2 = ("ckvo0", "ckvo1", "ckvb", "ckvT", "kpeo", "kpeb", "kpeT", "rtmp", "cqnb", "cqnT",
                    "kst0", "kst1", "kst2", "kst3", "vst0", "vst1", "vst2", "vst3", "junk2", "cs")
        Y.reset()
        self.kpeK = Y.take([128, SEQ], BF16)
        self.kp_ = [Y.take([128, 1024], BF16) for _ in range(4)]
        self.vp_ = [Y.take([128, 8, 128], BF16) for _ in range(4)]
        self.pT = [Y.take([128, 512], BF16) for _ in range(3)]
        self.rcp = [Y.take([128, 512], F32) for _ in range(2)]
        self.racc = [[Y.take([128, 512], F32) for _ in range(2)] for _ in range(2)]
        self.KB = ("kpeK", "kp0", "kp1", "kp2", "kp3", "vp0", "vp1", "vp2", "vp3", "pT0", "pT1", "pT2", "rcp0", "rcp1", "racc00", "racc01", "racc10", "racc11")
        Y.reset()
        self.dtt = Y.take([64, 8, 64], F32)
        self.atok = Y.take([64, 8, 64], F32)
        self.acum = Y.take([64, 8, 64], F32)
        self.etot = Y.take([128, 8, 64], F32)
        self.decs = Y.take([64, 8, 64], F32)
        self.dtdec = Y.take([64, 8, 64], F32)
        self.xsT = Y.take([128, 4, 512], F32)
        self.BT = Y.take([128, 512], BF16)
        self.CT = Y.take([128, 512], BF16)
        self.pre = [Y.take([128, 516], F32) for _ in range(2)]
        self.acc = [Y.take([128, 512], F32) for _ in range(2)]
        o = Y.off
        self.Btok = [Y.take([64, 128], BF16) for _ in range(2)]
        self.xdt = [Y.take([64, 512], BF16) for _ in range(2)]
        self.xdtd = [Y.take([64, 512], BF16) for _ in range(2)]
        self.cbm = [Y.take([64, 64], F32) for _ in range(2)]
        self.Ebc = [Y.take([128, 512], F32) for _ in range(2)]
        self.seg = [Y.take([64, 512], F32) for _ in range(2)]
        self.MT = [Y.take([64, 8, 64], BF16) for _ in range(2)]
        self.Ce = [Y.take([128, 8, 64], BF16) for _ in range(2)]
        self.abcs = [Y.take([128, 512], F32) for _ in range(2)]
        self.htmp = Y.take([128, 512], F32)
        Y.reset(o)
        self.zs = [Y.take([128, 512], F32) for _ in range(2)]
        self.sq = [Y.take([128, 512], F32) for _ in range(2)]
        self.rt = Y.take([128, 512], F32)
        self.KC = ("dtt", "atok", "acum", "etot", "decs", "dtdec", "xsT0", "xsT1", "xsT2", "xsT3", "xsT4", "xsT5", "xsT6", "xsT7", "BT", "CT", "pre0", "pre1", "acc0", "acc1",
                   "htmp", "zs0", "zs1", "sq0", "sq1", "rt") + tuple(
                       n + str(i) for n in ("Btok", "xdt", "xdtd", "cbm", "Ebc", "seg", "MT", "Ce", "abcs") for i in range(2))
        Y.reset()
        self.smT = Y.take([128, 16, 512], BF16)
        o = Y.off
        self.sg = [Y.take([128, 512], F32) for _ in range(2)]
        Y.reset(o)
        self.xnb1 = Y.take([128, D], BF16)
        self.t12 = [Y.take([128, 512], F32) for _ in range(2)]
        self.mixsb = Y.take([128, D], F32)
        self.xr = Y.take([128, D], F32)
        self.grow = Y.take([128, D], F32)
        self.KD1 = ("smT", "sgk", "t120", "t121", "mixsb", "xr", "grow")
        Y.reset()
        self.fpre = [Y.take([128, 516], F32) for _ in range(4)]
        self.facc = [Y.take([128, 512], F32) for _ in range(4)]
        self.asil = [Y.take([128, 512], F32) for _ in range(4)]
        self.KD2 = ("fpre0", "fpre1", "fpre2", "fpre3", "facc0", "facc1", "facc2", "facc3", "asil0", "asil1", "asil2", "asil3")
        Y.reset()
        self.dnsb = Y.take([128, 4, D], F32)
        self.xr2 = Y.take([128, D], F32)
        self.grow2 = Y.take([128, D], F32)
        self.KD3 = ("dnsb", "xr2", "grow2")
        Y.reset()
        self.stg = Y.take([128, 32, 128], F32)
        self.KALL = tuple(set(self.KA1 + self.KA2 + self.KB + self.KC + self.KD1 + self.KD2 + self.KD3
                              + ("stg", "attT", "qnT", "qpeT", "qpeb", "ynT", "hdnT")))

    def rstd(self, col, n, TP=128):
        st = self.st
        self.act(st[0:TP, col:col + 1], st[0:TP, col:col + 1], AF.Sqrt, ("st",), ("st",), bias=self.epsc[0:TP, 0:1], scale=1.0 / n)
        self.recip(st[0:TP, col:col + 1], st[0:TP, col:col + 1], ("st",), ("st",))

    def _prompt(self):
        B = self
        self.kT_scr = self.dscr("kT_scr", [H, 128, SEQ])
        self.v_scr = self.dscr("v_scr", [H, 128, SEQ // 128, 128])
        self.kpeT_scr = self.dscr("kpeT_scr", [128, SEQ])
        self.xmid_scr = self.dscr("xmid_scr", [SEQ, D], F32)
        self.acT_scr = self.dscr("acT_scr", [8, SH, 64], F32)
        B.memset("dve", self.hst[:], 0.0, tuple("hst%d" % g for g in range(SG)) + ("hst",))
        B.memset("pool", self.hbf[:], 0.0, tuple("hbf%d" % g for g in range(SG)))
        B.memset("dve", self.shist[:], 0.0, ("shist",))
        B.memset("pool", self.fhist[:], 0.0, ("fhist",))
        for s in range(self.nslot):
            ctx = dict(s=s, T=512, TP=128, xsrc=self.xp[s * 512:(s + 1) * 512, :], pos_tile0=4 * s,
                       o_ckv=self.o_ckvp[s * 512:(s + 1) * 512, :], o_kpe=self.o_kpep[s * 512:(s + 1) * 512, :],
                       o_y=self.o_yp[s * 512:(s + 1) * 512, :], xmid=self.xmid_scr[s * 512:(s + 1) * 512, :],
                       key0=s * 512, tag="p", kT=self.kT_scr, vS=self.v_scr, kpS=self.kpeT_scr, masked=True,
                       last=(s == self.nslot - 1), o_sconv=self.o_sconvp, o_ssd=self.o_ssdp, o_fconv=self.o_fconvp)
            self._slot(ctx)

            if self.stage < 9:
                return
        self._dump_states(self.o_sconvp, self.o_ssdp, self.o_fconvp, "p")

    def _dump_states(self, o_sconv, o_ssd, o_fconv, tag):
        B = self
        B.barrier()
        for q in range(4):
            for k in range(3):
                B.dma("sp", o_sconv[k, q * 1536:(q + 1) * 1536].rearrange("(c p) -> p c", p=128),
                      self.shist[:, q * 12:(q + 1) * 12, k], "o_sconv" + tag, ("shist",), (), allow_slow_non_contiguous=True)
        for q in range(8):
            for k in range(2):
                B.dma("sp", o_fconv[k, q * 1408:(q + 1) * 1408].rearrange("(c p) -> p c", p=128),
                      self.fhist[:, q * 11:(q + 1) * 11, k], "o_fconv" + tag, ("fhist",), (), allow_slow_non_contiguous=True)
        for pc in range(32):
            if pc % 4 == 0:
                b = B.bank()
                pk = "ps%d" % b
            B.tr(self.ps[b][:, (pc % 4) * 128:(pc % 4 + 1) * 128], self.hst[:, pc * 128:(pc + 1) * 128], self.identf,
                 tuple("hst%d" % g for g in range(SG)) + ("hst", "cst"), (pk,))
            if pc % 4 == 3:
                B.cp("act" if (pc // 4) % 2 else "dve", self.stg[:, pc - 3:pc + 1, :],
                     self.ps[b][:, :].rearrange("p (c n) -> p c n", c=4), (pk,), ("stg",))
        B.dma("sp", o_ssd.rearrange("(c p) n -> p c n", p=128), self.stg[:, :, :], "o_ssd" + tag, ("stg",), ())

    def _sample(self):
        B = self
        kT_s = self.dscr("kT_s", [H, 128, PAST + DEC_SEQ])
        v_s = self.dscr("v_s", [H, 128, PAST // 128 + 1, 128])
        kpeT_s = self.dscr("kpeT_s", [128, PAST + DEC_SEQ])
        xmid_s = self.dscr("xmid_s", [DEC_SEQ, D], F32)
        c = dict(s=0, T=DEC_SEQ, TP=64, xsrc=self.xs, pos_tile0=64, o_ckv=self.o_ckvs, o_kpe=self.o_kpes, o_y=self.o_ys,
                 xmid=xmid_s, key0=PAST, tag="s", kT=kT_s, vS=v_s, kpS=kpeT_s, masked=False, last=False)
        B.barrier()
        hk = tuple("hst%d" % g for g in range(SG)) + ("hst",)
        B.dma("sp", self.stg[:, :, :], self.c_ssd.rearrange("(c p) n -> p c n", p=128), "stg_in", (), ("stg",))
        for pc in range(32):
            if pc % 4 == 0:
                b = B.bank()
                pk = "ps%d" % b
            B.tr(self.ps[b][:, (pc % 4) * 128:(pc % 4 + 1) * 128], self.stg[:, pc, :], self.identf, ("stg", "cst"), (pk,))
            if pc % 4 == 3:
                B.cp("act" if (pc // 4) % 2 else "dve", self.hst[:, (pc - 3) * 128:(pc + 1) * 128], self.ps[b][:, :], (pk,), hk)
        B.cp("act", self.hbf[:, :], self.hst[:, :], hk, tuple("hbf%d" % g for g in range(SG)))
        for q in range(4):
            for k in range(3):
                B.dma("sp", self.shist[:, q * 12:(q + 1) * 12, k],
                      self.c_sconv[k, q * 1536:(q + 1) * 1536].rearrange("(c p) -> p c", p=128), "shist_in", (), ("shist",),
                      allow_slow_non_contiguous=True)
        for q in range(8):
            for k in range(2):
                B.dma("sp", self.fhist[:, q * 11:(q + 1) * 11, k],
                      self.c_fconv[k, q * 1408:(q + 1) * 1408].rearrange("(c p) -> p c", p=128), "fhist_in", (), ("fhist",),
                      allow_slow_non_contiguous=True)
        for blk in range(PAST // 512):
            B.barrier()
            for tt in range(4):
                r0 = blk * 512 + tt * 128
                i = B.rot("ckvo", 2)
                co, cok = self.ckvo[i], "ckvo%d" % i
                B.dma("sp", co[:, :], self.c_ckv[r0:r0 + 128, :], "ld_" + cok, (), (cok,))
                B.cp("act" if tt % 2 else "dve", self.ckvb[:, tt, :], co[:, :], (cok,), ("ckvb",))
            B.dma("sp", self.kpeo[:, :, :], self.c_kpe[blk * 512:(blk + 1) * 512, :].rearrange("(t p) d -> p t d", p=128),
                  "ld_kpeo", (), ("kpeo",))
            B.cp("act", self.kpeb[:, :, 0:64], self.kpeo[:, :, :], ("kpeo",), ("kpeb",))
            B.cp("dve", self.kpeb[:, :, 64:128], self.kpeo[:, :, :], ("kpeo",), ("kpeb",))
            for rc in range(4):
                b = B.bank()
                pk = "ps%d" % b
                for tt in range(4):
                    B.tr(B.psb(b)[:, tt * 128:(tt + 1) * 128], self.ckvb[:, tt, rc * 128:(rc + 1) * 128],
                         self.identb[:, :], ("ckvb", "cst2"), (pk,))
                B.cp("act" if rc % 2 else "dve", self.ckvT[:, rc, :], B.psb(b)[:, 0:512], (pk,), ("ckvT",))
            b = B.bank()
            pk = "ps%d" % b
            for tt in range(4):
                B.tr(B.psb(b)[:, tt * 128:(tt + 1) * 128], self.kpeb[:, tt, :], self.identb[:, :], ("kpeb", "cst2"), (pk,))
            B.cp("dve", self.kpeT[:, :], B.psb(b)[:, 0:512], (pk,), ("kpeT",))
            B.dma("pool", kpeT_s[:, blk * 512:(blk + 1) * 512], self.kpeT[:, :], "kpSs", ("kpeT",), ("kpSs",))
            self._s4_expand(c, key0=blk * 512, T=512, TP=128)
        self._slot(c)
        if self.stage < 9:
            return
        self._dump_states(self.o_sconvs, self.o_ssds, self.o_fconvs, "s")

    def _slot(self, c):
        B = self
        B.barrier(self.KD3 + ("hdnT",), self.KA1)
        self._s1_norm(c)
        if self.stage < 2:
            return
        B.barrier(self.KA1, self.KA2 + ("qnT", "qpeT", "qpeb", "attT"))
        self._s2_kv(c)
        if c["tag"] == "p" and c["s"] == 0:
            self._emit_conversions(12, len(self._conv_items))
        if self.stage < 3:
            return
        self._s3_q(c)
        self._s4_expand(c)
        if self.stage < 5:
            return
        B.barrier(self.KA2, self.KB)
        self._s5_attn(c)
        if self.stage < 6:
            return
        B.barrier(self.KB + ("qnT", "qpeT", "qpeb"), self.KC + ("ynT",))
        self._s6_ssd(c)
        if self.stage < 7:
            return
        B.barrier(self.KC, self.KD1)
        self._s7_merge(c)
        if self.stage < 8:
            return
        B.barrier(self.KD1 + ("attT", "ynT"), self.KD2 + ("hdnT",))
        self._s8_ffn_up(c)
        B.barrier(self.KD2, self.KD3)
        self._s9_ffn_down(c)

    def _s1_norm(self, c):
        B = self
        T, TP = c["T"], c["TP"]
        NTT = T // TP
        st, xnb, uT = self.st, self.xnb, self.uT
        for tt in range(NTT):
            i = B.rot("xin", 2)
            xin, xk = self.xin[i], "xin%d" % i
            B.dma("sp", xin[0:TP, :], c["xsrc"][tt * TP:(tt + 1) * TP, :], xk, (), (xk,))
            B.act(self.junk[0:TP, :], xin[0:TP, :], AF.Square, (xk,), ("junk", "st"), accum=st[0:TP, 0:1])
            B.rstd(0, D, TP)
            B.ts("dve", xnb[0:TP, tt, :], xin[0:TP, :], st[0:TP, 0:1], None, ALU.mult, None, (xk, "st"), ("xnb",))
        for kc in range(16):
            b = B.bank()
            pk = "ps%d" % b
            for tt in range(NTT):
                B.tr(B.psb(b)[:, tt * TP:(tt + 1) * TP], xnb[0:TP, tt, kc * 128:(kc + 1) * 128], self.identb[0:TP, 0:TP],
                     ("xnb", "cst2"), (pk,))
            if kc % 2 == 0:
                B.ts("dve", uT[:, kc, 0:T], B.psb(b)[:, 0:T], self.g_premix[:, kc:kc + 1], None, ALU.mult, None,
                     (pk, "cst"), ("uT",))
            else:
                B.act(uT[:, kc, 0:T], B.psb(b)[:, 0:T], AF.Copy, (pk, "cst"), ("uT",), scale=self.g_premix[:, kc:kc + 1])
        if "uT" in self.dbg_out and c["last"]:
            B.dma("pool", self.dbg_out["uT"], self.uT[:], "dbg_uT", ("uT",), ())

    def _s2_kv(self, c):
        B = self
        T, TP, tag = c["T"], c["TP"], c["tag"]
        NTT = T // TP
        st, uT = self.st, self.uT
        B.dma("sp", self.cs[0:TP, 0, 0:NTT, :], self.t_cos[0:TP, c["pos_tile0"]:c["pos_tile0"] + NTT, :], "cs", (), ("cs",))
        B.dma("sp", self.cs[0:TP, 1, 0:NTT, :], self.t_sin[0:TP, c["pos_tile0"]:c["pos_tile0"] + NTT, :], "cs", (), ("cs",))
        wck, kck = B.wload("ckv")
        wkp, kkp = B.wload("kpe")
        for tt in range(NTT):
            ba, bb = B.bank(), B.bank()
            pa, pb = "ps%d" % ba, "ps%d" % bb
            psa, psk = self.ps[ba], self.ps[bb]
            for kc in range(16):
                B.mm(psa[0:TP, :], uT[:, kc, tt * TP:(tt + 1) * TP], wck[:, kc, :], kc == 0, kc == 15, ("uT", kck), (pa,))
            for kc in range(16):
                B.mm(psk[0:TP, 0:ROPE], uT[:, kc, tt * TP:(tt + 1) * TP], wkp[:, kc, :], kc == 0, kc == 15, ("uT", kkp), (pb,))
            B.act(self.junk2[0:TP, 0:KVL], psa[0:TP, :], AF.Square, (pa,), ("junk2", "st"), accum=st[0:TP, 1:2])
            B.rstd(1, KVL, TP)
            i = B.rot("ckvo", 2)
            co, cok = self.ckvo[i], "ckvo%d" % i
            B.stt("dve", co[0:TP, :], psa[0:TP, :], st[0:TP, 1:2], self.g_kvn_r[0:TP, :], ALU.mult, ALU.mult,
                  (pa, "st", "cst"), (cok,))
            B.cp("act", self.ckvb[0:TP, tt, :], co[0:TP, :], (cok,), ("ckvb",))
            B.dma("pool", c["o_ckv"][tt * TP:(tt + 1) * TP, :], co[0:TP, :], "o_" + cok, (cok,), ())
            c_, s_ = self.cs[0:TP, 0, tt, :], self.cs[0:TP, 1, tt, :]
            x1, x2 = psk[0:TP, 0:32], psk[0:TP, 32:64]
            r = self.rtmp
            B.tt("dve", r[0:TP, 0, 0:32], x1, c_, ALU.mult, (pb, "cs"), ("rtmp",))
            B.tt("dve", r[0:TP, 1, 0:32], x2, s_, ALU.mult, (pb, "cs"), ("rtmp",))
            B.tt("dve", r[0:TP, 2, 0:32], x1, s_, ALU.mult, (pb, "cs"), ("rtmp",))
            B.tt("dve", r[0:TP, 3, 0:32], x2, c_, ALU.mult, (pb, "cs"), ("rtmp",))
            B.tt("dve", self.kpeo[0:TP, tt, 0:32], r[0:TP, 0, 0:32], r[0:TP, 1, 0:32], ALU.subtract, ("rtmp",), ("kpeo",))
            B.tt("dve", self.kpeo[0:TP, tt, 32:64], r[0:TP, 2, 0:32], r[0:TP, 3, 0:32], ALU.add, ("rtmp",), ("kpeo",))
            B.cp("act", self.kpeb[0:TP, tt, 0:64], self.kpeo[0:TP, tt, :], ("kpeo",), ("kpeb",))
            B.cp("act", self.kpeb[0:TP, tt, 64:128], self.kpeo[0:TP, tt, :], ("kpeo",), ("kpeb",))
        B.dma("pool", c["o_kpe"].rearrange("(t p) d -> p t d", p=TP), self.kpeo[0:TP, 0:NTT, :], "o_kpe" + tag, ("kpeo",), ())
        for rc in range(4):
            b = B.bank()
            pk = "ps%d" % b
            for tt in range(NTT):
                B.tr(B.psb(b)[:, tt * TP:(tt + 1) * TP], self.ckvb[0:TP, tt, rc * 128:(rc + 1) * 128],
                     self.identb[0:TP, 0:TP], ("ckvb", "cst2"), (pk,))
            B.cp("act" if rc % 2 else "dve", self.ckvT[:, rc, 0:T], B.psb(b)[:, 0:T], (pk,), ("ckvT",))
        b = B.bank()
        pk = "ps%d" % b
        for tt in range(NTT):
            B.tr(B.psb(b)[:, tt * TP:(tt + 1) * TP], self.kpeb[0:TP, tt, :], self.identb[0:TP, 0:TP], ("kpeb", "cst2"), (pk,))
        B.cp("dve", self.kpeT[:, 0:T], B.psb(b)[:, 0:T], (pk,), ("kpeT",))
        B.dma("pool", c["kpS"][:, c["key0"]:c["key0"] + T], self.kpeT[:, 0:T], "kpS" + tag, ("kpeT",), ("kpS" + tag,))
        if "ckvT" in self.dbg_out and c["last"]:
            B.dma("pool", self.dbg_out["ckvT"], self.ckvT[:], "dbg_ckvT", ("ckvT",), ())

    def _s3_q(self, c):
        B = self
        T, TP = c["T"], c["TP"]
        NTT = T // TP
        st, uT = self.st, self.uT
        wq, kq = B.wload("q")
        for tt in range(NTT):
            b = B.bank()
            pk = "ps%d" % b
            for kc in range(16):
                B.mm(self.ps[b][0:TP, :], uT[:, kc, tt * TP:(tt + 1) * TP], wq[:, kc, :], kc == 0, kc == 15, ("uT", kq), (pk,))
            B.act(self.junk2[0:TP, 0:QL], self.ps[b][0:TP, :], AF.Square, (pk,), ("junk2", "st"), accum=st[0:TP, 2:3])
            B.rstd(2, QL, TP)
            B.ts("dve", self.cqnb[0:TP, tt, :], self.ps[b][0:TP, :], st[0:TP, 2:3], None, ALU.mult, None, (pk, "st"), ("cqnb",))
        for rc in range(4):
            b = B.bank()
            pk = "ps%d" % b
            for tt in range(NTT):
                B.tr(B.psb(b)[:, tt * TP:(tt + 1) * TP], self.cqnb[0:TP, tt, rc * 128:(rc + 1) * 128],
                     self.identb[0:TP, 0:TP], ("cqnb", "cst2"), (pk,))
            B.ts("dve", self.cqnT[:, rc, 0:T], B.psb(b)[:, 0:T], self.g_qn[:, rc:rc + 1], None, ALU.mult, None,
                 (pk, "cst"), ("cqnT",))
        wn, kn = B.wload("uqn")
        for h in range(H):
            b = B.bank()
            pk = "ps%d" % b
            for rc in range(4):
                B.mm(self.ps[b][:, 0:T], wn[:, rc, h * 128:(h + 1) * 128], self.cqnT[:, rc, 0:T], rc == 0, rc == 3,
                     ("cqnT", kn), (pk,))
            B.cp("act" if h % 2 else "dve", self.qnT[:, h, 0:T], self.ps[b][:, 0:T], (pk,), ("qnT",))
        wr, kr = B.wload("uqr")
        for tt in range(NTT):
            for cb in range(2):
                b = B.bank()
                pk = "ps%d" % b
                for rc in range(4):
                    B.mm(self.ps[b][0:TP, :], self.cqnT[:, rc, tt * TP:(tt + 1) * TP], wr[:, rc, cb * 512:(cb + 1) * 512],
                         rc == 0, rc == 3, ("cqnT", kr), (pk,))
                pv = self.ps[b][0:TP, :].rearrange("p (h d) -> p h d", h=8)
                x1, x2 = pv[:, :, 0:32], pv[:, :, 32:64]
                c_ = self.cs[0:TP, 0, tt, :].unsqueeze(1).broadcast_to([TP, 8, 32])
                s_ = self.cs[0:TP, 1, tt, :].unsqueeze(1).broadcast_to([TP, 8, 32])
                r = self.rtmp
                rv = [r[0:TP, k, :].rearrange("p (h d) -> p h d", h=8) for k in range(4)]
                B.tt("dve", rv[0], x1, c_, ALU.mult, (pk, "cs"), ("rtmp",))
                B.tt("dve", rv[1], x2, s_, ALU.mult, (pk, "cs"), ("rtmp",))
                B.tt("dve", rv[2], x1, s_, ALU.mult, (pk, "cs"), ("rtmp",))
                B.tt("dve", rv[3], x2, c_, ALU.mult, (pk, "cs"), ("rtmp",))
                qv = self.qpeb[0:TP, tt, cb * 512:(cb + 1) * 512].rearrange("p (h d) -> p h d", h=8)
                B.tt("pool", qv[:, :, 0:32], rv[0], rv[1], ALU.subtract, ("rtmp",), ("qpeb",))
                B.tt("pool", qv[:, :, 32:64], rv[2], rv[3], ALU.add, ("rtmp",), ("qpeb",))
        for hp in range(8):
            b = B.bank()
            pk = "ps%d" % b
            for tt in range(NTT):
                B.tr(B.psb(b)[:, tt * TP:(tt + 1) * TP], self.qpeb[0:TP, tt, hp * 128:(hp + 1) * 128],
                     self.identb[0:TP, 0:TP], ("qpeb", "cst2"), (pk,))
            B.cp("act" if hp % 2 else "dve", self.qpeT[:, hp, 0:T], B.psb(b)[:, 0:T], (pk,), ("qpeT",))

    def _s4_expand(self, c, key0=None, T=None, TP=None):
        B = self
        T = T or c["T"]
        TP = TP or c["TP"]
        key0 = c["key0"] if key0 is None else key0
        tag = c["tag"]
        NTT = T // TP
        wk, kk = B.wload("ukn")
        for h in range(H):
            b = B.bank()
            pk = "ps%d" % b
            for rc in range(4):
                B.mm(self.ps[b][:, 0:T], wk[:, rc, h * 128:(h + 1) * 128], self.ckvT[:, rc, 0:T], rc == 0, rc == 3,
                     ("ckvT", kk), (pk,))
            i = B.rot("kst", 4)
            B.cp("act" if h % 2 else "dve", self.kst[i][:, 0:T], self.ps[b][:, 0:T], (pk,), ("kst%d" % i,))
            B.dma("pool", c["kT"][h, :, key0:key0 + T], self.kst[i][:, 0:T], "kst%d" % i, ("kst%d" % i,), ("kT" + tag,))
        wv, kv = B.wload("ukv")
        for tt in range(NTT):
            kt = (key0 + tt * TP) // 128
            for cb in range(4):
                b = B.bank()
                pk = "ps%d" % b
                for rc in range(4):
                    B.mm(self.ps[b][0:TP, :], self.ckvT[:, rc, tt * TP:(tt + 1) * TP], wv[:, rc, cb * 512:(cb + 1) * 512],
                         rc == 0, rc == 3, ("ckvT", kv), (pk,))
                i = B.rot("vst", 4)
                B.cp("act" if cb % 2 else "dve", self.vst[i][0:TP, :], self.ps[b][0:TP, :], (pk,), ("vst%d" % i,))
                B.dma("pool", c["vS"][4 * cb:4 * cb + 4, 0:TP, kt, :].rearrange("h p d -> p h d"),
                      self.vst[i][0:TP, :].rearrange("p (h d) -> p h d", h=4), "vst%d" % i, ("vst%d" % i,), ("vS" + tag,))

    def _s5_attn(self, c):
        B = self
        T, tag = c["T"], c["tag"]
        nk = c["key0"] + T
        tiles = []
        k = 0
        while k < nk:
            n = min(128, nk - k)
            mi = (k - c["key0"]) // 128 if (c["masked"] and k >= c["key0"]) else None
            tiles.append((k, n, mi))
            k += n
        B.dma("sp", self.kpeK[:, 0:nk], c["kpS"][:, 0:nk], "kpeK", ("kpS" + tag,), ("kpeK",))
        steps = [(h, ti) for h in range(H) for ti in range(len(tiles))]
        state = {}

        def qk(h, ti):
            k0, n, mi = tiles[ti]
            if k0 % 1024 == 0:
                pi = B.rot("kvp", 4)
                npk = min(1024, nk - k0)
                B.dma("sp", self.kp_[pi][:, 0:npk], c["kT"][h, :, k0:k0 + npk], "kp%d" % pi, ("kT" + tag,), ("kp%d" % pi,))
                nkt = (npk + 127) // 128
                pv = min(128, npk)
                B.dma("sp", self.vp_[pi][0:pv, 0:nkt, :], c["vS"][h, 0:pv, k0 // 128:k0 // 128 + nkt, :], "vp%d" % pi,
                      ("vS" + tag,), ("vp%d" % pi,))
                state["pi"] = pi
            pi = state["pi"]
            hb = 64 * (h % 2)
            kl = k0 % 1024
            b = B.rot("abank", 4)
            pk = "ps%d" % b
            B.mm(self.ps[b][0:n, 0:T], self.kp_[pi][:, kl:kl + n], self.qnT[:, h, 0:T], True, False, ("kp%d" % pi, "qnT"), (pk,))
            B.mm(self.ps[b][0:n, 0:T], self.kpeK[hb:hb + 64, k0:k0 + n], self.qpeT[hb:hb + 64, h // 2, 0:T], False, True,
                 ("kpeK", "qpeT"), (pk,))
            return (b, pk, pi, kl)

        nxt = qk(*steps[0])
        for si, (h, ti) in enumerate(steps):
            b, pk, pi, kl = nxt
            if si + 1 < len(steps):
                nxt = qk(*steps[si + 1])
            k0, n, mi = tiles[ti]
            bo, br = (4, 5) if h % 2 == 0 else (6, 7)
            po, pr = "ps%d" % bo, "ps%d" % br
            i = B.rot("pT", 3)
            pT, ptk = self.pT[i], "pT%d" % i
            B.act(pT[0:n, 0:T], self.ps[b][0:n, 0:T], AF.Exp, (pk,), (ptk,), scale=SCALE)
            if mi is not None:
                B.tt("pool", pT[0:n, 0:T], pT[0:n, 0:T], self.amask[0:n, mi, 0:T], ALU.mult, (ptk, "cst"), (ptk,))
            first, last = ti == 0, ti == len(tiles) - 1
            B.mm(self.ps[bo][:, 0:T], self.vp_[pi][0:n, kl // 128, :], pT[0:n, 0:T], first, last, ("vp%d" % pi, ptk), (po,))
            hp_ = h % 2
            e_ = 1 if (ti % 3 == 2 and len(tiles) > 1) else 0
            acc, acck = self.racc[hp_][e_], "racc%d%d" % (hp_, e_)
            eng_ = "pool" if e_ else "dve"
            if n < 128 or ti < 2 or (ti == 2 and e_ == 1):
                if (ti == 0 and e_ == 0) or (ti == 2 and e_ == 1):
                    if n < 128:
                        B.memset(eng_, acc[:, 0:T], 0.0, (acck,))
                    B.cp(eng_, acc[0:n, 0:T], pT[0:n, 0:T], (ptk,), (acck,))
                else:
                    B.tt(eng_, acc[0:n, 0:T], acc[0:n, 0:T], pT[0:n, 0:T], ALU.add, (acck, ptk), (acck,))
            else:
                B.tt(eng_, acc[0:n, 0:T], acc[0:n, 0:T], pT[0:n, 0:T], ALU.add, (acck, ptk), (acck,))
            if last:
                two = len(tiles) > 2
                B.mm(self.ps[br][:, 0:T], self.onesf[:, :], self.racc[hp_][0][:, 0:T], True, not two, ("racc%d0" % hp_, "cst2"), (pr,))
                if two:
                    B.mm(self.ps[br][:, 0:T], self.onesf[:, :], self.racc[hp_][1][:, 0:T], False, True, ("racc%d1" % hp_, "cst2"), (pr,))
                ri = B.rot("rcp", 2)
                rc_ = self.rcp[ri]
                B.recip(rc_[:, 0:T], self.ps[br][:, 0:T], (pr,), ("rcp%d" % ri,))
                B.tt("dve", self.attT[:, h, 0:T], self.ps[bo][:, 0:T], rc_[:, 0:T], ALU.mult, (po, "rcp%d" % ri), ("attT",))
        if "attT" in self.dbg_out and c["last"]:
            B.dma("pool", self.dbg_out["attT"], self.attT[:], "dbg_attT", ("attT",), ())

    def _conv(self, b, T, K, wts, bias, hist, fc, pre, prek, acc, acck, eng):
        B = self
        pk = "ps%d" % b
        H_ = K - 1
        B.cp("act", pre[:, H_:H_ + T], self.ps[b][:, 0:T], (pk,), (prek,))
        B.cp("pool", pre[:, 0:H_], hist[:, fc, :], (hist_key(hist, self),), (prek,))
        B.ts(eng, acc[:, 0:T], pre[:, H_:H_ + T], wts[:, K - 1, fc:fc + 1], bias[:, fc:fc + 1], ALU.mult, ALU.add,
             (prek, "cst"), (acck,))
        for k in range(K - 1):
            B.stt(eng, acc[:, 0:T], pre[:, k:k + T], wts[:, k, fc:fc + 1], acc[:, 0:T], ALU.mult, ALU.add,
                  (prek, acck, "cst"), (acck,))
        B.cp("pool", hist[:, fc, :], pre[:, T:T + H_], (prek,), (hist_key(hist, self),))

    def _s6_ssd(self, c):
        B = self
        T = c["T"]
        NCH = T // 64
        uT = self.uT
        dtt, atok, acum, etot, decs, dtdec = self.dtt, self.atok, self.acum, self.etot, self.decs, self.dtdec
        wdt, kdt = B.wload("dt")
        b = B.bank()
        pk = "ps%d" % b
        for ch in range(NCH):
            for kc in range(16):
                B.mm(self.ps[b][0:64, ch * 64:(ch + 1) * 64], uT[:, kc, ch * 64:(ch + 1) * 64], wdt[:, kc, :], kc == 0, kc == 15,
                     ("uT", kdt), (pk,))
        pv = self.ps[b][0:64, 0:NCH * 64].rearrange("p (c h) -> p c h", c=NCH)
        B.tt("dve", dtt[:, 0:NCH, :], pv, self.dtb_r[:, :].unsqueeze(1).broadcast_to([64, NCH, 64]), ALU.add, (pk, "cst"), ("dtt",))
        B.ts("dve", dtt[:, 0:NCH, :], dtt[:, 0:NCH, :], 30.0, None, ALU.min, None, ("dtt",), ("dtt",))
        B.act(dtt[:, 0:NCH, :], dtt[:, 0:NCH, :], AF.Exp, ("dtt",), ("dtt",))
        B.act(dtt[:, 0:NCH, :], dtt[:, 0:NCH, :], AF.Ln, ("dtt",), ("dtt",), bias=self.onesf[0:64, 0:1])
        B.tt("dve", atok[:, 0:NCH, :], dtt[:, 0:NCH, :], self.a_r[:, :].unsqueeze(1).broadcast_to([64, NCH, 64]), ALU.mult,
             ("dtt", "cst2"), ("atok",))
        b = B.bank()
        pk = "ps%d" % b
        for ch in range(NCH):
            B.mm(self.ps[b][0:64, ch * 64:(ch + 1) * 64], self.tri, atok[:, ch, :], True, True, ("atok", "cst"), (pk,))
        B.cp("dve", acum[:, 0:NCH, :], self.ps[b][0:64, 0:NCH * 64].rearrange("p (c h) -> p c h", c=NCH), (pk,), ("acum",))
        b = B.bank()
        pk = "ps%d" % b
        for ch in range(NCH):
            B.mm(self.ps[b][:, ch * 64:(ch + 1) * 64], self.onesf[0:64, :], atok[:, ch, :], True, True, ("atok", "cst2"), (pk,))
        pv = self.ps[b][:, 0:NCH * 64].rearrange("p (c h) -> p c h", c=NCH)
        B.act(etot[:, 0:NCH, :], pv, AF.Exp, (pk,), ("etot",))
        B.tt("dve", decs[:, 0:NCH, :], pv[0:64], acum[:, 0:NCH, :], ALU.subtract, (pk, "acum"), ("decs",))
        B.act(decs[:, 0:NCH, :], decs[:, 0:NCH, :], AF.Exp, ("decs",), ("decs",))
        B.tt("dve", dtdec[:, 0:NCH, :], decs[:, 0:NCH, :], dtt[:, 0:NCH, :], ALU.mult, ("decs", "dtt"), ("dtdec",))
        b = B.bank()
        pk = "ps%d" % b
        for ch in range(NCH):
            B.tr(self.ps[b][0:64, ch * 64:(ch + 1) * 64], acum[:, ch, :], self.identf[0:64, 0:64], ("acum", "cst"), (pk,))
        B.cp("act", atok[:, 0:NCH, :], self.ps[b][0:64, 0:NCH * 64].rearrange("p (c h) -> p c h", c=NCH), (pk,), ("atok",))
        B.dma("pool", self.acT_scr[0:NCH].rearrange("c h i -> h c i"), atok[:, 0:NCH, :], "acT_w", ("atok",), ("acT",))

        xk_all = tuple("xsT%d" % ch for ch in range(8))
        for g in range(SG):
            wx, kx = B.wload("xs", g)
            wb_, kb_ = B.wload("bm", g)
            wc_, kc_ = B.wload("cm", g)
            plan = [(wx, kx, j * 128, 4 * g + j, ("xs", j)) for j in range(4)]
            plan += [(wb_, kb_, 0, 32 + g, ("B", 0)), (wc_, kc_, 0, 40 + g, ("C", 0))]
            for (wt_, wk_, cl, fc, (kind, j)) in plan:
                b = B.bank()
                pk = "ps%d" % b
                for kc in range(16):
                    B.mm(self.ps[b][:, 0:T], wt_[:, kc, cl:cl + 128], uT[:, kc, 0:T], kc == 0, kc == 15, ("uT", wk_), (pk,))
                i = B.rot("pre", 2)
                eng = "dve"
                B._conv(b, T, 4, self.sconv_w, self.sconv_b, self.shist, fc, self.pre[i], "pre%d" % i, self.acc[i], "acc%d" % i, eng)
                if kind == "xs":
                    B.act(self.xsT[:, j, 0:T], self.acc[i][:, 0:T], AF.Silu, ("acc%d" % i,), xk_all)
                elif kind == "B":
                    B.act(self.BT[:, 0:T], self.acc[i][:, 0:T], AF.Silu, ("acc%d" % i,), ("BT",))
                else:
                    B.act(self.CT[:, 0:T], self.acc[i][:, 0:T], AF.Silu, ("acc%d" % i,), ("CT",))
            hs = slice(g * 512, (g + 1) * 512)
            hk, hbk = "hst%d" % g, "hbf%d" % g

            def prep(ch, g=g):
                q = ch % 2
                cs_ = slice(ch * 64, (ch + 1) * 64)
                K_ = lambda n: n + str(q)
                b = B.bank()
                pk = "ps%d" % b
                B.mm(self.ps[b][0:64, 0:64], self.BT[:, cs_], self.CT[:, cs_], True, True, ("BT", "CT"), (pk,))
                B.tt("dve", self.cbm[q][:, :], self.ps[b][0:64, 0:64], self.tri, ALU.mult, (pk, "cst"), (K_("cbm"),))
                ab, abk = self.abcs[q], K_("abcs")
                B.dma("sp", ab[:, :], self.acT_scr[ch, 8 * g:8 * g + 8, :].rearrange("h i -> (h i)").rearrange("(o n) -> o n", o=1)
                      .broadcast_to([128, 512]), abk, ("acT",), (abk,))
                B.act(self.Ebc[q][:, :], ab[:, :], AF.Exp, (abk,), (K_("Ebc"),))
                B.tt("dve", self.seg[q][:, :].rearrange("p (h i) -> p h i", h=8), ab[0:64, :].rearrange("p (h i) -> p h i", h=8),
                     acum[:, ch, 8 * g:8 * g + 8].unsqueeze(2).broadcast_to([64, 8, 64]), ALU.subtract, (abk, "acum"), (K_("seg"),))
                B.ts("dve", self.seg[q][:, :], self.seg[q][:, :], 0.0, None, ALU.min, None, (K_("seg"),), (K_("seg"),))
                B.act(self.seg[q][:, :], self.seg[q][:, :], AF.Exp, (K_("seg"),), (K_("seg"),))
                B.tt("dve", self.MT[q][:, :, :], self.seg[q][:, :].rearrange("p (h i) -> p h i", h=8),
                     self.cbm[q][:, :].unsqueeze(1).broadcast_to([64, 8, 64]), ALU.mult, (K_("seg"), K_("cbm")), (K_("MT"),))
                B.tt("pool", self.Ce[q][:, :, :], self.Ebc[q][:, :].rearrange("p (h i) -> p h i", h=8),
                     self.CT[:, cs_].unsqueeze(1).broadcast_to([128, 8, 64]), ALU.mult, (K_("Ebc"), "CT"), (K_("Ce"),))
                bt = B.bank()
                pt = "ps%d" % bt
                for j in range(4):
                    B.tr(self.ps[bt][0:64, j * 128:(j + 1) * 128], self.xsT[:, j, cs_], self.identf, ("xsT%d" % ch, "cst"), (pt,))
                pv = self.ps[bt][0:64, :].rearrange("p (h d) -> p h d", h=8)
                B.tt("dve", self.xdt[q][:, :].rearrange("p (h d) -> p h d", h=8), pv,
                     dtt[:, ch, 8 * g:8 * g + 8].unsqueeze(2).broadcast_to([64, 8, 64]), ALU.mult, (pt, "dtt"), (K_("xdt"),))
                B.tt("dve", self.xdtd[q][:, :].rearrange("p (h d) -> p h d", h=8), pv,
                     dtdec[:, ch, 8 * g:8 * g + 8].unsqueeze(2).broadcast_to([64, 8, 64]), ALU.mult, (pt, "dtdec"), (K_("xdtd"),))
                b2 = B.bank()
                p2 = "ps%d" % b2
                B.tr(B.psb(b2)[0:64, 0:128], self.BT[:, cs_], self.identb[:, :], ("BT", "cst2"), (p2,))
                B.cp("act", self.Btok[q][:, :], B.psb(b2)[0:64, 0:128], (p2,), (K_("Btok"),))

            def yst(ch, g=g):
                q = ch % 2
                cs_ = slice(ch * 64, (ch + 1) * 64)
                K_ = lambda n: n + str(q)
                by = B.bank()
                py = "ps%d" % by
                for hh in range(8):
                    h = 8 * g + hh
                    o_ = self.ps[by][64 * (hh % 2):64 * (hh % 2) + 64, (hh // 2) * 64:(hh // 2 + 1) * 64]
                    B.mm(o_, self.xdt[q][:, hh * 64:(hh + 1) * 64], self.MT[q][:, hh, :], True, False, (K_("xdt"), K_("MT")), (py,))
                    B.mm(o_, self.hbf[:, h * 64:(h + 1) * 64], self.Ce[q][:, hh, :], False, True, (hbk, K_("Ce")), (py,))
                B.tt("pool", self.xsT[:, :, cs_], self.xsT[:, :, cs_], self.dcol[:, 4 * g:4 * g + 4].unsqueeze(2).broadcast_to([128, 4, 64]),
                     ALU.mult, ("xsT%d" % ch, "cst"), ("xsT%d" % ch,))
                B.tt("dve", self.xsT[:, :, cs_], self.xsT[:, :, cs_], self.ps[by][:, 0:256].rearrange("p (j i) -> p j i", j=4),
                     ALU.add, ("xsT%d" % ch, py), ("xsT%d" % ch,))
                bs_ = B.bank()
                ps_ = "ps%d" % bs_
                B.mm(self.ps[bs_][:, :], self.Btok[q][:, :], self.xdtd[q][:, :], True, True, (K_("Btok"), K_("xdtd")), (ps_,))
                B.tt("pool", self.htmp[:, :].rearrange("p (h d) -> p h d", h=8), self.hst[:, hs].rearrange("p (h d) -> p h d", h=8),
                     etot[:, ch, 8 * g:8 * g + 8].unsqueeze(2).broadcast_to([128, 8, 64]), ALU.mult, (hk, "etot"), ("htmp",))
                B.tt("dve", self.hst[:, hs], self.htmp[:, :], self.ps[bs_][:, :], ALU.add, ("htmp", ps_), (hk,))
                B.cp("act", self.hbf[:, hs], self.hst[:, hs], (hk,), (hbk,))

            prep(0)
            for ch in range(NCH):
                if ch + 1 < NCH:
                    prep(ch + 1)
                yst(ch)
            if "yscan" in self.dbg_out and c["last"]:
                B.dma("pool", self.dbg_out["yscan"][:, 4 * g:4 * g + 4, :], self.xsT[:, :, :], "dbg_yscan", xk_all, ())
            wz, kz = B.wload("z", g)
            bn = B.bank()
            pn = "ps%d" % bn
            for j in range(4):
                b = B.bank()
                pk = "ps%d" % b
                for kc in range(16):
                    B.mm(self.ps[b][:, 0:T], wz[:, kc, j * 128:(j + 1) * 128], uT[:, kc, 0:T], kc == 0, kc == 15, ("uT", kz), (pk,))
                i = B.rot("zs", 2)
                B.act(self.zs[i][:, 0:T], self.ps[b][:, 0:T], AF.Silu, (pk,), ("zs%d" % i,))
                B.tt("dve", self.xsT[:, j, 0:T], self.xsT[:, j, 0:T], self.zs[i][:, 0:T], ALU.mult, xk_all + ("zs%d" % i,), xk_all)
                B.act(self.sq[i][:, 0:T], self.xsT[:, j, 0:T], AF.Square, xk_all, ("sq%d" % i,))
                B.mm(self.ps[bn][:, 0:T], self.onesf[:, :], self.sq[i][:, 0:T], j == 0, j == 3, ("sq%d" % i, "cst2"), (pn,))
            B.act(self.rt[:, 0:T], self.ps[bn][:, 0:T], AF.Sqrt, (pn,), ("rt",), bias=self.epsc[:, 0:1], scale=1.0 / 512)
            B.recip(self.rt[:, 0:T], self.rt[:, 0:T], ("rt",), ("rt",))
            for j in range(4):
                pc = 4 * g + j
                B.stt("dve", self.ynT[:, pc, 0:T], self.xsT[:, j, 0:T], self.g_ssdn[:, pc:pc + 1], self.rt[:, 0:T],
                      ALU.mult, ALU.mult, xk_all + ("rt", "cst"), ("ynT",))
        if "ynT" in self.dbg_out and c["last"]:
            B.dma("pool", self.dbg_out["ynT"], self.ynT[:], "dbg_ynT", ("ynT",), ())

    def _s7_merge(self, c):
        B = self
        T, TP = c["T"], c["TP"]
        NTT = T // TP
        uT, st = self.uT, self.st
        t1buf = self.mixsb[:, :].rearrange("p (j t) -> p j t", j=4)
        sga = self.xr[:, :].rearrange("p (j t) -> p j t", j=4)
        sgb = self.grow[:, :].rearrange("p (j t) -> p j t", j=4)
        for cb in range(4):
            wga, kga = B.wload("gate", cb)
            for j in range(4):
                cl = j * 128
                b3 = B.bank()
                p3 = "ps%d" % b3
                for kc in range(16):
                    B.mm(self.ps[b3][:, 0:T], wga[:, kc, cl:cl + 128], uT[:, kc, 0:T], kc == 0, kc == 15, ("uT", kga), (p3,))
                B.act(sga[:, j, 0:T], self.ps[b3][:, 0:T], AF.Sigmoid, (p3,), ("xr",))
            wm, km = B.wload("omla", cb)
            for j in range(4):
                cl = j * 128
                b1 = B.bank()
                p1 = "ps%d" % b1
                for kc in range(16):
                    B.mm(self.ps[b1][:, 0:T], wm[:, kc, cl:cl + 128], self.attT[:, kc, 0:T], kc == 0, kc == 15, ("attT", km), (p1,))
                B.tt("dve", t1buf[:, j, 0:T], self.ps[b1][:, 0:T], sga[:, j, 0:T], ALU.mult, (p1, "xr"), ("mixsb",))
            wgb, kgb = B.wload("gate", 4 + cb)
            for j in range(4):
                cl = j * 128
                b4 = B.bank()
                p4 = "ps%d" % b4
                for kc in range(16):
                    B.mm(self.ps[b4][:, 0:T], wgb[:, kc, cl:cl + 128], uT[:, kc, 0:T], kc == 0, kc == 15, ("uT", kgb), (p4,))
                B.act(sgb[:, j, 0:T], self.ps[b4][:, 0:T], AF.Sigmoid, (p4,), ("grow",))
            banks = [B.bank() for _ in range(4)]
            for kp in range(2):
                ws, ks = B.wload("ossd", cb, kp)
                for j in range(4):
                    cl = j * 128
                    b2 = banks[j]
                    for kc in range(16):
                        B.mm(self.ps[b2][:, 0:T], ws[:, kc, cl:cl + 128], self.ynT[:, kp * 16 + kc, 0:T],
                             kp == 0 and kc == 0, kp == 1 and kc == 15, ("ynT", ks), ("ps%d" % b2,))
            for j in range(4):
                dc = 4 * cb + j
                b2 = banks[j]
                i = B.rot("t12", 2)
                B.tt("dve", self.t12[i][:, 0:T], self.ps[b2][:, 0:T], sgb[:, j, 0:T], ALU.mult, ("ps%d" % b2, "grow"), ("t12%d" % i,))
                B.tt("pool", self.smT[:, dc, 0:T], t1buf[:, j, 0:T], self.t12[i][:, 0:T], ALU.add, ("mixsb", "t12%d" % i), ("smT",))
        if "smT" in self.dbg_out and c["last"]:
            B.dma("pool", self.dbg_out["smT"], self.smT[:], "dbg_smT", ("smT",), ())
        B.dma("sp", self.grow[:, :], self.w["post_mix_g"].rearrange("(o n) -> o n", o=1).broadcast_to([128, D]), "grow", (), ("grow",))
        for tt in range(NTT):
            tsl = slice(tt * TP, (tt + 1) * TP)
            B.dma("sp", self.xr[0:TP, :], c["xsrc"][tsl, :], "xr", (), ("xr",))
            for cb in range(4):
                wo, ko = B.wload("out", cb)
                b = B.bank()
                pk = "ps%d" % b
                for kc in range(16):
                    B.mm(self.ps[b][0:TP, :], self.smT[:, kc, tsl], wo[:, kc, :], kc == 0, kc == 15, ("smT", ko), (pk,))
                B.cp("act", self.mixsb[0:TP, cb * 512:(cb + 1) * 512], self.ps[b][0:TP, :], (pk,), ("mixsb",))
            B.act(self.xnb1[0:TP, :], self.mixsb[0:TP, :], AF.Square, ("mixsb",), ("sgk", "st"), accum=st[0:TP, 3:4])
            B.rstd(3, D, TP)
            B.stt("dve", self.mixsb[0:TP, :], self.mixsb[0:TP, :], st[0:TP, 3:4], self.grow[0:TP, :], ALU.mult, ALU.mult,
                  ("mixsb", "st", "grow"), ("mixsb",))
            B.tt("dve", self.xr[0:TP, :], self.xr[0:TP, :], self.mixsb[0:TP, :], ALU.add, ("xr", "mixsb"), ("xr",))
            B.dma("pool", c["xmid"][tsl, :], self.xr[0:TP, :], "xmid_w", ("xr",), ("xmid" + c["tag"],))
            B.act(self.xnb1[0:TP, :], self.xr[0:TP, :], AF.Square, ("xr",), ("sgk", "st"), accum=st[0:TP, 4:5])
            B.rstd(4, D, TP)
            B.ts("dve", self.xnb1[0:TP, :], self.xr[0:TP, :], st[0:TP, 4:5], None, ALU.mult, None, ("xr", "st"), ("sgk",))
            for half in range(2):
                b = B.bank()
                pk = "ps%d" % b
                for k8 in range(8):
                    kc = half * 8 + k8
                    B.tr(B.psb(b)[:, k8 * TP:(k8 + 1) * TP], self.xnb1[0:TP, kc * 128:(kc + 1) * 128], self.identb[0:TP, 0:TP],
                         ("sgk", "cst2"), (pk,))
                B.tt("dve", uT[:, half * 8:half * 8 + 8, tsl], B.psb(b)[:, 0:8 * TP].rearrange("p (k t) -> p k t", k=8),
                     self.g_preffn[:, half * 8:half * 8 + 8].unsqueeze(2).broadcast_to([128, 8, TP]), ALU.mult, (pk, "cst"), ("uT",))
        if "xnT" in self.dbg_out and c["last"]:
            B.dma("pool", self.dbg_out["xnT"], self.uT[:], "dbg_xnT", ("uT",), ())

    def _s8_ffn_up(self, c):
        B = self
        T = c["T"]
        uT = self.uT
        for cb1 in range(11):
            for half in range(2):
                wt_, wk_ = B.wload("up", cb1 + 11 * half)
                for j in range(4):
                    fc = 4 * cb1 + j
                    cl = j * 128
                    b = B.bank()
                    pk = "ps%d" % b
                    for kc in range(16):
                        B.mm(self.ps[b][:, 0:T], wt_[:, kc, cl:cl + 128], uT[:, kc, 0:T], kc == 0, kc == 15, ("uT", wk_), (pk,))
                    i = B.rot("fpre", 4)
                    B._conv(b, T, 3, self.fconv_w, self.fconv_b, self.fhist, fc + 44 * half, self.fpre[i], "fpre%d" % i,
                            self.facc[i], "facc%d" % i, "dve")
                    if half == 0:
                        B.act(self.asil[j][:, 0:T], self.facc[i][:, 0:T], AF.Silu, ("facc%d" % i,), ("asil%d" % j,))
                    else:
                        B.tt("pool", self.hdnT[:, fc, 0:T], self.asil[j][:, 0:T], self.facc[i][:, 0:T], ALU.mult,
                             ("asil%d" % j, "facc%d" % i), ("hdnT",))
        if "hdnT" in self.dbg_out and c["last"]:
            B.dma("pool", self.dbg_out["hdnT"], self.hdnT[:], "dbg_hdnT", ("hdnT",), ())

    def _s9_ffn_down(self, c):
        B = self
        T, TP, tag = c["T"], c["TP"], c["tag"]
        NTT = T // TP
        st = self.st
        B.dma("sp", self.grow2[:, :], self.w["post_ffn_g"].rearrange("(o n) -> o n", o=1).broadcast_to([128, D]), "grow2", (), ("grow2",))
        for cb in range(4):
            banks = [B.bank() for _ in range(NTT)]
            for kp in range(4):
                wd, kd = B.wload("down", cb, kp)
                for tt in range(NTT):
                    b = banks[tt]
                    for kcl in range(11):
                        fc = kp * 11 + kcl
                        B.mm(self.ps[b][0:TP, :], self.hdnT[:, fc, tt * TP:(tt + 1) * TP], wd[:, kcl, :], fc == 0, fc == 43,
                             ("hdnT", kd), ("ps%d" % b,))
            for tt in range(NTT):
                b = banks[tt]
                B.cp("act" if tt % 2 else "dve", self.dnsb[0:TP, tt, cb * 512:(cb + 1) * 512], self.ps[b][0:TP, :], ("ps%d" % b,), ("dnsb",))
        for tt in range(NTT):
            tsl = slice(tt * TP, (tt + 1) * TP)
            B.dma("sp", self.xr2[0:TP, :], c["xmid"][tsl, :], "xr2", ("xmid" + tag,), ("xr2",))
            B.act(self.hdnT[0:TP, 0:4, :].rearrange("p a b -> p (a b)"), self.dnsb[0:TP, tt, :], AF.Square, ("dnsb",), ("hdnT", "st"),
                  accum=st[0:TP, 5:6])
            B.rstd(5, D, TP)
            B.stt("dve", self.dnsb[0:TP, tt, :], self.dnsb[0:TP, tt, :], st[0:TP, 5:6], self.grow2[0:TP, :], ALU.mult, ALU.mult,
                  ("dnsb", "st", "grow2"), ("dnsb",))
            B.tt("dve", self.xr2[0:TP, :], self.xr2[0:TP, :], self.dnsb[0:TP, tt, :], ALU.add, ("xr2", "dnsb"), ("xr2",))
            B.dma("pool", c["o_y"][tsl, :], self.xr2[0:TP, :], "o_y" + tag, ("xr2",), ())


def hist_key(hist, B):
    return "shist" if hist is B.shist else "fhist"


def build_nc(**kw):
    return Builder(**kw).build()


def _tables():
    half = 32
    inv = (np.float32(10000.0) ** (-np.arange(half, dtype=np.float32) / np.float32(half))).astype(np.float32)
    pos = np.concatenate([np.arange(SEQ), PAST + np.arange(DEC_SEQ), np.zeros(64)]).astype(np.float32)
    ang = pos[:, None] * inv[None, :]
    cos = np.cos(ang).astype(np.float32).reshape(65, 128, 32).transpose(1, 0, 2)
    sin = np.sin(ang).astype(np.float32).reshape(65, 128, 32).transpose(1, 0, 2)
    k = np.arange(512)[:, None] // 64
    q = np.arange(512)[None, :] // 64
    am = (k <= q).astype(np.float32).reshape(4, 128, 512).transpose(1, 0, 2)
    misc = np.zeros((128, 1024), np.float32)
    misc[:, 0:128] = np.eye(128, dtype=np.float32)
    j = np.arange(64)[:, None]
    i = np.arange(64)[None, :]
    misc[0:64, 128:192] = (j <= i)
    misc[0:64, 192:704] = np.tile((j <= i).astype(np.float32), (1, 8))
    misc[0:64, 704:768] = np.where(j > i, NEG, 0.0)
    return dict(t_cos=np.ascontiguousarray(cos), t_sin=np.ascontiguousarray(sin),
                t_amask=np.ascontiguousarray(am), t_misc=misc)


def make_in_maps(inputs):
    tabs = _tables()
    f = lambda a: np.ascontiguousarray(np.asarray(a, dtype=np.float32))
    wnames = ["pre_mix_g", "w_in", "q_norm_g", "w_uq", "kv_norm_g", "w_ukv", "ssd_conv_w", "ssd_conv_b", "ssd_dt_bias",
              "ssd_A_log", "ssd_D", "ssd_norm_g", "w_o_mla", "w_o_ssd", "w_out", "post_mix_g", "pre_ffn_g", "w_up",
              "ffn_conv_w", "ffn_conv_b", "w_down", "post_ffn_g"]
    shared = {n: f(inputs[n][0]) for n in wnames}
    shared.update(tabs)
    maps = []
    zero_prompt = np.zeros((SEQ, D), np.float32)
    for c in range(8):
        m = dict(shared)
        m["xp"] = f(inputs["x_prompt"][PROMPT_CORES.index(c)]) if c in PROMPT_CORES else zero_prompt
        m["xs"] = f(inputs["x_sample"][c])
        m["c_ckv"] = f(inputs["cache_mla_ckv"][0, c])
        m["c_kpe"] = f(inputs["cache_mla_kpe"][0, c])
        m["c_sconv"] = f(inputs["state_ssd_conv"][0, c])
        m["c_ssd"] = f(inputs["state_ssd"][0, c]).reshape(SH * SP_, SN)
        m["c_fconv"] = f(inputs["state_ffn_conv"][0, c])
        maps.append(m)
    return maps


def kernel(**inputs):
    nc = build_nc()
    res = run_bass_kernel_spmd(nc, make_in_maps(inputs), core_ids=list(range(8)))
    r = res.results
    st = lambda name, cores: np.stack([r[c][name] for c in cores])[None]
    P4, S8 = PROMPT_CORES, range(8)
    return (st("o_yp", P4)[0], st("o_ys", S8)[0],
            st("o_ckvp", P4), st("o_kpep", P4), st("o_sconvp", P4),
            st("o_ssdp", P4).reshape(1, 4, SH, SP_, SN), st("o_fconvp", P4),
            st("o_ckvs", S8), st("o_kpes", S8), st("o_sconvs", S8),
            st("o_ssds", S8).reshape(1, 8, SH, SP_, SN), st("o_fconvs", S8))
```

```python
from contextlib import ExitStack

import numpy as np
import concourse.bass as bass
import concourse.mybir as mybir
from concourse.bass_utils import run_bass_kernel_spmd

F32 = mybir.dt.float32
BF16 = mybir.dt.bfloat16
AF = mybir.ActivationFunctionType
ALU = mybir.AluOpType

D = 2048
SEQ = 8192
DEC_SEQ = 64
PAST = 2048
H = 16
QL = 512
KVL = 512
ROPE = 64
SI = 4096
SH = 64
SP_ = 64
SG = 8
SN = 128
CONV_DIM = 6144
DFF = 5632
EPS = 1e-6
OFF_Q, OFF_KV, OFF_Z, OFF_XBC, OFF_DT, OFF_GATE = 0, 512, 1088, 5184, 11328, 11392
IN_DIM = 15488
SCALE = 192 ** -0.5
NEG = -30000.0
PROMPT_CORES = (0, 1, 4, 5)


class Op:
    __slots__ = ("eng", "fn", "deps", "sem", "val", "needed", "idx")


class Prog:
    ENGS = ("pe", "act", "dve", "pool", "sp")

    def __init__(self, nc):
        self.nc = nc
        self.ops = {e: [] for e in self.ENGS}
        self.res = {}
        self.dma_cnt = {}

    def add(self, eng, fn, reads=(), writes=(), dma=None):
        op = Op()
        op.eng, op.fn, op.sem, op.val, op.needed, op.idx = eng, fn, None, 0, False, 0
        if dma is not None:
            self.dma_cnt[dma] = self.dma_cnt.get(dma, 0) + 1
            op.sem, op.val = dma, 16 * self.dma_cnt[dma]
        stream = ("dma:" + dma) if dma is not None else eng
        deps = []
        for k in reads:
            st = self.res.get(k)
            if st is not None and st[0] is not None:
                deps.append(st[0])
            if st is not None and k.startswith("ps"):
                deps.extend(v for s_, v in st[1].items() if s_ != stream)
        for k in writes:
            st = self.res.get(k)
            if st is not None:
                same = lambda d: dma is None and d.sem is None and d.eng == eng
                if st[0] is not None and not same(st[0]):
                    deps.append(st[0])
                deps.extend(v for v in st[1].values() if not same(v))
        out = []
        for d in deps:
            if d is op or d in out:
                continue
            if d.sem is None and d.eng == eng == "pe":
                continue
            d.needed = True
            out.append(d)
        op.deps = out
        for k in reads:
            st = self.res.get(k)
            if st is None:
                st = self.res[k] = [None, {}]
            st[1][stream] = op
        for k in writes:
            self.res[k] = [op, {}]
        self.ops[eng].append(op)
        return op

    def emit(self, es):
        nc = self.nc
        esem = {e: es.enter_context(nc.semaphore("sem_" + e)) for e in self.ENGS}
        dsem = {k: es.enter_context(nc.semaphore("dsem_" + k)) for k in self.dma_cnt}
        for e in self.ENGS:
            c = 0
            for op in self.ops[e]:
                if op.sem is None and op.needed:
                    c += 1
                    op.idx = c
        final = [(dsem[k], 16 * n) for k, n in self.dma_cnt.items()]

        def run(e, eng):
            waited = {}
            for op in self.ops[e]:
                need = {}
                for d in op.deps:
                    if d.sem is not None:
                        s, v, key = dsem[d.sem], d.val, "d" + d.sem
                    else:
                        s, v, key = esem[d.eng], d.idx, d.eng
                    if key not in need or need[key][1] < v:
                        need[key] = (s, v)
                for key, (s, v) in need.items():
                    if waited.get(key, 0) < v:
                        eng.wait_ge(s, v)
                        waited[key] = v
                ins = op.fn(eng)
                if op.sem is not None:
                    ins.then_inc(dsem[op.sem], 16)
                elif op.needed:
                    ins.then_inc(esem[e], 1)
            if e == "sp":
                for s, v in final:
                    eng.wait_ge(s, v)

        block = es.enter_context(nc.Block())

        @block.tensor
        def _(eng):
            run("pe", eng)

        @block.scalar
        def _(eng):
            run("act", eng)

        @block.vector
        def _(eng):
            run("dve", eng)

        @block.gpsimd
        def _(eng):
            run("pool", eng)

        @block.sync
        def _(eng):
            run("sp", eng)


class Arena:
    def __init__(self, B, name, nbytes):
        self.t = B.sb(name, [128, nbytes // 4], F32)
        self.nbytes = nbytes
        self.off = 0

    def reset(self, off=0):
        self.off = off

    def take(self, shape, dt):
        n = 1
        for s in shape[1:]:
            n *= s
        nb = n * (4 if dt == F32 else 2)
        nb = (nb + 31) // 32 * 32
        assert self.off + nb <= self.nbytes, ("arena overflow", shape, self.off, nb, self.nbytes)
        o4 = self.off // 4
        self.off += nb
        ap = self.t[0:shape[0], o4:o4 + nb // 4]
        if dt == BF16:
            ap = ap.bitcast(BF16)
        ap = ap[:, 0:n]
        if len(shape) == 3:
            ap = ap.rearrange("p (a b) -> p a b", a=shape[1])
        elif len(shape) == 4:
            ap = ap.rearrange("p (a b c) -> p a b c", a=shape[1], b=shape[2])
        return ap


class Builder:
    def __init__(self, nslot=16, do_sample=True, stage=99, dbg=()):
        self.nslot = nslot
        self.do_sample = do_sample
        self.stage = stage
        self.dbg = dbg
        self.nc = bass.Bass("TRN2", target_bir_lowering=False)
        self.P = Prog(self.nc)
        self.es = ExitStack()
        self.psrr = 0
        self.wrr = 0
        self.rr = {}

    def din(self, name, shape, dt=F32):
        return self.nc.dram_tensor(name, list(shape), dt, kind="ExternalInput").ap()

    def dout(self, name, shape, dt=F32):
        return self.nc.dram_tensor(name, list(shape), dt, kind="ExternalOutput").ap()

    def dscr(self, name, shape, dt=BF16):
        return self.nc.dram_tensor(name, list(shape), dt, kind="Internal").ap()

    def sb(self, name, shape, dt=F32):
        return self.es.enter_context(self.nc.sbuf_tensor(name, list(shape), dt))

    def mm(self, out, lhsT, rhs, start, stop, r, w):
        self.P.add("pe", lambda e: e.matmul(out, lhsT=lhsT, rhs=rhs, start=start, stop=stop), r, w)

    def tr(self, out, in_, ident, r, w):
        self.P.add("pe", lambda e: e.transpose(out, in_, ident), r, w)

    def act(self, out, in_, func, r, w, bias=None, scale=None, accum=None):
        kw = {}
        if bias is not None:
            kw["bias"] = bias
        if scale is not None:
            kw["scale"] = scale
        if accum is not None:
            kw["accum_out"] = accum
        self.P.add("act", lambda e: e.activation(out, in_, func, **kw), r, w)

    def tt(self, eng, out, in0, in1, op, r, w):
        self.P.add(eng, lambda e: e.tensor_tensor(out, in0, in1, op), r, w)

    def ts(self, eng, out, in0, s1, s2, op0, op1, r, w):
        if op1 is None:
            self.P.add(eng, lambda e: e.tensor_scalar(out, in0, s1, None, op0), r, w)
        else:
            self.P.add(eng, lambda e: e.tensor_scalar(out, in0, s1, s2, op0, op1), r, w)

    def stt(self, eng, out, in0, scalar, in1, op0, op1, r, w):
        self.P.add(eng, lambda e: e.scalar_tensor_tensor(out, in0, scalar, in1, op0, op1), r, w)

    def cp(self, eng, out, in_, r, w):
        if eng == "act":
            self.P.add("act", lambda e: e.copy(out, in_), r, w)
        else:
            self.P.add(eng, lambda e: e.tensor_copy(out, in_), r, w)

    def recip(self, out, in_, r, w):
        self.P.add("dve", lambda e: e.reciprocal(out, in_), r, w)

    def memset(self, eng, ap, val, w):
        self.P.add(eng, lambda e: e.memset(ap, val), (), w)

    def dma(self, q, out, in_, sem, r, w, **kw):
        self.P.add(q, lambda e: e.dma_start(out=out, in_=in_, **kw), r, w, dma=sem)

    def barrier(self, rkeys=(), wkeys=()):
        t = self.bar_t
        self.P.add("pool", lambda e: e.memset(t[:, 0:1], 0.0), (), tuple(self.KALL) + tuple(rkeys) + tuple(wkeys) + ("bar_t",))

    def bank(self):
        i = self.psrr
        self.psrr = (self.psrr + 1) % 6
        return i

    def rot(self, name, n):
        i = self.rr.get(name, 0)
        self.rr[name] = (i + 1) % n
        return i

    def psb(self, i):
        return self.ps[i][:].bitcast(BF16)

    def build(self):
        with self.es:
            self._declare()
            self._consts()
            self._layout()
            self._convert_weights()
            self._prompt()
            if self.do_sample:
                self._sample()
            self.P.emit(self.es)
        return self.nc

    def _declare(self):
        B = self
        self.xp = B.din("xp", [SEQ, D])
        self.xs = B.din("xs", [DEC_SEQ, D])
        self.c_ckv = B.din("c_ckv", [PAST, KVL])
        self.c_kpe = B.din("c_kpe", [PAST, ROPE])
        self.c_sconv = B.din("c_sconv", [3, CONV_DIM])
        self.c_ssd = B.din("c_ssd", [SH * SP_, SN])
        self.c_fconv = B.din("c_fconv", [2, 2 * DFF])
        self.w = {}
        for n, shp in [("pre_mix_g", [D]), ("w_in", [D, IN_DIM]), ("q_norm_g", [QL]), ("w_uq", [QL, H, 192]),
                       ("kv_norm_g", [KVL]), ("w_ukv", [KVL, H, 256]), ("ssd_conv_w", [4, CONV_DIM]),
                       ("ssd_conv_b", [CONV_DIM]), ("ssd_dt_bias", [SH]), ("ssd_A_log", [SH]), ("ssd_D", [SH]),
                       ("ssd_norm_g", [SI]), ("w_o_mla", [D, D]), ("w_o_ssd", [SI, D]), ("w_out", [D, D]),
                       ("post_mix_g", [D]), ("pre_ffn_g", [D]), ("w_up", [D, 2 * DFF]), ("ffn_conv_w", [3, 2 * DFF]),
                       ("ffn_conv_b", [2 * DFF]), ("w_down", [DFF, D]), ("post_ffn_g", [D])]:
            self.w[n] = B.din(n, shp)
        self.t_cos = B.din("t_cos", [128, 65, 32])
        self.t_sin = B.din("t_sin", [128, 65, 32])
        self.t_amask = B.din("t_amask", [128, 4, 512])
        self.t_misc = B.din("t_misc", [128, 1024])
        self.o_yp = B.dout("o_yp", [SEQ, D])
        self.o_ckvp = B.dout("o_ckvp", [SEQ, KVL])
        self.o_kpep = B.dout("o_kpep", [SEQ, ROPE])
        self.o_sconvp = B.dout("o_sconvp", [3, CONV_DIM])
        self.o_ssdp = B.dout("o_ssdp", [SH * SP_, SN])
        self.o_fconvp = B.dout("o_fconvp", [2, 2 * DFF])
        self.o_ys = B.dout("o_ys", [DEC_SEQ, D])
        self.o_ckvs = B.dout("o_ckvs", [DEC_SEQ, KVL])
        self.o_kpes = B.dout("o_kpes", [DEC_SEQ, ROPE])
        self.o_sconvs = B.dout("o_sconvs", [3, CONV_DIM])
        self.o_ssds = B.dout("o_ssds", [SH * SP_, SN])
        self.o_fconvs = B.dout("o_fconvs", [2, 2 * DFF])
        self.dbg_out = {}
        for name, shp in self.dbg:
            self.dbg_out[name] = B.dout("dbg_" + name, shp)
        self.ps = [self.es.enter_context(self.nc.psum_tensor("ps%d" % i, [128, 512], F32)) for i in range(8)]

    def _wdecl(self, name, src2d, K, N, nb, kpart=None):
        kc = K // 128
        kpart = kpart or kc
        scr = self.dscr("wb_" + name, [N // nb, kc // kpart, 128, kpart, nb])
        self.wt[name] = (scr, src2d, kc, kpart, nb, N // nb)

    def _convert_weights(self):
        B = self
        w = self.w
        self.wt = {}
        self.convkeys = {}
        win = w["w_in"]
        small = {}
        for name, src, lo, hi in (("uqn", w["w_uq"], 0, 128), ("uqr", w["w_uq"], 128, 192),
                                  ("ukn", w["w_ukv"], 0, 128), ("ukv", w["w_ukv"], 128, 256)):
            small[name] = (src, lo, hi)
        order = [("ckv", win[:, OFF_KV:OFF_KV + 512], D, 512, 512, None), ("kpe", win[:, OFF_KV + 512:OFF_Z], D, 64, 64, None),
                 ("q", win[:, OFF_Q:OFF_Q + 512], D, 512, 512, None), "uqn", "uqr", "ukn", "ukv",
                 ("dt", win[:, OFF_DT:OFF_GATE], D, 64, 64, None),
                 ("xs", win[:, OFF_XBC:OFF_XBC + SI], D, SI, 512, None),
                 ("bm", win[:, OFF_XBC + SI:OFF_XBC + SI + 1024], D, 1024, 128, None),
                 ("cm", win[:, OFF_XBC + SI + 1024:OFF_DT], D, 1024, 128, None),
                 ("z", win[:, OFF_Z:OFF_XBC], D, SI, 512, None),
                 ("gate", win[:, OFF_GATE:IN_DIM], D, 2 * D, 512, None),
                 ("omla", w["w_o_mla"], D, D, 512, None), ("ossd", w["w_o_ssd"], SI, D, 512, 16),
                 ("out", w["w_out"], D, D, 512, None), ("up", w["w_up"], D, 2 * DFF, 512, None),
                 ("down", w["w_down"], DFF, D, 512, 11)]
        for item in order:
            if not isinstance(item, str):
                name, src, K, N, nb, kpart = item
                B._wdecl(name, src, K, N, nb, kpart)
        self._conv_items = order
        self._conv_small = small
        self._emit_conversions(0, 12)

    def _emit_conversions(self, lo, hi):
        B = self
        small = self._conv_small
        for item in self._conv_items[lo:hi]:
            if isinstance(item, str):
                name = item
                src, lo, hi = small[name]
                wd = hi - lo
                scr = self.dscr("wb_" + name, [1, 1, 128, 4, H * wd])
                self.wt[name] = (scr, None, 4, 4, H * wd, 1)
                self.convkeys[name] = []
                for rc in range(4):
                    ck = "wc_%s_%d" % (name, rc)
                    self.convkeys[name].append(ck)
                    B.dma("pool", scr[0, 0, :, rc, :].rearrange("p (h d) -> p h d", h=H),
                          src[rc * 128:(rc + 1) * 128, :, lo:hi], "wc_" + name, (), (ck,))
                continue
            name = item[0]
            scr, src, kc, kpart, nb, ncb = self.wt[name]
            self.convkeys[name] = []
            for cb in range(ncb):
                for kp in range(kc // kpart):
                    s_ = src[kp * kpart * 128:(kp + 1) * kpart * 128, cb * nb:(cb + 1) * nb]
                    s_ = s_.rearrange("(kc p) n -> p kc n", p=128)
                    ck = "wc_%s_%d_%d" % (name, cb, kp)
                    self.convkeys[name].append(ck)
                    B.dma("pool", scr[cb, kp], s_, "wc_" + name, (), (ck,))

    def wload(self, name, cb=0, kp=0):
        scr, _, kc, kpart, nb, ncb = self.wt[name]
        i = self.wrr
        self.wrr = (self.wrr + 1) % len(self.wbuf)
        buf = self.wbuf[i]
        key = "wbuf%d" % i
        ap = buf[:, 0:kpart * nb].rearrange("p (k n) -> p k n", k=kpart)
        assert self.convkeys.get(name), ("weight used before its conversion was issued", name)
        self.dma("sp", ap, scr[cb, kp], key, tuple(self.convkeys[name]), (key,))
        return ap, key

    def _consts(self):
        B = self
        w = self.w
        sb = B.sb
        self.bar_t = sb("bar_t", [128, 8], F32)
        self.wbuf = [sb("wbuf%d" % i, [128, 8192], BF16) for i in range(3)]
        self.misc = sb("misc", [128, 1024], F32)
        B.dma("sp", self.misc[:], self.t_misc[:, :], "cst", (), ("cst",))
        self.identf = self.misc[:, 0:128]
        self.tri = self.misc[0:64, 128:192]
        self.amask = sb("amask", [128, 4, 512], BF16)
        B.dma("pool", self.amask[:], self.t_amask[:, :, :], "cstp", (), ("cst",))
        self.identb = sb("identb", [128, 128], BF16)
        B.cp("dve", self.identb[:], self.identf, ("cst",), ("cst2",))
        self.onesb = sb("onesb", [128, 128], BF16)
        B.memset("dve", self.onesb[:], 1.0, ("cst2",))
        self.onesf = sb("onesf", [128, 128], F32)
        B.memset("dve", self.onesf[:], 1.0, ("cst2",))
        self.epsc = sb("epsc", [128, 1], F32)
        B.memset("dve", self.epsc[:], EPS, ("cst2",))
        self.st = sb("st", [128, 24], F32)

        def col(name, src, n):
            t = sb(name, [128, n // 128], F32)
            with self.nc.allow_non_contiguous_dma(reason="tiny one-time gain/bias column loads"):
                B.dma("sp", t[:], src.rearrange("(c p) -> p c", p=128), "cst", (), ("cst",), allow_slow_non_contiguous=True)
            return t
        self.g_premix = col("g_premix", w["pre_mix_g"], D)
        self.g_preffn = col("g_preffn", w["pre_ffn_g"], D)
        self.g_qn = col("g_qn", w["q_norm_g"], QL)
        self.g_ssdn = col("g_ssdn", w["ssd_norm_g"], SI)
        self.sconv_b = col("sconv_b", w["ssd_conv_b"], CONV_DIM)
        self.fconv_b = col("fconv_b", w["ffn_conv_b"], 2 * DFF)
        self.sconv_w = sb("sconv_w", [128, 4, CONV_DIM // 128], F32)
        self.fconv_w = sb("fconv_w", [128, 3, 2 * DFF // 128], F32)
        self.dcol = sb("dcol", [128, 32], F32)
        with self.nc.allow_non_contiguous_dma(reason="tiny one-time conv tap loads"):
            for k in range(4):
                B.dma("sp", self.sconv_w[:, k, :], w["ssd_conv_w"][k].rearrange("(c p) -> p c", p=128), "cst", (), ("cst",), allow_slow_non_contiguous=True)
            for k in range(3):
                B.dma("sp", self.fconv_w[:, k, :], w["ffn_conv_w"][k].rearrange("(c p) -> p c", p=128), "cst", (), ("cst",), allow_slow_non_contiguous=True)
            dsrc = w["ssd_D"].rearrange("(c two) -> two c", two=2)
            for half in range(2):
                B.dma("sp", self.dcol[half * 64:(half + 1) * 64, :], dsrc[half:half + 1, :].broadcast_to([64, 32]),
                      "cst", (), ("cst",), allow_slow_non_contiguous=True)

        def row(name, src, n, parts=128):
            t = sb(name, [parts, n], F32)
            B.dma("sp", t[:], src.rearrange("(o n) -> o n", o=1).broadcast_to([parts, n]), "cst", (), ("cst",))
            return t
        self.g_kvn_r = row("g_kvn_r", w["kv_norm_g"], KVL)
        self.dtb_r = row("dtb_r", w["ssd_dt_bias"], SH, 64)
        self.alog_r = row("alog_r", w["ssd_A_log"], SH, 64)
        self.a_r = sb("a_r", [64, SH], F32)
        B.act(self.a_r[:], self.alog_r[:], AF.Exp, ("cst",), ("cst2",))
        B.ts("dve", self.a_r[:], self.a_r[:], -1.0, None, ALU.mult, None, ("cst2",), ("cst2",))
        self.hst = sb("hst", [128, SI], F32)
        self.hbf = sb("hbf", [128, SI], BF16)
        self.shist = sb("shist", [128, 48, 3], F32)
        self.fhist = sb("fhist", [128, 88, 2], F32)
        self.uT = sb("uT", [128, 16, 512], BF16)

    def _layout(self):
        X = self.arX = Arena(self, "arenaX", 48 * 1024)
        Y = self.arY = Arena(self, "arenaY", 56 * 1024 - 64)
        X.reset()
        self.attT = X.take([128, H, 512], BF16)
        o = X.off
        self.qnT = X.take([128, H, 512], BF16)
        self.qpeT = X.take([128, 8, 512], BF16)
        self.qpeb = X.take([128, 4, H * ROPE], BF16)
        X.reset(o)
        self.ynT = X.take([128, 32, 512], BF16)
        X.reset()
        self.hdnT = X.take([128, 44, 512], BF16)
        Y.reset()
        self.xin = [Y.take([128, D], F32) for _ in range(2)]
        self.xnb = Y.take([128, 4, D], BF16)
        self.junk = Y.take([128, D], BF16)
        self.KA1 = ("xin0", "xin1", "xnb", "junk")
        Y.reset()
        self.ckvo = [Y.take([128, KVL], F32) for _ in range(2)]
        self.ckvb = Y.take([128, 4, KVL], BF16)
        self.ckvT = Y.take([128, 4, 512], BF16)
        self.kpeo = Y.take([128, 4, ROPE], F32)
        self.kpeb = Y.take([128, 4, 128], BF16)
        self.kpeT = Y.take([128, 512], BF16)
        self.rtmp = Y.take([128, 4, 256], F32)
        self.cqnb = Y.take([128, 4, QL], BF16)
        self.cqnT = Y.take([128, 4, 512], BF16)
        self.kst = [Y.take([128, 512], BF16) for _ in range(4)]
        self.vst = [Y.take([128, 512], BF16) for _ in range(4)]
        self.junk2 = Y.take([128, 512], BF16)
        self.cs = Y.take([128, 2, 4, 32], F32)
        self.KA2 = ("ckvo0", "ckvo1", "ckvb", "ckvT", "kpeo", "kpeb", "kpeT", "rtmp", "cqnb", "cqnT",
                    "kst0", "kst1", "kst2", "kst3", "vst0", "vst1", "vst2", "vst3", "junk2", "cs")
        Y.reset()
        self.kpeK = Y.take([128, SEQ], BF16)
        self.kp_ = [Y.take([128, 1024], BF16) for _ in range(4)]
        self.vp_ = [Y.take([128, 8, 128], BF16) for _ in range(4)]
        self.pT = [Y.take([128, 512], BF16) for _ in range(3)]
        self.rcp = [Y.take([128, 512], F32) for _ in range(2)]
        self.KB = ("kpeK", "kp0", "kp1", "kp2", "kp3", "vp0", "vp1", "vp2", "vp3", "pT0", "pT1", "pT2", "rcp0", "rcp1")
        Y.reset()
        self.dtt = Y.take([64, 8, 64], F32)
        self.atok = Y.take([64, 8, 64], F32)
        self.acum = Y.take([64, 8, 64], F32)
        self.etot = Y.take([128, 8, 64], F32)
        self.decs = Y.take([64, 8, 64], F32)
        self.dtdec = Y.take([64, 8, 64], F32)
        self.xsT = Y.take([128, 4, 512], F32)
        self.BT = Y.take([128, 512], BF16)
        self.CT = Y.take([128, 512], BF16)
        self.pre = [Y.take([128, 516], F32) for _ in range(2)]
        self.acc = [Y.take([128, 512], F32) for _ in range(2)]
        o = Y.off
        self.Btok = [Y.take([64, 128], BF16) for _ in range(2)]
        self.xdt = [Y.take([64, 512], BF16) for _ in range(2)]
        self.xdtd = [Y.take([64, 512], BF16) for _ in range(2)]
        self.cbm = [Y.take([64, 64], F32) for _ in range(2)]
        self.Ebc = [Y.take([128, 512], F32) for _ in range(2)]
        self.seg = [Y.take([64, 512], F32) for _ in range(2)]
        self.MT = [Y.take([64, 8, 64], BF16) for _ in range(2)]
        self.Ce = [Y.take([128, 8, 64], BF16) for _ in range(2)]
        self.abcs = [Y.take([128, 512], F32) for _ in range(2)]
        self.htmp = Y.take([128, 512], F32)
        Y.reset(o)
        self.zs = [Y.take([128, 512], F32) for _ in range(2)]
        self.sq = [Y.take([128, 512], F32) for _ in range(2)]
        self.rt = Y.take([128, 512], F32)
        self.KC = ("dtt", "atok", "acum", "etot", "decs", "dtdec", "xsT0", "xsT1", "xsT2", "xsT3", "xsT4", "xsT5", "xsT6", "xsT7", "BT", "CT", "pre0", "pre1", "acc0", "acc1",
                   "htmp", "zs0", "zs1", "sq0", "sq1", "rt") + tuple(
                       n + str(i) for n in ("Btok", "xdt", "xdtd", "cbm", "Ebc", "seg", "MT", "Ce", "abcs") for i in range(2))
        Y.reset()
        self.smT = Y.take([128, 16, 512], BF16)
        o = Y.off
        self.sg = [Y.take([128, 512], F32) for _ in range(2)]
        Y.reset(o)
        self.xnb1 = Y.take([128, D], BF16)
        self.t12 = [Y.take([128, 512], F32) for _ in range(2)]
        self.mixsb = Y.take([128, D], F32)
        self.xr = Y.take([128, D], F32)
        self.grow = Y.take([128, D], F32)
        self.KD1 = ("smT", "sgk", "t120", "t121", "mixsb", "xr", "grow")
        Y.reset()
        self.fpre = [Y.take([128, 516], F32) for _ in range(4)]
        self.facc = [Y.take([128, 512], F32) for _ in range(4)]
        self.asil = [Y.take([128, 512], F32) for _ in range(4)]
        self.KD2 = ("fpre0", "fpre1", "fpre2", "fpre3", "facc0", "facc1", "facc2", "facc3", "asil0", "asil1", "asil2", "asil3")
        Y.reset()
        self.dnsb = Y.take([128, 4, D], F32)
        self.xr2 = Y.take([128, D], F32)
        self.grow2 = Y.take([128, D], F32)
        self.KD3 = ("dnsb", "xr2", "grow2")
        Y.reset()
        self.stg = Y.take([128, 32, 128], F32)
        self.KALL = tuple(set(self.KA1 + self.KA2 + self.KB + self.KC + self.KD1 + self.KD2 + self.KD3
                              + ("stg", "attT", "qnT", "qpeT", "qpeb", "ynT", "hdnT")))

    def rstd(self, col, n, TP=128):
        st = self.st
        key = "st%d" % col
        self.act(st[0:TP, col:col + 1], st[0:TP, col:col + 1], AF.Sqrt, (key,), (key,), bias=self.epsc[0:TP, 0:1], scale=1.0 / n)
        self.recip(st[0:TP, col:col + 1], st[0:TP, col:col + 1], (key,), (key,))

    def _prompt(self):
        B = self
        self.kT_scr = self.dscr("kT_scr", [H, 128, SEQ])
        self.v_scr = self.dscr("v_scr", [H, 128, SEQ // 128, 128])
        self.kpeT_scr = self.dscr("kpeT_scr", [128, SEQ])
        self.xmid_scr = self.dscr("xmid_scr", [SEQ, D], F32)
        self.acT_scr = self.dscr("acT_scr", [8, SH, 64], F32)
        B.memset("dve", self.hst[:], 0.0, tuple("hst%d" % g for g in range(SG)) + ("hst",))
        B.memset("pool", self.hbf[:], 0.0, tuple("hbf%d" % g for g in range(SG)))
        B.memset("dve", self.shist[:], 0.0, ("shist",))
        B.memset("pool", self.fhist[:], 0.0, ("fhist",))
        for s in range(self.nslot):
            ctx = dict(s=s, T=512, TP=128, xsrc=self.xp[s * 512:(s + 1) * 512, :], pos_tile0=4 * s,
                       o_ckv=self.o_ckvp[s * 512:(s + 1) * 512, :], o_kpe=self.o_kpep[s * 512:(s + 1) * 512, :],
                       o_y=self.o_yp[s * 512:(s + 1) * 512, :], xmid=self.xmid_scr[s * 512:(s + 1) * 512, :],
                       key0=s * 512, tag="p", kT=self.kT_scr, vS=self.v_scr, kpS=self.kpeT_scr, masked=True,
                       last=(s == self.nslot - 1), o_sconv=self.o_sconvp, o_ssd=self.o_ssdp, o_fconv=self.o_fconvp)
            self._slot(ctx)

            if self.stage < 9:
                return
        self._dump_states(self.o_sconvp, self.o_ssdp, self.o_fconvp, "p")

    def _dump_states(self, o_sconv, o_ssd, o_fconv, tag):
        B = self
        B.barrier()
        for q in range(4):
            for k in range(3):
                B.dma("sp", o_sconv[k, q * 1536:(q + 1) * 1536].rearrange("(c p) -> p c", p=128),
                      self.shist[:, q * 12:(q + 1) * 12, k], "o_sconv" + tag, ("shist",), (), allow_slow_non_contiguous=True)
        for q in range(8):
            for k in range(2):
                B.dma("sp", o_fconv[k, q * 1408:(q + 1) * 1408].rearrange("(c p) -> p c", p=128),
                      self.fhist[:, q * 11:(q + 1) * 11, k], "o_fconv" + tag, ("fhist",), (), allow_slow_non_contiguous=True)
        for pc in range(32):
            if pc % 4 == 0:
                b = B.bank()
                pk = "ps%d" % b
            B.tr(self.ps[b][:, (pc % 4) * 128:(pc % 4 + 1) * 128], self.hst[:, pc * 128:(pc + 1) * 128], self.identf,
                 tuple("hst%d" % g for g in range(SG)) + ("hst", "cst"), (pk,))
            if pc % 4 == 3:
                B.cp("act" if (pc // 4) % 2 else "dve", self.stg[:, pc - 3:pc + 1, :],
                     self.ps[b][:, :].rearrange("p (c n) -> p c n", c=4), (pk,), ("stg",))
        B.dma("sp", o_ssd.rearrange("(c p) n -> p c n", p=128), self.stg[:, :, :], "o_ssd" + tag, ("stg",), ())

    def _sample(self):
        B = self
        kT_s = self.dscr("kT_s", [H, 128, PAST + DEC_SEQ])
        v_s = self.dscr("v_s", [H, 128, PAST // 128 + 1, 128])
        kpeT_s = self.dscr("kpeT_s", [128, PAST + DEC_SEQ])
        xmid_s = self.dscr("xmid_s", [DEC_SEQ, D], F32)
        c = dict(s=0, T=DEC_SEQ, TP=64, xsrc=self.xs, pos_tile0=64, o_ckv=self.o_ckvs, o_kpe=self.o_kpes, o_y=self.o_ys,
                 xmid=xmid_s, key0=PAST, tag="s", kT=kT_s, vS=v_s, kpS=kpeT_s, masked=False, last=False)
        B.barrier()
        hk = tuple("hst%d" % g for g in range(SG)) + ("hst",)
        B.dma("sp", self.stg[:, :, :], self.c_ssd.rearrange("(c p) n -> p c n", p=128), "stg_in", (), ("stg",))
        for pc in range(32):
            if pc % 4 == 0:
                b = B.bank()
                pk = "ps%d" % b
            B.tr(self.ps[b][:, (pc % 4) * 128:(pc % 4 + 1) * 128], self.stg[:, pc, :], self.identf, ("stg", "cst"), (pk,))
            if pc % 4 == 3:
                B.cp("act" if (pc // 4) % 2 else "dve", self.hst[:, (pc - 3) * 128:(pc + 1) * 128], self.ps[b][:, :], (pk,), hk)
        B.cp("act", self.hbf[:, :], self.hst[:, :], hk, tuple("hbf%d" % g for g in range(SG)))
        for q in range(4):
            for k in range(3):
                B.dma("sp", self.shist[:, q * 12:(q + 1) * 12, k],
                      self.c_sconv[k, q * 1536:(q + 1) * 1536].rearrange("(c p) -> p c", p=128), "shist_in", (), ("shist",),
                      allow_slow_non_contiguous=True)
        for q in range(8):
            for k in range(2):
                B.dma("sp", self.fhist[:, q * 11:(q + 1) * 11, k],
                      self.c_fconv[k, q * 1408:(q + 1) * 1408].rearrange("(c p) -> p c", p=128), "fhist_in", (), ("fhist",),
                      allow_slow_non_contiguous=True)
        for blk in range(PAST // 512):
            B.barrier()
            for tt in range(4):
                r0 = blk * 512 + tt * 128
                i = B.rot("ckvo", 2)
                co, cok = self.ckvo[i], "ckvo%d" % i
                B.dma("sp", co[:, :], self.c_ckv[r0:r0 + 128, :], "ld_" + cok, (), (cok,))
                B.cp("act" if tt % 2 else "dve", self.ckvb[:, tt, :], co[:, :], (cok,), ("ckvb",))
            B.dma("sp", self.kpeo[:, :, :], self.c_kpe[blk * 512:(blk + 1) * 512, :].rearrange("(t p) d -> p t d", p=128),
                  "ld_kpeo", (), ("kpeo",))
            B.cp("act", self.kpeb[:, :, 0:64], self.kpeo[:, :, :], ("kpeo",), ("kpeb",))
            B.cp("dve", self.kpeb[:, :, 64:128], self.kpeo[:, :, :], ("kpeo",), ("kpeb",))
            for rc in range(4):
                b = B.bank()
                pk = "ps%d" % b
                for tt in range(4):
                    B.tr(B.psb(b)[:, tt * 128:(tt + 1) * 128], self.ckvb[:, tt, rc * 128:(rc + 1) * 128],
                         self.identb[:, :], ("ckvb", "cst2"), (pk,))
                B.cp("act" if rc % 2 else "dve", self.ckvT[:, rc, :], B.psb(b)[:, 0:512], (pk,), ("ckvT",))
            b = B.bank()
            pk = "ps%d" % b
            for tt in range(4):
                B.tr(B.psb(b)[:, tt * 128:(tt + 1) * 128], self.kpeb[:, tt, :], self.identb[:, :], ("kpeb", "cst2"), (pk,))
            B.cp("dve", self.kpeT[:, :], B.psb(b)[:, 0:512], (pk,), ("kpeT",))
            B.dma("pool", kpeT_s[:, blk * 512:(blk + 1) * 512], self.kpeT[:, :], "kpSs", ("kpeT",), ("kpSs",))
            self._s4_expand(c, key0=blk * 512, T=512, TP=128)
        self._slot(c)
        if self.stage < 9:
            return
        self._dump_states(self.o_sconvs, self.o_ssds, self.o_fconvs, "s")

    def _slot(self, c):
        B = self
        B.barrier(self.KD3 + ("hdnT",), self.KA1)
        self._s1_norm(c)
        if self.stage < 2:
            return
        B.barrier(self.KA1, self.KA2 + ("qnT", "qpeT", "qpeb", "attT"))
        self._s2_kv(c)
        if c["tag"] == "p" and c["s"] == 0:
            self._emit_conversions(12, len(self._conv_items))
        if self.stage < 3:
            return
        self._s3_q(c)
        self._s4_expand(c)
        if self.stage < 5:
            return
        B.barrier(self.KA2, self.KB)
        self._s5_attn(c)
        if self.stage < 6:
            return
        B.barrier(self.KB + ("qnT", "qpeT", "qpeb"), self.KC + ("ynT",))
        self._s6_ssd(c)
        if self.stage < 7:
            return
        B.barrier(self.KC, self.KD1)
        self._s7_merge(c)
        if self.stage < 8:
            return
        B.barrier(self.KD1 + ("attT", "ynT"), self.KD2 + ("hdnT",))
        self._s8_ffn_up(c)
        B.barrier(self.KD2, self.KD3)
        self._s9_ffn_down(c)

    def _s1_norm(self, c):
        B = self
        T, TP = c["T"], c["TP"]
        NTT = T // TP
        st, xnb, uT = self.st, self.xnb, self.uT
        for tt in range(NTT):
            i = B.rot("xin", 2)
            xin, xk = self.xin[i], "xin%d" % i
            B.dma("sp", xin[0:TP, :], c["xsrc"][tt * TP:(tt + 1) * TP, :], xk, (), (xk,))
            c0 = tt
            B.act(self.junk[0:TP, :], xin[0:TP, :], AF.Square, (xk,), ("junk", "st%d" % c0), accum=st[0:TP, c0:c0 + 1])
            B.rstd(c0, D, TP)
            B.ts("dve", xnb[0:TP, tt, :], xin[0:TP, :], st[0:TP, c0:c0 + 1], None, ALU.mult, None, (xk, "st%d" % c0), ("xnb",))
        for kc in range(16):
            b = B.bank()
            pk = "ps%d" % b
            for tt in range(NTT):
                B.tr(B.psb(b)[:, tt * TP:(tt + 1) * TP], xnb[0:TP, tt, kc * 128:(kc + 1) * 128], self.identb[0:TP, 0:TP],
                     ("xnb", "cst2"), (pk,))
            if kc % 2 == 0:
                B.ts("dve", uT[:, kc, 0:T], B.psb(b)[:, 0:T], self.g_premix[:, kc:kc + 1], None, ALU.mult, None,
                     (pk, "cst"), ("uT",))
            else:
                B.act(uT[:, kc, 0:T], B.psb(b)[:, 0:T], AF.Copy, (pk, "cst"), ("uT",), scale=self.g_premix[:, kc:kc + 1])
        if "uT" in self.dbg_out and c["last"]:
            B.dma("pool", self.dbg_out["uT"], self.uT[:], "dbg_uT", ("uT",), ())

    def _s2_kv(self, c):
        B = self
        T, TP, tag = c["T"], c["TP"], c["tag"]
        NTT = T // TP
        st, uT = self.st, self.uT
        B.dma("sp", self.cs[0:TP, 0, 0:NTT, :], self.t_cos[0:TP, c["pos_tile0"]:c["pos_tile0"] + NTT, :], "cs", (), ("cs",))
        B.dma("sp", self.cs[0:TP, 1, 0:NTT, :], self.t_sin[0:TP, c["pos_tile0"]:c["pos_tile0"] + NTT, :], "cs", (), ("cs",))
        wck, kck = B.wload("ckv")
        wkp, kkp = B.wload("kpe")
        for tt in range(NTT):
            ba, bb = B.bank(), B.bank()
            pa, pb = "ps%d" % ba, "ps%d" % bb
            psa, psk = self.ps[ba], self.ps[bb]
            for kc in range(16):
                B.mm(psa[0:TP, :], uT[:, kc, tt * TP:(tt + 1) * TP], wck[:, kc, :], kc == 0, kc == 15, ("uT", kck), (pa,))
            for kc in range(16):
                B.mm(psk[0:TP, 0:ROPE], uT[:, kc, tt * TP:(tt + 1) * TP], wkp[:, kc, :], kc == 0, kc == 15, ("uT", kkp), (pb,))
            c1 = 4 + tt
            B.act(self.junk2[0:TP, 0:KVL], psa[0:TP, :], AF.Square, (pa,), ("junk2", "st%d" % c1), accum=st[0:TP, c1:c1 + 1])
            B.rstd(c1, KVL, TP)
            i = B.rot("ckvo", 2)
            co, cok = self.ckvo[i], "ckvo%d" % i
            B.stt("dve", co[0:TP, :], psa[0:TP, :], st[0:TP, c1:c1 + 1], self.g_kvn_r[0:TP, :], ALU.mult, ALU.mult,
                  (pa, "st%d" % c1, "cst"), (cok,))
            B.cp("act", self.ckvb[0:TP, tt, :], co[0:TP, :], (cok,), ("ckvb",))
            B.dma("pool", c["o_ckv"][tt * TP:(tt + 1) * TP, :], co[0:TP, :], "o_" + cok, (cok,), ())
            c_, s_ = self.cs[0:TP, 0, tt, :], self.cs[0:TP, 1, tt, :]
            x1, x2 = psk[0:TP, 0:32], psk[0:TP, 32:64]
            r = self.rtmp
            B.tt("dve", r[0:TP, 0, 0:32], x1, c_, ALU.mult, (pb, "cs"), ("rtmp",))
            B.tt("dve", r[0:TP, 1, 0:32], x2, s_, ALU.mult, (pb, "cs"), ("rtmp",))
            B.tt("dve", r[0:TP, 2, 0:32], x1, s_, ALU.mult, (pb, "cs"), ("rtmp",))
            B.tt("dve", r[0:TP, 3, 0:32], x2, c_, ALU.mult, (pb, "cs"), ("rtmp",))
            B.tt("dve", self.kpeo[0:TP, tt, 0:32], r[0:TP, 0, 0:32], r[0:TP, 1, 0:32], ALU.subtract, ("rtmp",), ("kpeo",))
            B.tt("dve", self.kpeo[0:TP, tt, 32:64], r[0:TP, 2, 0:32], r[0:TP, 3, 0:32], ALU.add, ("rtmp",), ("kpeo",))
            B.cp("act", self.kpeb[0:TP, tt, 0:64], self.kpeo[0:TP, tt, :], ("kpeo",), ("kpeb",))
            B.cp("act", self.kpeb[0:TP, tt, 64:128], self.kpeo[0:TP, tt, :], ("kpeo",), ("kpeb",))
        B.dma("pool", c["o_kpe"].rearrange("(t p) d -> p t d", p=TP), self.kpeo[0:TP, 0:NTT, :], "o_kpe" + tag, ("kpeo",), ())
        for rc in range(4):
            b = B.bank()
            pk = "ps%d" % b
            for tt in range(NTT):
                B.tr(B.psb(b)[:, tt * TP:(tt + 1) * TP], self.ckvb[0:TP, tt, rc * 128:(rc + 1) * 128],
                     self.identb[0:TP, 0:TP], ("ckvb", "cst2"), (pk,))
            B.cp("act" if rc % 2 else "dve", self.ckvT[:, rc, 0:T], B.psb(b)[:, 0:T], (pk,), ("ckvT",))
        b = B.bank()
        pk = "ps%d" % b
        for tt in range(NTT):
            B.tr(B.psb(b)[:, tt * TP:(tt + 1) * TP], self.kpeb[0:TP, tt, :], self.identb[0:TP, 0:TP], ("kpeb", "cst2"), (pk,))
        B.cp("dve", self.kpeT[:, 0:T], B.psb(b)[:, 0:T], (pk,), ("kpeT",))
        B.dma("pool", c["kpS"][:, c["key0"]:c["key0"] + T], self.kpeT[:, 0:T], "kpS" + tag, ("kpeT",), ("kpS" + tag,))
        if "ckvT" in self.dbg_out and c["last"]:
            B.dma("pool", self.dbg_out["ckvT"], self.ckvT[:], "dbg_ckvT", ("ckvT",), ())

    def _s3_q(self, c):
        B = self
        T, TP = c["T"], c["TP"]
        NTT = T // TP
        st, uT = self.st, self.uT
        wq, kq = B.wload("q")
        for tt in range(NTT):
            b = B.bank()
            pk = "ps%d" % b
            for kc in range(16):
                B.mm(self.ps[b][0:TP, :], uT[:, kc, tt * TP:(tt + 1) * TP], wq[:, kc, :], kc == 0, kc == 15, ("uT", kq), (pk,))
            c2 = 8 + tt
            B.act(self.junk2[0:TP, 0:QL], self.ps[b][0:TP, :], AF.Square, (pk,), ("junk2", "st%d" % c2), accum=st[0:TP, c2:c2 + 1])
            B.rstd(c2, QL, TP)
            B.ts("dve", self.cqnb[0:TP, tt, :], self.ps[b][0:TP, :], st[0:TP, c2:c2 + 1], None, ALU.mult, None, (pk, "st%d" % c2), ("cqnb",))
        for rc in range(4):
            b = B.bank()
            pk = "ps%d" % b
            for tt in range(NTT):
                B.tr(B.psb(b)[:, tt * TP:(tt + 1) * TP], self.cqnb[0:TP, tt, rc * 128:(rc + 1) * 128],
                     self.identb[0:TP, 0:TP], ("cqnb", "cst2"), (pk,))
            B.ts("dve", self.cqnT[:, rc, 0:T], B.psb(b)[:, 0:T], self.g_qn[:, rc:rc + 1], None, ALU.mult, None,
                 (pk, "cst"), ("cqnT",))
        wn, kn = B.wload("uqn")
        for h in range(H):
            b = B.bank()
            pk = "ps%d" % b
            for rc in range(4):
                B.mm(self.ps[b][:, 0:T], wn[:, rc, h * 128:(h + 1) * 128], self.cqnT[:, rc, 0:T], rc == 0, rc == 3,
                     ("cqnT", kn), (pk,))
            B.cp("act" if h % 2 else "dve", self.qnT[:, h, 0:T], self.ps[b][:, 0:T], (pk,), ("qnT",))
        wr, kr = B.wload("uqr")
        for tt in range(NTT):
            for cb in range(2):
                b = B.bank()
                pk = "ps%d" % b
                for rc in range(4):
                    B.mm(self.ps[b][0:TP, :], self.cqnT[:, rc, tt * TP:(tt + 1) * TP], wr[:, rc, cb * 512:(cb + 1) * 512],
                         rc == 0, rc == 3, ("cqnT", kr), (pk,))
                pv = self.ps[b][0:TP, :].rearrange("p (h d) -> p h d", h=8)
                x1, x2 = pv[:, :, 0:32], pv[:, :, 32:64]
                c_ = self.cs[0:TP, 0, tt, :].unsqueeze(1).broadcast_to([TP, 8, 32])
                s_ = self.cs[0:TP, 1, tt, :].unsqueeze(1).broadcast_to([TP, 8, 32])
                r = self.rtmp
                rv = [r[0:TP, k, :].rearrange("p (h d) -> p h d", h=8) for k in range(4)]
                B.tt("dve", rv[0], x1, c_, ALU.mult, (pk, "cs"), ("rtmp",))
                B.tt("dve", rv[1], x2, s_, ALU.mult, (pk, "cs"), ("rtmp",))
                B.tt("dve", rv[2], x1, s_, ALU.mult, (pk, "cs"), ("rtmp",))
                B.tt("dve", rv[3], x2, c_, ALU.mult, (pk, "cs"), ("rtmp",))
                qv = self.qpeb[0:TP, tt, cb * 512:(cb + 1) * 512].rearrange("p (h d) -> p h d", h=8)
                B.tt("pool", qv[:, :, 0:32], rv[0], rv[1], ALU.subtract, ("rtmp",), ("qpeb",))
                B.tt("pool", qv[:, :, 32:64], rv[2], rv[3], ALU.add, ("rtmp",), ("qpeb",))
        for hp in range(8):
            b = B.bank()
            pk = "ps%d" % b
            for tt in range(NTT):
                B.tr(B.psb(b)[:, tt * TP:(tt + 1) * TP], self.qpeb[0:TP, tt, hp * 128:(hp + 1) * 128],
                     self.identb[0:TP, 0:TP], ("qpeb", "cst2"), (pk,))
            B.cp("act" if hp % 2 else "dve", self.qpeT[:, hp, 0:T], B.psb(b)[:, 0:T], (pk,), ("qpeT",))

    def _s4_expand(self, c, key0=None, T=None, TP=None):
        B = self
        T = T or c["T"]
        TP = TP or c["TP"]
        key0 = c["key0"] if key0 is None else key0
        tag = c["tag"]
        NTT = T // TP
        wk, kk = B.wload("ukn")
        for h in range(H):
            b = B.bank()
            pk = "ps%d" % b
            for rc in range(4):
                B.mm(self.ps[b][:, 0:T], wk[:, rc, h * 128:(h + 1) * 128], self.ckvT[:, rc, 0:T], rc == 0, rc == 3,
                     ("ckvT", kk), (pk,))
            i = B.rot("kst", 4)
            B.cp("act" if h % 2 else "dve", self.kst[i][:, 0:T], self.ps[b][:, 0:T], (pk,), ("kst%d" % i,))
            B.dma("pool", c["kT"][h, :, key0:key0 + T], self.kst[i][:, 0:T], "kst%d" % i, ("kst%d" % i,), ("kT" + tag,))
        wv, kv = B.wload("ukv")
        for tt in range(NTT):
            kt = (key0 + tt * TP) // 128
            for cb in range(4):
                b = B.bank()
                pk = "ps%d" % b
                for rc in range(4):
                    B.mm(self.ps[b][0:TP, :], self.ckvT[:, rc, tt * TP:(tt + 1) * TP], wv[:, rc, cb * 512:(cb + 1) * 512],
                         rc == 0, rc == 3, ("ckvT", kv), (pk,))
                i = B.rot("vst", 4)
                B.cp("act" if cb % 2 else "dve", self.vst[i][0:TP, :], self.ps[b][0:TP, :], (pk,), ("vst%d" % i,))
                B.dma("pool", c["vS"][4 * cb:4 * cb + 4, 0:TP, kt, :].rearrange("h p d -> p h d"),
                      self.vst[i][0:TP, :].rearrange("p (h d) -> p h d", h=4), "vst%d" % i, ("vst%d" % i,), ("vS" + tag,))

    def _s5_attn(self, c):
        B = self
        T, tag = c["T"], c["tag"]
        nk = c["key0"] + T
        tiles = []
        k = 0
        while k < nk:
            n = min(128, nk - k)
            mi = (k - c["key0"]) // 128 if (c["masked"] and k >= c["key0"]) else None
            tiles.append((k, n, mi))
            k += n
        B.dma("sp", self.kpeK[:, 0:nk], c["kpS"][:, 0:nk], "kpeK", ("kpS" + tag,), ("kpeK",))
        steps = [(h, ti) for h in range(H) for ti in range(len(tiles))]
        state = {}

        def qk(h, ti):
            k0, n, mi = tiles[ti]
            if k0 % 1024 == 0:
                pi = B.rot("kvp", 4)
                npk = min(1024, nk - k0)
                B.dma("sp", self.kp_[pi][:, 0:npk], c["kT"][h, :, k0:k0 + npk], "kp%d" % pi, ("kT" + tag,), ("kp%d" % pi,))
                nkt = (npk + 127) // 128
                pv = min(128, npk)
                B.dma("sp", self.vp_[pi][0:pv, 0:nkt, :], c["vS"][h, 0:pv, k0 // 128:k0 // 128 + nkt, :], "vp%d" % pi,
                      ("vS" + tag,), ("vp%d" % pi,))
                state["pi"] = pi
            pi = state["pi"]
            hb = 64 * (h % 2)
            kl = k0 % 1024
            b = B.rot("abank", 4)
            pk = "ps%d" % b
            B.mm(self.ps[b][0:n, 0:T], self.kp_[pi][:, kl:kl + n], self.qnT[:, h, 0:T], True, False, ("kp%d" % pi, "qnT"), (pk,))
            B.mm(self.ps[b][0:n, 0:T], self.kpeK[hb:hb + 64, k0:k0 + n], self.qpeT[hb:hb + 64, h // 2, 0:T], False, True,
                 ("kpeK", "qpeT"), (pk,))
            return (b, pk, pi, kl)

        nxt = qk(*steps[0])
        for si, (h, ti) in enumerate(steps):
            b, pk, pi, kl = nxt
            if si + 1 < len(steps):
                nxt = qk(*steps[si + 1])
            k0, n, mi = tiles[ti]
            bo, br = (4, 5) if h % 2 == 0 else (6, 7)
            po, pr = "ps%d" % bo, "ps%d" % br
            i = B.rot("pT", 3)
            pT, ptk = self.pT[i], "pT%d" % i
            B.act(pT[0:n, 0:T], self.ps[b][0:n, 0:T], AF.Exp, (pk,), (ptk,), scale=SCALE)
            if mi is not None:
                B.tt("pool", pT[0:n, 0:T], pT[0:n, 0:T], self.amask[0:n, mi, 0:T], ALU.mult, (ptk, "cst"), (ptk,))
            first, last = ti == 0, ti == len(tiles) - 1
            B.mm(self.ps[bo][:, 0:T], self.vp_[pi][0:n, kl // 128, :], pT[0:n, 0:T], first, last, ("vp%d" % pi, ptk), (po,))
            B.mm(self.ps[br][:, 0:T], self.onesb[0:n, :], pT[0:n, 0:T], first, last, (ptk, "cst2"), (pr,))
            if last:
                ri = B.rot("rcp", 2)
                rc_ = self.rcp[ri]
                B.recip(rc_[:, 0:T], self.ps[br][:, 0:T], (pr,), ("rcp%d" % ri,))
                B.tt("dve", self.attT[:, h, 0:T], self.ps[bo][:, 0:T], rc_[:, 0:T], ALU.mult, (po, "rcp%d" % ri), ("attT",))
        if "attT" in self.dbg_out and c["last"]:
            B.dma("pool", self.dbg_out["attT"], self.attT[:], "dbg_attT", ("attT",), ())

    def _conv(self, b, T, K, wts, bias, hist, fc, pre, prek, acc, acck, eng):
        B = self
        pk = "ps%d" % b
        H_ = K - 1
        B.cp("act", pre[:, H_:H_ + T], self.ps[b][:, 0:T], (pk,), (prek,))
        B.cp("pool", pre[:, 0:H_], hist[:, fc, :], (hist_key(hist, self),), (prek,))
        B.ts(eng, acc[:, 0:T], pre[:, H_:H_ + T], wts[:, K - 1, fc:fc + 1], bias[:, fc:fc + 1], ALU.mult, ALU.add,
             (prek, "cst"), (acck,))
        for k in range(K - 1):
            B.stt(eng, acc[:, 0:T], pre[:, k:k + T], wts[:, k, fc:fc + 1], acc[:, 0:T], ALU.mult, ALU.add,
                  (prek, acck, "cst"), (acck,))
        B.cp("pool", hist[:, fc, :], pre[:, T:T + H_], (prek,), (hist_key(hist, self),))

    def _s6_ssd(self, c):
        B = self
        T = c["T"]
        NCH = T // 64
        uT = self.uT
        dtt, atok, acum, etot, decs, dtdec = self.dtt, self.atok, self.acum, self.etot, self.decs, self.dtdec
        wdt, kdt = B.wload("dt")
        b = B.bank()
        pk = "ps%d" % b
        for ch in range(NCH):
            for kc in range(16):
                B.mm(self.ps[b][0:64, ch * 64:(ch + 1) * 64], uT[:, kc, ch * 64:(ch + 1) * 64], wdt[:, kc, :], kc == 0, kc == 15,
                     ("uT", kdt), (pk,))
        pv = self.ps[b][0:64, 0:NCH * 64].rearrange("p (c h) -> p c h", c=NCH)
        B.tt("dve", dtt[:, 0:NCH, :], pv, self.dtb_r[:, :].unsqueeze(1).broadcast_to([64, NCH, 64]), ALU.add, (pk, "cst"), ("dtt",))
        B.ts("dve", dtt[:, 0:NCH, :], dtt[:, 0:NCH, :], 30.0, None, ALU.min, None, ("dtt",), ("dtt",))
        B.act(dtt[:, 0:NCH, :], dtt[:, 0:NCH, :], AF.Exp, ("dtt",), ("dtt",))
        B.act(dtt[:, 0:NCH, :], dtt[:, 0:NCH, :], AF.Ln, ("dtt",), ("dtt",), bias=self.onesf[0:64, 0:1])
        B.tt("dve", atok[:, 0:NCH, :], dtt[:, 0:NCH, :], self.a_r[:, :].unsqueeze(1).broadcast_to([64, NCH, 64]), ALU.mult,
             ("dtt", "cst2"), ("atok",))
        b = B.bank()
        pk = "ps%d" % b
        for ch in range(NCH):
            B.mm(self.ps[b][0:64, ch * 64:(ch + 1) * 64], self.tri, atok[:, ch, :], True, True, ("atok", "cst"), (pk,))
        B.cp("dve", acum[:, 0:NCH, :], self.ps[b][0:64, 0:NCH * 64].rearrange("p (c h) -> p c h", c=NCH), (pk,), ("acum",))
        b = B.bank()
        pk = "ps%d" % b
        for ch in range(NCH):
            B.mm(self.ps[b][:, ch * 64:(ch + 1) * 64], self.onesf[0:64, :], atok[:, ch, :], True, True, ("atok", "cst2"), (pk,))
        pv = self.ps[b][:, 0:NCH * 64].rearrange("p (c h) -> p c h", c=NCH)
        B.act(etot[:, 0:NCH, :], pv, AF.Exp, (pk,), ("etot",))
        B.tt("dve", decs[:, 0:NCH, :], pv[0:64], acum[:, 0:NCH, :], ALU.subtract, (pk, "acum"), ("decs",))
        B.act(decs[:, 0:NCH, :], decs[:, 0:NCH, :], AF.Exp, ("decs",), ("decs",))
        B.tt("dve", dtdec[:, 0:NCH, :], decs[:, 0:NCH, :], dtt[:, 0:NCH, :], ALU.mult, ("decs", "dtt"), ("dtdec",))
        b = B.bank()
        pk = "ps%d" % b
        for ch in range(NCH):
            B.tr(self.ps[b][0:64, ch * 64:(ch + 1) * 64], acum[:, ch, :], self.identf[0:64, 0:64], ("acum", "cst"), (pk,))
        B.cp("act", atok[:, 0:NCH, :], self.ps[b][0:64, 0:NCH * 64].rearrange("p (c h) -> p c h", c=NCH), (pk,), ("atok",))
        B.dma("pool", self.acT_scr[0:NCH].rearrange("c h i -> h c i"), atok[:, 0:NCH, :], "acT_w", ("atok",), ("acT",))

        xk_all = tuple("xsT%d" % ch for ch in range(8))
        for g in range(SG):
            wx, kx = B.wload("xs", g)
            wb_, kb_ = B.wload("bm", g)
            wc_, kc_ = B.wload("cm", g)
            plan = [(wx, kx, j * 128, 4 * g + j, ("xs", j)) for j in range(4)]
            plan += [(wb_, kb_, 0, 32 + g, ("B", 0)), (wc_, kc_, 0, 40 + g, ("C", 0))]
            for (wt_, wk_, cl, fc, (kind, j)) in plan:
                b = B.bank()
                pk = "ps%d" % b
                for kc in range(16):
                    B.mm(self.ps[b][:, 0:T], wt_[:, kc, cl:cl + 128], uT[:, kc, 0:T], kc == 0, kc == 15, ("uT", wk_), (pk,))
                i = B.rot("pre", 2)
                eng = "dve"
                B._conv(b, T, 4, self.sconv_w, self.sconv_b, self.shist, fc, self.pre[i], "pre%d" % i, self.acc[i], "acc%d" % i, eng)
                if kind == "xs":
                    B.act(self.xsT[:, j, 0:T], self.acc[i][:, 0:T], AF.Silu, ("acc%d" % i,), xk_all)
                elif kind == "B":
                    B.act(self.BT[:, 0:T], self.acc[i][:, 0:T], AF.Silu, ("acc%d" % i,), ("BT",))
                else:
                    B.act(self.CT[:, 0:T], self.acc[i][:, 0:T], AF.Silu, ("acc%d" % i,), ("CT",))
            hs = slice(g * 512, (g + 1) * 512)
            hk, hbk = "hst%d" % g, "hbf%d" % g

            def prep(ch, g=g):
                q = ch % 2
                cs_ = slice(ch * 64, (ch + 1) * 64)
                K_ = lambda n: n + str(q)
                b = B.bank()
                pk = "ps%d" % b
                B.mm(self.ps[b][0:64, 0:64], self.BT[:, cs_], self.CT[:, cs_], True, True, ("BT", "CT"), (pk,))
                B.tt("dve", self.cbm[q][:, :], self.ps[b][0:64, 0:64], self.tri, ALU.mult, (pk, "cst"), (K_("cbm"),))
                ab, abk = self.abcs[q], K_("abcs")
                B.dma("sp", ab[:, :], self.acT_scr[ch, 8 * g:8 * g + 8, :].rearrange("h i -> (h i)").rearrange("(o n) -> o n", o=1)
                      .broadcast_to([128, 512]), abk, ("acT",), (abk,))
                B.act(self.Ebc[q][:, :], ab[:, :], AF.Exp, (abk,), (K_("Ebc"),))
                B.tt("dve", self.seg[q][:, :].rearrange("p (h i) -> p h i", h=8), ab[0:64, :].rearrange("p (h i) -> p h i", h=8),
                     acum[:, ch, 8 * g:8 * g + 8].unsqueeze(2).broadcast_to([64, 8, 64]), ALU.subtract, (abk, "acum"), (K_("seg"),))
                B.ts("dve", self.seg[q][:, :], self.seg[q][:, :], 0.0, None, ALU.min, None, (K_("seg"),), (K_("seg"),))
                B.act(self.seg[q][:, :], self.seg[q][:, :], AF.Exp, (K_("seg"),), (K_("seg"),))
                B.tt("dve", self.MT[q][:, :, :], self.seg[q][:, :].rearrange("p (h i) -> p h i", h=8),
                     self.cbm[q][:, :].unsqueeze(1).broadcast_to([64, 8, 64]), ALU.mult, (K_("seg"), K_("cbm")), (K_("MT"),))
                B.tt("pool", self.Ce[q][:, :, :], self.Ebc[q][:, :].rearrange("p (h i) -> p h i", h=8),
                     self.CT[:, cs_].unsqueeze(1).broadcast_to([128, 8, 64]), ALU.mult, (K_("Ebc"), "CT"), (K_("Ce"),))
                bt = B.bank()
                pt = "ps%d" % bt
                for j in range(4):
                    B.tr(self.ps[bt][0:64, j * 128:(j + 1) * 128], self.xsT[:, j, cs_], self.identf, ("xsT%d" % ch, "cst"), (pt,))
                pv = self.ps[bt][0:64, :].rearrange("p (h d) -> p h d", h=8)
                B.tt("dve", self.xdt[q][:, :].rearrange("p (h d) -> p h d", h=8), pv,
                     dtt[:, ch, 8 * g:8 * g + 8].unsqueeze(2).broadcast_to([64, 8, 64]), ALU.mult, (pt, "dtt"), (K_("xdt"),))
                B.tt("dve", self.xdtd[q][:, :].rearrange("p (h d) -> p h d", h=8), pv,
                     dtdec[:, ch, 8 * g:8 * g + 8].unsqueeze(2).broadcast_to([64, 8, 64]), ALU.mult, (pt, "dtdec"), (K_("xdtd"),))
                b2 = B.bank()
                p2 = "ps%d" % b2
                B.tr(B.psb(b2)[0:64, 0:128], self.BT[:, cs_], self.identb[:, :], ("BT", "cst2"), (p2,))
                B.cp("act", self.Btok[q][:, :], B.psb(b2)[0:64, 0:128], (p2,), (K_("Btok"),))

            def yst(ch, g=g):
                q = ch % 2
                cs_ = slice(ch * 64, (ch + 1) * 64)
                K_ = lambda n: n + str(q)
                by = B.bank()
                py = "ps%d" % by
                for hh in range(8):
                    h = 8 * g + hh
                    o_ = self.ps[by][64 * (hh % 2):64 * (hh % 2) + 64, (hh // 2) * 64:(hh // 2 + 1) * 64]
                    B.mm(o_, self.xdt[q][:, hh * 64:(hh + 1) * 64], self.MT[q][:, hh, :], True, False, (K_("xdt"), K_("MT")), (py,))
                    B.mm(o_, self.hbf[:, h * 64:(h + 1) * 64], self.Ce[q][:, hh, :], False, True, (hbk, K_("Ce")), (py,))
                B.tt("pool", self.xsT[:, :, cs_], self.xsT[:, :, cs_], self.dcol[:, 4 * g:4 * g + 4].unsqueeze(2).broadcast_to([128, 4, 64]),
                     ALU.mult, ("xsT%d" % ch, "cst"), ("xsT%d" % ch,))
                B.tt("dve", self.xsT[:, :, cs_], self.xsT[:, :, cs_], self.ps[by][:, 0:256].rearrange("p (j i) -> p j i", j=4),
                     ALU.add, ("xsT%d" % ch, py), ("xsT%d" % ch,))
                bs_ = B.bank()
                ps_ = "ps%d" % bs_
                B.mm(self.ps[bs_][:, :], self.Btok[q][:, :], self.xdtd[q][:, :], True, True, (K_("Btok"), K_("xdtd")), (ps_,))
                B.tt("pool", self.htmp[:, :].rearrange("p (h d) -> p h d", h=8), self.hst[:, hs].rearrange("p (h d) -> p h d", h=8),
                     etot[:, ch, 8 * g:8 * g + 8].unsqueeze(2).broadcast_to([128, 8, 64]), ALU.mult, (hk, "etot"), ("htmp",))
                B.tt("dve", self.hst[:, hs], self.htmp[:, :], self.ps[bs_][:, :], ALU.add, ("htmp", ps_), (hk,))
                B.cp("act", self.hbf[:, hs], self.hst[:, hs], (hk,), (hbk,))

            prep(0)
            for ch in range(NCH):
                if ch + 1 < NCH:
                    prep(ch + 1)
                yst(ch)
            if "yscan" in self.dbg_out and c["last"]:
                B.dma("pool", self.dbg_out["yscan"][:, 4 * g:4 * g + 4, :], self.xsT[:, :, :], "dbg_yscan", xk_all, ())
            wz, kz = B.wload("z", g)
            bn = B.bank()
            pn = "ps%d" % bn
            for j in range(4):
                b = B.bank()
                pk = "ps%d" % b
                for kc in range(16):
                    B.mm(self.ps[b][:, 0:T], wz[:, kc, j * 128:(j + 1) * 128], uT[:, kc, 0:T], kc == 0, kc == 15, ("uT", kz), (pk,))
                i = B.rot("zs", 2)
                B.act(self.zs[i][:, 0:T], self.ps[b][:, 0:T], AF.Silu, (pk,), ("zs%d" % i,))
                B.tt("dve", self.xsT[:, j, 0:T], self.xsT[:, j, 0:T], self.zs[i][:, 0:T], ALU.mult, xk_all + ("zs%d" % i,), xk_all)
                B.act(self.sq[i][:, 0:T], self.xsT[:, j, 0:T], AF.Square, xk_all, ("sq%d" % i,))
                B.mm(self.ps[bn][:, 0:T], self.onesf[:, :], self.sq[i][:, 0:T], j == 0, j == 3, ("sq%d" % i, "cst2"), (pn,))
            B.act(self.rt[:, 0:T], self.ps[bn][:, 0:T], AF.Sqrt, (pn,), ("rt",), bias=self.epsc[:, 0:1], scale=1.0 / 512)
            B.recip(self.rt[:, 0:T], self.rt[:, 0:T], ("rt",), ("rt",))
            for j in range(4):
                pc = 4 * g + j
                B.stt("dve", self.ynT[:, pc, 0:T], self.xsT[:, j, 0:T], self.g_ssdn[:, pc:pc + 1], self.rt[:, 0:T],
                      ALU.mult, ALU.mult, xk_all + ("rt", "cst"), ("ynT",))
        if "ynT" in self.dbg_out and c["last"]:
            B.dma("pool", self.dbg_out["ynT"], self.ynT[:], "dbg_ynT", ("ynT",), ())

    def _s7_merge(self, c):
        B = self
        T, TP = c["T"], c["TP"]
        NTT = T // TP
        uT, st = self.uT, self.st
        t1buf = self.mixsb[:, :].rearrange("p (j t) -> p j t", j=4)
        sga = self.xr[:, :].rearrange("p (j t) -> p j t", j=4)
        sgb = self.grow[:, :].rearrange("p (j t) -> p j t", j=4)
        for cb in range(4):
            wga, kga = B.wload("gate", cb)
            for j in range(4):
                cl = j * 128
                b3 = B.bank()
                p3 = "ps%d" % b3
                for kc in range(16):
                    B.mm(self.ps[b3][:, 0:T], wga[:, kc, cl:cl + 128], uT[:, kc, 0:T], kc == 0, kc == 15, ("uT", kga), (p3,))
                B.act(sga[:, j, 0:T], self.ps[b3][:, 0:T], AF.Sigmoid, (p3,), ("xr",))
            wm, km = B.wload("omla", cb)
            for j in range(4):
                cl = j * 128
                b1 = B.bank()
                p1 = "ps%d" % b1
                for kc in range(16):
                    B.mm(self.ps[b1][:, 0:T], wm[:, kc, cl:cl + 128], self.attT[:, kc, 0:T], kc == 0, kc == 15, ("attT", km), (p1,))
                B.tt("dve", t1buf[:, j, 0:T], self.ps[b1][:, 0:T], sga[:, j, 0:T], ALU.mult, (p1, "xr"), ("mixsb",))
            wgb, kgb = B.wload("gate", 4 + cb)
            for j in range(4):
                cl = j * 128
                b4 = B.bank()
                p4 = "ps%d" % b4
                for kc in range(16):
                    B.mm(self.ps[b4][:, 0:T], wgb[:, kc, cl:cl + 128], uT[:, kc, 0:T], kc == 0, kc == 15, ("uT", kgb), (p4,))
                B.act(sgb[:, j, 0:T], self.ps[b4][:, 0:T], AF.Sigmoid, (p4,), ("grow",))
            banks = [B.bank() for _ in range(4)]
            for kp in range(2):
                ws, ks = B.wload("ossd", cb, kp)
                for j in range(4):
                    cl = j * 128
                    b2 = banks[j]
                    for kc in range(16):
                        B.mm(self.ps[b2][:, 0:T], ws[:, kc, cl:cl + 128], self.ynT[:, kp * 16 + kc, 0:T],
                             kp == 0 and kc == 0, kp == 1 and kc == 15, ("ynT", ks), ("ps%d" % b2,))
            for j in range(4):
                dc = 4 * cb + j
                b2 = banks[j]
                i = B.rot("t12", 2)
                B.tt("dve", self.t12[i][:, 0:T], self.ps[b2][:, 0:T], sgb[:, j, 0:T], ALU.mult, ("ps%d" % b2, "grow"), ("t12%d" % i,))
                B.tt("pool", self.smT[:, dc, 0:T], t1buf[:, j, 0:T], self.t12[i][:, 0:T], ALU.add, ("mixsb", "t12%d" % i), ("smT",))
        if "smT" in self.dbg_out and c["last"]:
            B.dma("pool", self.dbg_out["smT"], self.smT[:], "dbg_smT", ("smT",), ())
        B.dma("sp", self.grow[:, :], self.w["post_mix_g"].rearrange("(o n) -> o n", o=1).broadcast_to([128, D]), "grow", (), ("grow",))
        for tt in range(NTT):
            tsl = slice(tt * TP, (tt + 1) * TP)
            B.dma("sp", self.xr[0:TP, :], c["xsrc"][tsl, :], "xr", (), ("xr",))
            for cb in range(4):
                wo, ko = B.wload("out", cb)
                b = B.bank()
                pk = "ps%d" % b
                for kc in range(16):
                    B.mm(self.ps[b][0:TP, :], self.smT[:, kc, tsl], wo[:, kc, :], kc == 0, kc == 15, ("smT", ko), (pk,))
                B.cp("act", self.mixsb[0:TP, cb * 512:(cb + 1) * 512], self.ps[b][0:TP, :], (pk,), ("mixsb",))
            c3, c4 = 12 + tt, 16 + tt
            B.act(self.xnb1[0:TP, :], self.mixsb[0:TP, :], AF.Square, ("mixsb",), ("sgk", "st%d" % c3), accum=st[0:TP, c3:c3 + 1])
            B.rstd(c3, D, TP)
            B.stt("dve", self.mixsb[0:TP, :], self.mixsb[0:TP, :], st[0:TP, c3:c3 + 1], self.grow[0:TP, :], ALU.mult, ALU.mult,
                  ("mixsb", "st%d" % c3, "grow"), ("mixsb",))
            B.tt("dve", self.xr[0:TP, :], self.xr[0:TP, :], self.mixsb[0:TP, :], ALU.add, ("xr", "mixsb"), ("xr",))
            B.dma("pool", c["xmid"][tsl, :], self.xr[0:TP, :], "xmid_w", ("xr",), ("xmid" + c["tag"],))
            B.act(self.xnb1[0:TP, :], self.xr[0:TP, :], AF.Square, ("xr",), ("sgk", "st%d" % c4), accum=st[0:TP, c4:c4 + 1])
            B.rstd(c4, D, TP)
            B.ts("dve", self.xnb1[0:TP, :], self.xr[0:TP, :], st[0:TP, c4:c4 + 1], None, ALU.mult, None, ("xr", "st%d" % c4), ("sgk",))
            for half in range(2):
                b = B.bank()
                pk = "ps%d" % b
                for k8 in range(8):
                    kc = half * 8 + k8
                    B.tr(B.psb(b)[:, k8 * TP:(k8 + 1) * TP], self.xnb1[0:TP, kc * 128:(kc + 1) * 128], self.identb[0:TP, 0:TP],
                         ("sgk", "cst2"), (pk,))
                B.tt("dve", uT[:, half * 8:half * 8 + 8, tsl], B.psb(b)[:, 0:8 * TP].rearrange("p (k t) -> p k t", k=8),
                     self.g_preffn[:, half * 8:half * 8 + 8].unsqueeze(2).broadcast_to([128, 8, TP]), ALU.mult, (pk, "cst"), ("uT",))
        if "xnT" in self.dbg_out and c["last"]:
            B.dma("pool", self.dbg_out["xnT"], self.uT[:], "dbg_xnT", ("uT",), ())

    def _s8_ffn_up(self, c):
        B = self
        T = c["T"]
        uT = self.uT
        for cb1 in range(11):
            for half in range(2):
                wt_, wk_ = B.wload("up", cb1 + 11 * half)
                for j in range(4):
                    fc = 4 * cb1 + j
                    cl = j * 128
                    b = B.bank()
                    pk = "ps%d" % b
                    for kc in range(16):
                        B.mm(self.ps[b][:, 0:T], wt_[:, kc, cl:cl + 128], uT[:, kc, 0:T], kc == 0, kc == 15, ("uT", wk_), (pk,))
                    i = B.rot("fpre", 4)
                    B._conv(b, T, 3, self.fconv_w, self.fconv_b, self.fhist, fc + 44 * half, self.fpre[i], "fpre%d" % i,
                            self.facc[i], "facc%d" % i, "dve")
                    if half == 0:
                        B.act(self.asil[j][:, 0:T], self.facc[i][:, 0:T], AF.Silu, ("facc%d" % i,), ("asil%d" % j,))
                    else:
                        B.tt("pool", self.hdnT[:, fc, 0:T], self.asil[j][:, 0:T], self.facc[i][:, 0:T], ALU.mult,
                             ("asil%d" % j, "facc%d" % i), ("hdnT",))
        if "hdnT" in self.dbg_out and c["last"]:
            B.dma("pool", self.dbg_out["hdnT"], self.hdnT[:], "dbg_hdnT", ("hdnT",), ())

    def _s9_ffn_down(self, c):
        B = self
        T, TP, tag = c["T"], c["TP"], c["tag"]
        NTT = T // TP
        st = self.st
        B.dma("sp", self.grow2[:, :], self.w["post_ffn_g"].rearrange("(o n) -> o n", o=1).broadcast_to([128, D]), "grow2", (), ("grow2",))
        for cb in range(4):
            banks = [B.bank() for _ in range(NTT)]
            for kp in range(4):
                wd, kd = B.wload("down", cb, kp)
                for tt in range(NTT):
                    b = banks[tt]
                    for kcl in range(11):
                        fc = kp * 11 + kcl
                        B.mm(self.ps[b][0:TP, :], self.hdnT[:, fc, tt * TP:(tt + 1) * TP], wd[:, kcl, :], fc == 0, fc == 43,
                             ("hdnT", kd), ("ps%d" % b,))
            for tt in range(NTT):
                b = banks[tt]
                B.cp("act" if tt % 2 else "dve", self.dnsb[0:TP, tt, cb * 512:(cb + 1) * 512], self.ps[b][0:TP, :], ("ps%d" % b,), ("dnsb",))
        for tt in range(NTT):
            tsl = slice(tt * TP, (tt + 1) * TP)
            B.dma("sp", self.xr2[0:TP, :], c["xmid"][tsl, :], "xr2", ("xmid" + tag,), ("xr2",))
            c5 = 20 + tt
            B.act(self.hdnT[0:TP, 0:4, :].rearrange("p a b -> p (a b)"), self.dnsb[0:TP, tt, :], AF.Square, ("dnsb",), ("hdnT", "st%d" % c5),
                  accum=st[0:TP, c5:c5 + 1])
            B.rstd(c5, D, TP)
            B.stt("dve", self.dnsb[0:TP, tt, :], self.dnsb[0:TP, tt, :], st[0:TP, c5:c5 + 1], self.grow2[0:TP, :], ALU.mult, ALU.mult,
                  ("dnsb", "st%d" % c5, "grow2"), ("dnsb",))
            B.tt("dve", self.xr2[0:TP, :], self.xr2[0:TP, :], self.dnsb[0:TP, tt, :], ALU.add, ("xr2", "dnsb"), ("xr2",))
            B.dma("pool", c["o_y"][tsl, :], self.xr2[0:TP, :], "o_y" + tag, ("xr2",), ())


def hist_key(hist, B):
    return "shist" if hist is B.shist else "fhist"


def build_nc(**kw):
    return Builder(**kw).build()


def _tables():
    half = 32
    inv = (np.float32(10000.0) ** (-np.arange(half, dtype=np.float32) / np.float32(half))).astype(np.float32)
    pos = np.concatenate([np.arange(SEQ), PAST + np.arange(DEC_SEQ), np.zeros(64)]).astype(np.float32)
    ang = pos[:, None] * inv[None, :]
    cos = np.cos(ang).astype(np.float32).reshape(65, 128, 32).transpose(1, 0, 2)
    sin = np.sin(ang).astype(np.float32).reshape(65, 128, 32).transpose(1, 0, 2)
    k = np.arange(512)[:, None] // 64
    q = np.arange(512)[None, :] // 64
    am = (k <= q).astype(np.float32).reshape(4, 128, 512).transpose(1, 0, 2)
    misc = np.zeros((128, 1024), np.float32)
    misc[:, 0:128] = np.eye(128, dtype=np.float32)
    j = np.arange(64)[:, None]
    i = np.arange(64)[None, :]
    misc[0:64, 128:192] = (j <= i)
    misc[0:64, 192:704] = np.tile((j <= i).astype(np.float32), (1, 8))
    misc[0:64, 704:768] = np.where(j > i, NEG, 0.0)
    return dict(t_cos=np.ascontiguousarray(cos), t_sin=np.ascontiguousarray(sin),
                t_amask=np.ascontiguousarray(am), t_misc=misc)


def make_in_maps(inputs):
    tabs = _tables()
    f = lambda a: np.ascontiguousarray(np.asarray(a, dtype=np.float32))
    wnames = ["pre_mix_g", "w_in", "q_norm_g", "w_uq", "kv_norm_g", "w_ukv", "ssd_conv_w", "ssd_conv_b", "ssd_dt_bias",
              "ssd_A_log", "ssd_D", "ssd_norm_g", "w_o_mla", "w_o_ssd", "w_out", "post_mix_g", "pre_ffn_g", "w_up",
              "ffn_conv_w", "ffn_conv_b", "w_down", "post_ffn_g"]
    shared = {n: f(inputs[n][0]) for n in wnames}
    shared.update(tabs)
    maps = []
    zero_prompt = np.zeros((SEQ, D), np.float32)
    for c in range(8):
        m = dict(shared)
        m["xp"] = f(inputs["x_prompt"][PROMPT_CORES.index(c)]) if c in PROMPT_CORES else zero_prompt
        m["xs"] = f(inputs["x_sample"][c])
        m["c_ckv"] = f(inputs["cache_mla_ckv"][0, c])
        m["c_kpe"] = f(inputs["cache_mla_kpe"][0, c])
        m["c_sconv"] = f(inputs["state_ssd_conv"][0, c])
        m["c_ssd"] = f(inputs["state_ssd"][0, c]).reshape(SH * SP_, SN)
        m["c_fconv"] = f(inputs["state_ffn_conv"][0, c])
        maps.append(m)
    return maps


def kernel(**inputs):
    nc = build_nc()
    res = run_bass_kernel_spmd(nc, make_in_maps(inputs), core_ids=list(range(8)))
    r = res.results
    st = lambda name, cores: np.stack([r[c][name] for c in cores])[None]
    P4, S8 = PROMPT_CORES, range(8)
    return (st("o_yp", P4)[0], st("o_ys", S8)[0],
            st("o_ckvp", P4), st("o_kpep", P4), st("o_sconvp", P4),
            st("o_ssdp", P4).reshape(1, 4, SH, SP_, SN), st("o_fconvp", P4),
            st("o_ckvs", S8), st("o_kpes", S8), st("o_sconvs", S8),
            st("o_ssds", S8).reshape(1, 8, SH, SP_, SN), st("o_fconvs", S8))
```

```python
from contextlib import ExitStack

import numpy as np
import concourse.bass as bass
import concourse.mybir as mybir
from concourse.bass_utils import run_bass_kernel_spmd

F32 = mybir.dt.float32
BF16 = mybir.dt.bfloat16
AF = mybir.ActivationFunctionType
ALU = mybir.AluOpType

D = 2048
SEQ = 8192
DEC_SEQ = 64
PAST = 2048
H = 16
QL = 512
KVL = 512
ROPE = 64
SI = 4096
SH = 64
SP_ = 64
SG = 8
SN = 128
CONV_DIM = 6144
DFF = 5632
EPS = 1e-6
OFF_Q, OFF_KV, OFF_Z, OFF_XBC, OFF_DT, OFF_GATE = 0, 512, 1088, 5184, 11328, 11392
IN_DIM = 15488
SCALE = 192 ** -0.5
NEG = -30000.0
PROMPT_CORES = (0, 1, 4, 5)


class Op:
    __slots__ = ("eng", "fn", "deps", "sem", "val", "needed", "idx")


class Prog:
    ENGS = ("pe", "act", "dve", "pool", "sp")

    def __init__(self, nc):
        self.nc = nc
        self.ops = {e: [] for e in self.ENGS}
        self.res = {}
        self.dma_cnt = {}

    def add(self, eng, fn, reads=(), writes=(), dma=None):
        op = Op()
        op.eng, op.fn, op.sem, op.val, op.needed, op.idx = eng, fn, None, 0, False, 0
        if dma is not None:
            self.dma_cnt[dma] = self.dma_cnt.get(dma, 0) + 1
            op.sem, op.val = dma, 16 * self.dma_cnt[dma]
        stream = ("dma:" + dma) if dma is not None else eng
        deps = []
        for k in reads:
            st = self.res.get(k)
            if st is not None and st[0] is not None:
                deps.append(st[0])
            if st is not None and k.startswith("ps"):
                deps.extend(v for s_, v in st[1].items() if s_ != stream)
        for k in writes:
            st = self.res.get(k)
            if st is not None:
                same = lambda d: dma is None and d.sem is None and d.eng == eng
                if st[0] is not None and not same(st[0]):
                    deps.append(st[0])
                deps.extend(v for v in st[1].values() if not same(v))
        out = []
        for d in deps:
            if d is op or d in out:
                continue
            if d.sem is None and d.eng == eng == "pe":
                continue
            d.needed = True
            out.append(d)
        op.deps = out
        for k in reads:
            st = self.res.get(k)
            if st is None:
                st = self.res[k] = [None, {}]
            st[1][stream] = op
        for k in writes:
            self.res[k] = [op, {}]
        self.ops[eng].append(op)
        return op

    def emit(self, es):
        nc = self.nc
        esem = {e: es.enter_context(nc.semaphore("sem_" + e)) for e in self.ENGS}
        dsem = {k: es.enter_context(nc.semaphore("dsem_" + k)) for k in self.dma_cnt}
        for e in self.ENGS:
            c = 0
            for op in self.ops[e]:
                if op.sem is None and op.needed:
                    c += 1
                    op.idx = c
        final = [(dsem[k], 16 * n) for k, n in self.dma_cnt.items()]

        def run(e, eng):
            waited = {}
            for op in self.ops[e]:
                need = {}
                for d in op.deps:
                    if d.sem is not None:
                        s, v, key = dsem[d.sem], d.val, "d" + d.sem
                    else:
                        s, v, key = esem[d.eng], d.idx, d.eng
                    if key not in need or need[key][1] < v:
                        need[key] = (s, v)
                for key, (s, v) in need.items():
                    if waited.get(key, 0) < v:
                        eng.wait_ge(s, v)
                        waited[key] = v
                ins = op.fn(eng)
                if op.sem is not None:
                    ins.then_inc(dsem[op.sem], 16)
                elif op.needed:
                    ins.then_inc(esem[e], 1)
            if e == "sp":
                for s, v in final:
                    eng.wait_ge(s, v)

        block = es.enter_context(nc.Block())

        @block.tensor
        def _(eng):
            run("pe", eng)

        @block.scalar
        def _(eng):
            run("act", eng)

        @block.vector
        def _(eng):
            run("dve", eng)

        @block.gpsimd
        def _(eng):
            run("pool", eng)

        @block.sync
        def _(eng):
            run("sp", eng)


class Arena:
    def __init__(self, B, name, nbytes):
        self.t = B.sb(name, [128, nbytes // 4], F32)
        self.nbytes = nbytes
        self.off = 0

    def reset(self, off=0):
        self.off = off

    def take(self, shape, dt):
        n = 1
        for s in shape[1:]:
            n *= s
        nb = n * (4 if dt == F32 else 2)
        nb = (nb + 31) // 32 * 32
        assert self.off + nb <= self.nbytes, ("arena overflow", shape, self.off, nb, self.nbytes)
        o4 = self.off // 4
        self.off += nb
        ap = self.t[0:shape[0], o4:o4 + nb // 4]
        if dt == BF16:
            ap = ap.bitcast(BF16)
        ap = ap[:, 0:n]
        if len(shape) == 3:
            ap = ap.rearrange("p (a b) -> p a b", a=shape[1])
        elif len(shape) == 4:
            ap = ap.rearrange("p (a b c) -> p a b c", a=shape[1], b=shape[2])
        return ap


class Builder:
    def __init__(self, nslot=16, do_sample=True, stage=99, dbg=()):
        self.nslot = nslot
        self.do_sample = do_sample
        self.stage = stage
        self.dbg = dbg
        self.nc = bass.Bass("TRN2", target_bir_lowering=False)
        self.P = Prog(self.nc)
        self.es = ExitStack()
        self.psrr = 0
        self.wrr = 0
        self.rr = {}

    def din(self, name, shape, dt=F32):
        return self.nc.dram_tensor(name, list(shape), dt, kind="ExternalInput").ap()

    def dout(self, name, shape, dt=F32):
        return self.nc.dram_tensor(name, list(shape), dt, kind="ExternalOutput").ap()

    def dscr(self, name, shape, dt=BF16):
        return self.nc.dram_tensor(name, list(shape), dt, kind="Internal").ap()

    def sb(self, name, shape, dt=F32):
        return self.es.enter_context(self.nc.sbuf_tensor(name, list(shape), dt))

    def mm(self, out, lhsT, rhs, start, stop, r, w):
        self.P.add("pe", lambda e: e.matmul(out, lhsT=lhsT, rhs=rhs, start=start, stop=stop), r, w)

    def tr(self, out, in_, ident, r, w):
        self.P.add("pe", lambda e: e.transpose(out, in_, ident), r, w)

    def act(self, out, in_, func, r, w, bias=None, scale=None, accum=None):
        kw = {}
        if bias is not None:
            kw["bias"] = bias
        if scale is not None:
            kw["scale"] = scale
        if accum is not None:
            kw["accum_out"] = accum
        self.P.add("act", lambda e: e.activation(out, in_, func, **kw), r, w)

    def tt(self, eng, out, in0, in1, op, r, w):
        self.P.add(eng, lambda e: e.tensor_tensor(out, in0, in1, op), r, w)

    def ts(self, eng, out, in0, s1, s2, op0, op1, r, w):
        if op1 is None:
            self.P.add(eng, lambda e: e.tensor_scalar(out, in0, s1, None, op0), r, w)
        else:
            self.P.add(eng, lambda e: e.tensor_scalar(out, in0, s1, s2, op0, op1), r, w)

    def stt(self, eng, out, in0, scalar, in1, op0, op1, r, w):
        self.P.add(eng, lambda e: e.scalar_tensor_tensor(out, in0, scalar, in1, op0, op1), r, w)

    def cp(self, eng, out, in_, r, w):
        if eng == "act":
            self.P.add("act", lambda e: e.copy(out, in_), r, w)
        else:
            self.P.add(eng, lambda e: e.tensor_copy(out, in_), r, w)

    def recip(self, out, in_, r, w):
        self.P.add("dve", lambda e: e.reciprocal(out, in_), r, w)

    def memset(self, eng, ap, val, w):
        self.P.add(eng, lambda e: e.memset(ap, val), (), w)

    def dma(self, q, out, in_, sem, r, w, **kw):
        self.P.add(q, lambda e: e.dma_start(out=out, in_=in_, **kw), r, w, dma=sem)

    def barrier(self, rkeys=(), wkeys=()):
        t = self.bar_t
        self.P.add("pool", lambda e: e.memset(t[:, 0:1], 0.0), (), tuple(self.KALL) + tuple(rkeys) + tuple(wkeys) + ("bar_t",))

    def bank(self):
        i = self.psrr
        self.psrr = (self.psrr + 1) % 6
        return i

    def rot(self, name, n):
        i = self.rr.get(name, 0)
        self.rr[name] = (i + 1) % n
        return i

    def psb(self, i):
        return self.ps[i][:].bitcast(BF16)

    def build(self):
        with self.es:
            self._declare()
            self._consts()
            self._layout()
            self._convert_weights()
            self._prompt()
            if self.do_sample:
                self._sample()
            self.P.emit(self.es)
        return self.nc

    def _declare(self):
        B = self
        self.xp = B.din("xp", [SEQ, D])
        self.xs = B.din("xs", [DEC_SEQ, D])
        self.c_ckv = B.din("c_ckv", [PAST, KVL])
        self.c_kpe = B.din("c_kpe", [PAST, ROPE])
        self.c_sconv = B.din("c_sconv", [3, CONV_DIM])
        self.c_ssd = B.din("c_ssd", [SH * SP_, SN])
        self.c_fconv = B.din("c_fconv", [2, 2 * DFF])
        self.w = {}
        for n, shp in [("pre_mix_g", [D]), ("w_in", [D, IN_DIM]), ("q_norm_g", [QL]), ("w_uq", [QL, H, 192]),
                       ("kv_norm_g", [KVL]), ("w_ukv", [KVL, H, 256]), ("ssd_conv_w", [4, CONV_DIM]),
                       ("ssd_conv_b", [CONV_DIM]), ("ssd_dt_bias", [SH]), ("ssd_A_log", [SH]), ("ssd_D", [SH]),
                       ("ssd_norm_g", [SI]), ("w_o_mla", [D, D]), ("w_o_ssd", [SI, D]), ("w_out", [D, D]),
                       ("post_mix_g", [D]), ("pre_ffn_g", [D]), ("w_up", [D, 2 * DFF]), ("ffn_conv_w", [3, 2 * DFF]),
                       ("ffn_conv_b", [2 * DFF]), ("w_down", [DFF, D]), ("post_ffn_g", [D])]:
            self.w[n] = B.din(n, shp)
        self.t_cos = B.din("t_cos", [128, 65, 32])
        self.t_sin = B.din("t_sin", [128, 65, 32])
        self.t_amask = B.din("t_amask", [128, 4, 512])
        self.t_misc = B.din("t_misc", [128, 1024])
        self.o_yp = B.dout("o_yp", [SEQ, D])
        self.o_ckvp = B.dout("o_ckvp", [SEQ, KVL])
        self.o_kpep = B.dout("o_kpep", [SEQ, ROPE])
        self.o_sconvp = B.dout("o_sconvp", [3, CONV_DIM])
        self.o_ssdp = B.dout("o_ssdp", [SH * SP_, SN])
        self.o_fconvp = B.dout("o_fconvp", [2, 2 * DFF])
        self.o_ys = B.dout("o_ys", [DEC_SEQ, D])
        self.o_ckvs = B.dout("o_ckvs", [DEC_SEQ, KVL])
        self.o_kpes = B.dout("o_kpes", [DEC_SEQ, ROPE])
        self.o_sconvs = B.dout("o_sconvs", [3, CONV_DIM])
        self.o_ssds = B.dout("o_ssds", [SH * SP_, SN])
        self.o_fconvs = B.dout("o_fconvs", [2, 2 * DFF])
        self.dbg_out = {}
        for name, shp in self.dbg:
            self.dbg_out[name] = B.dout("dbg_" + name, shp)
        self.ps = [self.es.enter_context(self.nc.psum_tensor("ps%d" % i, [128, 512], F32)) for i in range(8)]

    def _wdecl(self, name, src2d, K, N, nb, kpart=None):
        kc = K // 128
        kpart = kpart or kc
        scr = self.dscr("wb_" + name, [N // nb, kc // kpart, 128, kpart, nb])
        self.wt[name] = (scr, src2d, kc, kpart, nb, N // nb)

    def _convert_weights(self):
        B = self
        w = self.w
        self.wt = {}
        self.convkeys = {}
        win = w["w_in"]
        small = {}
        for name, src, lo, hi in (("uqn", w["w_uq"], 0, 128), ("uqr", w["w_uq"], 128, 192),
                                  ("ukn", w["w_ukv"], 0, 128), ("ukv", w["w_ukv"], 128, 256)):
            small[name] = (src, lo, hi)
        order = [("ckv", win[:, OFF_KV:OFF_KV + 512], D, 512, 512, None), ("kpe", win[:, OFF_KV + 512:OFF_Z], D, 64, 64, None),
                 ("q", win[:, OFF_Q:OFF_Q + 512], D, 512, 512, None), "uqn", "uqr", "ukn", "ukv",
                 ("dt", win[:, OFF_DT:OFF_GATE], D, 64, 64, None),
                 ("xs", win[:, OFF_XBC:OFF_XBC + SI], D, SI, 512, None),
                 ("bm", win[:, OFF_XBC + SI:OFF_XBC + SI + 1024], D, 1024, 128, None),
                 ("cm", win[:, OFF_XBC + SI + 1024:OFF_DT], D, 1024, 128, None),
                 ("z", win[:, OFF_Z:OFF_XBC], D, SI, 512, None),
                 ("gate", win[:, OFF_GATE:IN_DIM], D, 2 * D, 512, None),
                 ("omla", w["w_o_mla"], D, D, 512, None), ("ossd", w["w_o_ssd"], SI, D, 512, 16),
                 ("out", w["w_out"], D, D, 512, None), ("up", w["w_up"], D, 2 * DFF, 512, None),
                 ("down", w["w_down"], DFF, D, 512, 11)]
        for item in order:
            if not isinstance(item, str):
                name, src, K, N, nb, kpart = item
                B._wdecl(name, src, K, N, nb, kpart)
        self._conv_items = order
        self._conv_small = small
        self._emit_conversions(0, 12)

    def _emit_conversions(self, lo, hi):
        B = self
        small = self._conv_small
        for item in self._conv_items[lo:hi]:
            if isinstance(item, str):
                name = item
                src, lo, hi = small[name]
                wd = hi - lo
                scr = self.dscr("wb_" + name, [1, 1, 128, 4, H * wd])
                self.wt[name] = (scr, None, 4, 4, H * wd, 1)
                self.convkeys[name] = []
                for rc in range(4):
                    ck = "wc_%s_%d" % (name, rc)
                    self.convkeys[name].append(ck)
                    B.dma("pool", scr[0, 0, :, rc, :].rearrange("p (h d) -> p h d", h=H),
                          src[rc * 128:(rc + 1) * 128, :, lo:hi], "wc_" + name, (), (ck,))
                continue
            name = item[0]
            scr, src, kc, kpart, nb, ncb = self.wt[name]
            self.convkeys[name] = []
            for cb in range(ncb):
                for kp in range(kc // kpart):
                    s_ = src[kp * kpart * 128:(kp + 1) * kpart * 128, cb * nb:(cb + 1) * nb]
                    s_ = s_.rearrange("(kc p) n -> p kc n", p=128)
                    ck = "wc_%s_%d_%d" % (name, cb, kp)
                    self.convkeys[name].append(ck)
                    B.dma("pool", scr[cb, kp], s_, "wc_" + name, (), (ck,))

    def wload(self, name, cb=0, kp=0):
        scr, _, kc, kpart, nb, ncb = self.wt[name]
        i = self.wrr
        self.wrr = (self.wrr + 1) % len(self.wbuf)
        buf = self.wbuf[i]
        key = "wbuf%d" % i
        ap = buf[:, 0:kpart * nb].rearrange("p (k n) -> p k n", k=kpart)
        assert self.convkeys.get(name), ("weight used before its conversion was issued", name)
        self.dma("sp", ap, scr[cb, kp], key, tuple(self.convkeys[name]), (key,))
        return ap, key

    def _consts(self):
        B = self
        w = self.w
        sb = B.sb
        self.bar_t = sb("bar_t", [128, 8], F32)
        self.wbuf = [sb("wbuf%d" % i, [128, 8192], BF16) for i in range(3)]
        self.misc = sb("misc", [128, 1024], F32)
        B.dma("sp", self.misc[:], self.t_misc[:, :], "cst", (), ("cst",))
        self.identf = self.misc[:, 0:128]
        self.tri = self.misc[0:64, 128:192]
        self.amask = sb("amask", [128, 4, 512], BF16)
        B.dma("pool", self.amask[:], self.t_amask[:, :, :], "cstp", (), ("cst",))
        self.identb = sb("identb", [128, 128], BF16)
        B.cp("dve", self.identb[:], self.identf, ("cst",), ("cst2",))
        self.onesb = sb("onesb", [128, 128], BF16)
        B.memset("dve", self.onesb[:], 1.0, ("cst2",))
        self.onesf = sb("onesf", [128, 128], F32)
        B.memset("dve", self.onesf[:], 1.0, ("cst2",))
        self.epsc = sb("epsc", [128, 1], F32)
        B.memset("dve", self.epsc[:], EPS, ("cst2",))
        self.st = sb("st", [128, 24], F32)

        def col(name, src, n):
            t = sb(name, [128, n // 128], F32)
            with self.nc.allow_non_contiguous_dma(reason="tiny one-time gain/bias column loads"):
                B.dma("sp", t[:], src.rearrange("(c p) -> p c", p=128), "cst", (), ("cst",), allow_slow_non_contiguous=True)
            return t
        self.g_premix = col("g_premix", w["pre_mix_g"], D)
        self.g_preffn = col("g_preffn", w["pre_ffn_g"], D)
        self.g_qn = col("g_qn", w["q_norm_g"], QL)
        self.g_ssdn = col("g_ssdn", w["ssd_norm_g"], SI)
        self.sconv_b = col("sconv_b", w["ssd_conv_b"], CONV_DIM)
        self.fconv_b = col("fconv_b", w["ffn_conv_b"], 2 * DFF)
        self.sconv_w = sb("sconv_w", [128, 4, CONV_DIM // 128], F32)
        self.fconv_w = sb("fconv_w", [128, 3, 2 * DFF // 128], F32)
        self.dcol = sb("dcol", [128, 32], F32)
        with self.nc.allow_non_contiguous_dma(reason="tiny one-time conv tap loads"):
            for k in range(4):
                B.dma("sp", self.sconv_w[:, k, :], w["ssd_conv_w"][k].rearrange("(c p) -> p c", p=128), "cst", (), ("cst",), allow_slow_non_contiguous=True)
            for k in range(3):
                B.dma("sp", self.fconv_w[:, k, :], w["ffn_conv_w"][k].rearrange("(c p) -> p c", p=128), "cst", (), ("cst",), allow_slow_non_contiguous=True)
            dsrc = w["ssd_D"].rearrange("(c two) -> two c", two=2)
            for half in range(2):
                B.dma("sp", self.dcol[half * 64:(half + 1) * 64, :], dsrc[half:half + 1, :].broadcast_to([64, 32]),
                      "cst", (), ("cst",), allow_slow_non_contiguous=True)

        def row(name, src, n, parts=128):
            t = sb(name, [parts, n], F32)
            B.dma("sp", t[:], src.rearrange("(o n) -> o n", o=1).broadcast_to([parts, n]), "cst", (), ("cst",))
            return t
        self.g_kvn_r = row("g_kvn_r", w["kv_norm_g"], KVL)
        self.dtb_r = row("dtb_r", w["ssd_dt_bias"], SH, 64)
        self.alog_r = row("alog_r", w["ssd_A_log"], SH, 64)
        self.a_r = sb("a_r", [64, SH], F32)
        B.act(self.a_r[:], self.alog_r[:], AF.Exp, ("cst",), ("cst2",))
        B.ts("dve", self.a_r[:], self.a_r[:], -1.0, None, ALU.mult, None, ("cst2",), ("cst2",))
        self.hst = sb("hst", [128, SI], F32)
        self.hbf = sb("hbf", [128, SI], BF16)
        self.shist = sb("shist", [128, 48, 3], F32)
        self.fhist = sb("fhist", [128, 88, 2], F32)
        self.uT = sb("uT", [128, 16, 512], BF16)

    def _layout(self):
        X = self.arX = Arena(self, "arenaX", 48 * 1024)
        Y = self.arY = Arena(self, "arenaY", 56 * 1024 - 64)
        X.reset()
        self.attT = X.take([128, H, 512], BF16)
        o = X.off
        self.qnT = X.take([128, H, 512], BF16)
        self.qpeT = X.take([128, 8, 512], BF16)
        self.qpeb = X.take([128, 4, H * ROPE], BF16)
        X.reset(o)
        self.ynT = X.take([128, 32, 512], BF16)
        X.reset()
        self.hdnT = X.take([128, 44, 512], BF16)
        Y.reset()
        self.xin = [Y.take([128, D], F32) for _ in range(2)]
        self.xnb = Y.take([128, 4, D], BF16)
        self.junk = Y.take([128, D], BF16)
        self.KA1 = ("xin0", "xin1", "xnb", "junk")
        Y.reset()
        self.ckvo = [Y.take([128, KVL], F32) for _ in range(2)]
        self.ckvb = Y.take([128, 4, KVL], BF16)
        self.ckvT = Y.take([128, 4, 512], BF16)
        self.kpeo = Y.take([128, 4, ROPE], F32)
        self.kpeb = Y.take([128, 4, 128], BF16)
        self.kpeT = Y.take([128, 512], BF16)
        self.rtmp = Y.take([128, 4, 256], F32)
        self.cqnb = Y.take([128, 4, QL], BF16)
        self.cqnT = Y.take([128, 4, 512], BF16)
        self.kst = [Y.take([128, 512], BF16) for _ in range(4)]
        self.vst = [Y.take([128, 512], BF16) for _ in range(4)]
        self.junk2 = Y.take([128, 512], BF16)
        self.cs = Y.take([128, 2, 4, 32], F32)
        self.KA2 = ("ckvo0", "ckvo1", "ckvb", "ckvT", "kpeo", "kpeb", "kpeT", "rtmp", "cqnb", "cqnT",
                    "kst0", "kst1", "kst2", "kst3", "vst0", "vst1", "vst2", "vst3", "junk2", "cs")
        Y.reset()
        self.kpeK = Y.take([128, SEQ], BF16)
        self.kp_ = [Y.take([128, 1024], BF16) for _ in range(4)]
        self.vp_ = [Y.take([128, 8, 128], BF16) for _ in range(4)]
        self.pT = [Y.take([128, 512], BF16) for _ in range(3)]
        self.rcp = [Y.take([128, 512], F32) for _ in range(2)]
        self.KB = ("kpeK", "kp0", "kp1", "kp2", "kp3", "vp0", "vp1", "vp2", "vp3", "pT0", "pT1", "pT2", "rcp0", "rcp1")
        Y.reset()
        self.dtt = Y.take([64, 8, 64], F32)
        self.atok = Y.take([64, 8, 64], F32)
        self.acum = Y.take([64, 8, 64], F32)
        self.etot = Y.take([128, 8, 64], F32)
        self.decs = Y.take([64, 8, 64], F32)
        self.dtdec = Y.take([64, 8, 64], F32)
        self.xsT = Y.take([128, 4, 512], F32)
        self.BT = Y.take([128, 512], BF16)
        self.CT = Y.take([128, 512], BF16)
        self.pre = [Y.take([128, 516], F32) for _ in range(2)]
        self.acc = [Y.take([128, 512], F32) for _ in range(2)]
        o = Y.off
        self.Btok = [Y.take([64, 128], BF16) for _ in range(2)]
        self.xdt = [Y.take([64, 512], BF16) for _ in range(2)]
        self.xdtd = [Y.take([64, 512], BF16) for _ in range(2)]
        self.cbm = [Y.take([64, 64], F32) for _ in range(2)]
        self.Ebc = [Y.take([128, 512], F32) for _ in range(2)]
        self.seg = [Y.take([64, 512], F32) for _ in range(2)]
        self.MT = [Y.take([64, 8, 64], BF16) for _ in range(2)]
        self.Ce = [Y.take([128, 8, 64], BF16) for _ in range(2)]
        self.abcs = [Y.take([128, 512], F32) for _ in range(2)]
        self.htmp = Y.take([128, 512], F32)
        Y.reset(o)
        self.zs = [Y.take([128, 512], F32) for _ in range(2)]
        self.sq = [Y.take([128, 512], F32) for _ in range(2)]
        self.rt = Y.take([128, 512], F32)
        self.KC = ("dtt", "atok", "acum", "etot", "decs", "dtdec", "xsT0", "xsT1", "xsT2", "xsT3", "xsT4", "xsT5", "xsT6", "xsT7", "BT", "CT", "pre0", "pre1", "acc0", "acc1",
                   "htmp", "zs0", "zs1", "sq0", "sq1", "rt") + tuple(
                       n + str(i) for n in ("Btok", "xdt", "xdtd", "cbm", "Ebc", "seg", "MT", "Ce", "abcs") for i in range(2))
        Y.reset()
        self.smT = Y.take([128, 16, 512], BF16)
        o = Y.off
        self.sg = [Y.take([128, 512], F32) for _ in range(2)]
        Y.reset(o)
        self.xnb1 = Y.take([128, D], BF16)
        self.t12 = [Y.take([128, 512], F32) for _ in range(2)]
        self.mixsb = Y.take([128, D], F32)
        self.xr = Y.take([128, D], F32)
        self.grow = Y.take([128, D], F32)
        self.KD1 = ("smT", "sgk", "t120", "t121", "mixsb", "xr", "grow")
        Y.reset()
        self.fpre = [Y.take([128, 516], F32) for _ in range(4)]
        self.facc = [Y.take([128, 512], F32) for _ in range(4)]
        self.asil = [Y.take([128, 512], F32) for _ in range(4)]
        self.KD2 = ("fpre0", "fpre1", "fpre2", "fpre3", "facc0", "facc1", "facc2", "facc3", "asil0", "asil1", "asil2", "asil3")
        Y.reset()
        self.dnsb = Y.take([128, 4, D], F32)
        self.xr2 = Y.take([128, D], F32)
        self.grow2 = Y.take([128, D], F32)
        self.KD3 = ("dnsb", "xr2", "grow2")
        Y.reset()
        self.stg = Y.take([128, 32, 128], F32)
        self.KALL = tuple(set(self.KA1 + self.KA2 + self.KB + self.KC + self.KD1 + self.KD2 + self.KD3
                              + ("stg", "attT", "qnT", "qpeT", "qpeb", "ynT", "hdnT")))

    def rstd(self, col, n, TP=128):
        st = self.st
        key = "st%d" % col
        self.act(st[0:TP, col:col + 1], st[0:TP, col:col + 1], AF.Sqrt, (key,), (key,), bias=self.epsc[0:TP, 0:1], scale=1.0 / n)
        self.recip(st[0:TP, col:col + 1], st[0:TP, col:col + 1], (key,), (key,))

    def _prompt(self):
        B = self
        self.kT_scr = self.dscr("kT_scr", [H, 128, SEQ])
        self.v_scr = self.dscr("v_scr", [H, 128, SEQ // 128, 128])
        self.kpeT_scr = self.dscr("kpeT_scr", [128, SEQ])
        self.xmid_scr = self.dscr("xmid_scr", [SEQ, D], F32)
        self.acT_scr = self.dscr("acT_scr", [8, SH, 64], F32)
        B.memset("dve", self.hst[:], 0.0, tuple("hst%d" % g for g in range(SG)) + ("hst",))
        B.memset("pool", self.hbf[:], 0.0, tuple("hbf%d" % g for g in range(SG)))
        B.memset("dve", self.shist[:], 0.0, ("shist",))
        B.memset("pool", self.fhist[:], 0.0, ("fhist",))
        for s in range(self.nslot):
            ctx = dict(s=s, T=512, TP=128, xsrc=self.xp[s * 512:(s + 1) * 512, :], pos_tile0=4 * s,
                       o_ckv=self.o_ckvp[s * 512:(s + 1) * 512, :], o_kpe=self.o_kpep[s * 512:(s + 1) * 512, :],
                       o_y=self.o_yp[s * 512:(s + 1) * 512, :], xmid=self.xmid_scr[s * 512:(s + 1) * 512, :],
                       key0=s * 512, tag="p", kT=self.kT_scr, vS=self.v_scr, kpS=self.kpeT_scr, masked=True,
                       last=(s == self.nslot - 1), o_sconv=self.o_sconvp, o_ssd=self.o_ssdp, o_fconv=self.o_fconvp)
            self._slot(ctx)

            if self.stage < 9:
                return
        self._dump_states(self.o_sconvp, self.o_ssdp, self.o_fconvp, "p")

    def _dump_states(self, o_sconv, o_ssd, o_fconv, tag):
        B = self
        B.barrier()
        for q in range(4):
            for k in range(3):
                B.dma("sp", o_sconv[k, q * 1536:(q + 1) * 1536].rearrange("(c p) -> p c", p=128),
                      self.shist[:, q * 12:(q + 1) * 12, k], "o_sconv" + tag, ("shist",), (), allow_slow_non_contiguous=True)
        for q in range(8):
            for k in range(2):
                B.dma("sp", o_fconv[k, q * 1408:(q + 1) * 1408].rearrange("(c p) -> p c", p=128),
                      self.fhist[:, q * 11:(q + 1) * 11, k], "o_fconv" + tag, ("fhist",), (), allow_slow_non_contiguous=True)
        for pc in range(32):
            if pc % 4 == 0:
                b = B.bank()
                pk = "ps%d" % b
            B.tr(self.ps[b][:, (pc % 4) * 128:(pc % 4 + 1) * 128], self.hst[:, pc * 128:(pc + 1) * 128], self.identf,
                 tuple("hst%d" % g for g in range(SG)) + ("hst", "cst"), (pk,))
            if pc % 4 == 3:
                B.cp("act" if (pc // 4) % 2 else "dve", self.stg[:, pc - 3:pc + 1, :],
                     self.ps[b][:, :].rearrange("p (c n) -> p c n", c=4), (pk,), ("stg",))
        B.dma("sp", o_ssd.rearrange("(c p) n -> p c n", p=128), self.stg[:, :, :], "o_ssd" + tag, ("stg",), ())

    def _sample(self):
        B = self
        kT_s = self.dscr("kT_s", [H, 128, PAST + DEC_SEQ])
        v_s = self.dscr("v_s", [H, 128, PAST // 128 + 1, 128])
        kpeT_s = self.dscr("kpeT_s", [128, PAST + DEC_SEQ])
        xmid_s = self.dscr("xmid_s", [DEC_SEQ, D], F32)
        c = dict(s=0, T=DEC_SEQ, TP=64, xsrc=self.xs, pos_tile0=64, o_ckv=self.o_ckvs, o_kpe=self.o_kpes, o_y=self.o_ys,
                 xmid=xmid_s, key0=PAST, tag="s", kT=kT_s, vS=v_s, kpS=kpeT_s, masked=False, last=False)
        B.barrier()
        hk = tuple("hst%d" % g for g in range(SG)) + ("hst",)
        B.dma("sp", self.stg[:, :, :], self.c_ssd.rearrange("(c p) n -> p c n", p=128), "stg_in", (), ("stg",))
        for pc in range(32):
            if pc % 4 == 0:
                b = B.bank()
                pk = "ps%d" % b
            B.tr(self.ps[b][:, (pc % 4) * 128:(pc % 4 + 1) * 128], self.stg[:, pc, :], self.identf, ("stg", "cst"), (pk,))
            if pc % 4 == 3:
                B.cp("act" if (pc // 4) % 2 else "dve", self.hst[:, (pc - 3) * 128:(pc + 1) * 128], self.ps[b][:, :], (pk,), hk)
        B.cp("act", self.hbf[:, :], self.hst[:, :], hk, tuple("hbf%d" % g for g in range(SG)))
        for q in range(4):
            for k in range(3):
                B.dma("sp", self.shist[:, q * 12:(q + 1) * 12, k],
                      self.c_sconv[k, q * 1536:(q + 1) * 1536].rearrange("(c p) -> p c", p=128), "shist_in", (), ("shist",),
                      allow_slow_non_contiguous=True)
        for q in range(8):
            for k in range(2):
                B.dma("sp", self.fhist[:, q * 11:(q + 1) * 11, k],
                      self.c_fconv[k, q * 1408:(q + 1) * 1408].rearrange("(c p) -> p c", p=128), "fhist_in", (), ("fhist",),
                      allow_slow_non_contiguous=True)
        for blk in range(PAST // 512):
            B.barrier()
            for tt in range(4):
                r0 = blk * 512 + tt * 128
                i = B.rot("ckvo", 2)
                co, cok = self.ckvo[i], "ckvo%d" % i
                B.dma("sp", co[:, :], self.c_ckv[r0:r0 + 128, :], "ld_" + cok, (), (cok,))
                B.cp("act" if tt % 2 else "dve", self.ckvb[:, tt, :], co[:, :], (cok,), ("ckvb",))
            B.dma("sp", self.kpeo[:, :, :], self.c_kpe[blk * 512:(blk + 1) * 512, :].rearrange("(t p) d -> p t d", p=128),
                  "ld_kpeo", (), ("kpeo",))
            B.cp("act", self.kpeb[:, :, 0:64], self.kpeo[:, :, :], ("kpeo",), ("kpeb",))
            B.cp("dve", self.kpeb[:, :, 64:128], self.kpeo[:, :, :], ("kpeo",), ("kpeb",))
            for rc in range(4):
                b = B.bank()
                pk = "ps%d" % b
                for tt in range(4):
                    B.tr(B.psb(b)[:, tt * 128:(tt + 1) * 128], self.ckvb[:, tt, rc * 128:(rc + 1) * 128],
                         self.identb[:, :], ("ckvb", "cst2"), (pk,))
                B.cp("act" if rc % 2 else "dve", self.ckvT[:, rc, :], B.psb(b)[:, 0:512], (pk,), ("ckvT",))
            b = B.bank()
            pk = "ps%d" % b
            for tt in range(4):
                B.tr(B.psb(b)[:, tt * 128:(tt + 1) * 128], self.kpeb[:, tt, :], self.identb[:, :], ("kpeb", "cst2"), (pk,))
            B.cp("dve", self.kpeT[:, :], B.psb(b)[:, 0:512], (pk,), ("kpeT",))
            B.dma("pool", kpeT_s[:, blk * 512:(blk + 1) * 512], self.kpeT[:, :], "kpSs", ("kpeT",), ("kpSs",))
            self._s4_expand(c, key0=blk * 512, T=512, TP=128)
        self._slot(c)
        if self.stage < 9:
            return
        self._dump_states(self.o_sconvs, self.o_ssds, self.o_fconvs, "s")

    def _slot(self, c):
        B = self
        B.barrier(self.KD3 + ("hdnT",), self.KA1)
        self._s1_norm(c)
        if self.stage < 2:
            return
        B.barrier(self.KA1, self.KA2 + ("qnT", "qpeT", "qpeb", "attT"))
        self._s2_kv(c)
        if c["tag"] == "p" and c["s"] == 0:
            self._emit_conversions(12, len(self._conv_items))
        if self.stage < 3:
            return
        self._s3_q(c)
        self._s4_expand(c)
        if self.stage < 5:
            return
        B.barrier(self.KA2, self.KB)
        self._s5_attn(c)
        if self.stage < 6:
            return
        B.barrier(self.KB + ("qnT", "qpeT", "qpeb"), self.KC + ("ynT",))
        self._s6_ssd(c)
        if self.stage < 7:
            return
        B.barrier(self.KC, self.KD1)
        self._s7_merge(c)
        if self.stage < 8:
            return
        B.barrier(self.KD1 + ("attT", "ynT"), self.KD2 + ("hdnT",))
        self._s8_ffn_up(c)
        B.barrier(self.KD2, self.KD3)
        self._s9_ffn_down(c)

    def _s1_norm(self, c):
        B = self
        T, TP = c["T"], c["TP"]
        NTT = T // TP
        st, xnb, uT = self.st, self.xnb, self.uT
        for tt in range(NTT):
            i = B.rot("xin", 2)
            xin, xk = self.xin[i], "xin%d" % i
            B.dma("sp", xin[0:TP, :], c["xsrc"][tt * TP:(tt + 1) * TP, :], xk, (), (xk,))
            c0 = tt
            B.act(self.junk[0:TP, :], xin[0:TP, :], AF.Square, (xk,), ("junk", "st%d" % c0), accum=st[0:TP, c0:c0 + 1])
            B.rstd(c0, D, TP)
            B.ts("dve", xnb[0:TP, tt, :], xin[0:TP, :], st[0:TP, c0:c0 + 1], None, ALU.mult, None, (xk, "st%d" % c0), ("xnb",))
        for kc in range(16):
            b = B.bank()
            pk = "ps%d" % b
            for tt in range(NTT):
                B.tr(B.psb(b)[:, tt * TP:(tt + 1) * TP], xnb[0:TP, tt, kc * 128:(kc + 1) * 128], self.identb[0:TP, 0:TP],
                     ("xnb", "cst2"), (pk,))
            if kc % 2 == 0:
                B.ts("dve", uT[:, kc, 0:T], B.psb(b)[:, 0:T], self.g_premix[:, kc:kc + 1], None, ALU.mult, None,
                     (pk, "cst"), ("uT",))
            else:
                B.act(uT[:, kc, 0:T], B.psb(b)[:, 0:T], AF.Copy, (pk, "cst"), ("uT",), scale=self.g_premix[:, kc:kc + 1])
        if "uT" in self.dbg_out and c["last"]:
            B.dma("pool", self.dbg_out["uT"], self.uT[:], "dbg_uT", ("uT",), ())

    def _s2_kv(self, c):
        B = self
        T, TP, tag = c["T"], c["TP"], c["tag"]
        NTT = T // TP
        st, uT = self.st, self.uT
        B.dma("sp", self.cs[0:TP, 0, 0:NTT, :], self.t_cos[0:TP, c["pos_tile0"]:c["pos_tile0"] + NTT, :], "cs", (), ("cs",))
        B.dma("sp", self.cs[0:TP, 1, 0:NTT, :], self.t_sin[0:TP, c["pos_tile0"]:c["pos_tile0"] + NTT, :], "cs", (), ("cs",))
        wck, kck = B.wload("ckv")
        wkp, kkp = B.wload("kpe")
        for tt in range(NTT):
            ba, bb = B.bank(), B.bank()
            pa, pb = "ps%d" % ba, "ps%d" % bb
            psa, psk = self.ps[ba], self.ps[bb]
            for kc in range(16):
                B.mm(psa[0:TP, :], uT[:, kc, tt * TP:(tt + 1) * TP], wck[:, kc, :], kc == 0, kc == 15, ("uT", kck), (pa,))
            for kc in range(16):
                B.mm(psk[0:TP, 0:ROPE], uT[:, kc, tt * TP:(tt + 1) * TP], wkp[:, kc, :], kc == 0, kc == 15, ("uT", kkp), (pb,))
            c1 = 4 + tt
            B.act(self.junk2[0:TP, 0:KVL], psa[0:TP, :], AF.Square, (pa,), ("junk2", "st%d" % c1), accum=st[0:TP, c1:c1 + 1])
            B.rstd(c1, KVL, TP)
            i = B.rot("ckvo", 2)
            co, cok = self.ckvo[i], "ckvo%d" % i
            B.stt("dve", co[0:TP, :], psa[0:TP, :], st[0:TP, c1:c1 + 1], self.g_kvn_r[0:TP, :], ALU.mult, ALU.mult,
                  (pa, "st%d" % c1, "cst"), (cok,))
            B.cp("act", self.ckvb[0:TP, tt, :], co[0:TP, :], (cok,), ("ckvb",))
            B.dma("pool", c["o_ckv"][tt * TP:(tt + 1) * TP, :], co[0:TP, :], "o_" + cok, (cok,), ())
            c_, s_ = self.cs[0:TP, 0, tt, :], self.cs[0:TP, 1, tt, :]
            x1, x2 = psk[0:TP, 0:32], psk[0:TP, 32:64]
            r = self.rtmp
            B.tt("dve", r[0:TP, 0, 0:32], x1, c_, ALU.mult, (pb, "cs"), ("rtmp",))
            B.tt("dve", r[0:TP, 1, 0:32], x2, s_, ALU.mult, (pb, "cs"), ("rtmp",))
            B.tt("dve", r[0:TP, 2, 0:32], x1, s_, ALU.mult, (pb, "cs"), ("rtmp",))
            B.tt("dve", r[0:TP, 3, 0:32], x2, c_, ALU.mult, (pb, "cs"), ("rtmp",))
            B.tt("dve", self.kpeo[0:TP, tt, 0:32], r[0:TP, 0, 0:32], r[0:TP, 1, 0:32], ALU.subtract, ("rtmp",), ("kpeo",))
            B.tt("dve", self.kpeo[0:TP, tt, 32:64], r[0:TP, 2, 0:32], r[0:TP, 3, 0:32], ALU.add, ("rtmp",), ("kpeo",))
            B.cp("act", self.kpeb[0:TP, tt, 0:64], self.kpeo[0:TP, tt, :], ("kpeo",), ("kpeb",))
            B.cp("act", self.kpeb[0:TP, tt, 64:128], self.kpeo[0:TP, tt, :], ("kpeo",), ("kpeb",))
        B.dma("pool", c["o_kpe"].rearrange("(t p) d -> p t d", p=TP), self.kpeo[0:TP, 0:NTT, :], "o_kpe" + tag, ("kpeo",), ())
        for rc in range(4):
            b = B.bank()
            pk = "ps%d" % b
            for tt in range(NTT):
                B.tr(B.psb(b)[:, tt * TP:(tt + 1) * TP], self.ckvb[0:TP, tt, rc * 128:(rc + 1) * 128],
                     self.identb[0:TP, 0:TP], ("ckvb", "cst2"), (pk,))
            B.cp("act" if rc % 2 else "dve", self.ckvT[:, rc, 0:T], B.psb(b)[:, 0:T], (pk,), ("ckvT",))
        b = B.bank()
        pk = "ps%d" % b
        for tt in range(NTT):
            B.tr(B.psb(b)[:, tt * TP:(tt + 1) * TP], self.kpeb[0:TP, tt, :], self.identb[0:TP, 0:TP], ("kpeb", "cst2"), (pk,))
        B.cp("dve", self.kpeT[:, 0:T], B.psb(b)[:, 0:T], (pk,), ("kpeT",))
        B.dma("pool", c["kpS"][:, c["key0"]:c["key0"] + T], self.kpeT[:, 0:T], "kpS" + tag, ("kpeT",), ("kpS" + tag,))
        if "ckvT" in self.dbg_out and c["last"]:
            B.dma("pool", self.dbg_out["ckvT"], self.ckvT[:], "dbg_ckvT", ("ckvT",), ())

    def _s3_q(self, c):
        B = self
        T, TP = c["T"], c["TP"]
        NTT = T // TP
        st, uT = self.st, self.uT
        wq, kq = B.wload("q")
        for tt in range(NTT):
            b = B.bank()
            pk = "ps%d" % b
            for kc in range(16):
                B.mm(self.ps[b][0:TP, :], uT[:, kc, tt * TP:(tt + 1) * TP], wq[:, kc, :], kc == 0, kc == 15, ("uT", kq), (pk,))
            c2 = 8 + tt
            B.act(self.junk2[0:TP, 0:QL], self.ps[b][0:TP, :], AF.Square, (pk,), ("junk2", "st%d" % c2), accum=st[0:TP, c2:c2 + 1])
            B.rstd(c2, QL, TP)
            B.ts("dve", self.cqnb[0:TP, tt, :], self.ps[b][0:TP, :], st[0:TP, c2:c2 + 1], None, ALU.mult, None, (pk, "st%d" % c2), ("cqnb",))
        for rc in range(4):
            b = B.bank()
            pk = "ps%d" % b
            for tt in range(NTT):
                B.tr(B.psb(b)[:, tt * TP:(tt + 1) * TP], self.cqnb[0:TP, tt, rc * 128:(rc + 1) * 128],
                     self.identb[0:TP, 0:TP], ("cqnb", "cst2"), (pk,))
            B.ts("dve", self.cqnT[:, rc, 0:T], B.psb(b)[:, 0:T], self.g_qn[:, rc:rc + 1], None, ALU.mult, None,
                 (pk, "cst"), ("cqnT",))
        wn, kn = B.wload("uqn")
        for h in range(H):
            b = B.bank()
            pk = "ps%d" % b
            for rc in range(4):
                B.mm(self.ps[b][:, 0:T], wn[:, rc, h * 128:(h + 1) * 128], self.cqnT[:, rc, 0:T], rc == 0, rc == 3,
                     ("cqnT", kn), (pk,))
            B.cp("act" if h % 2 else "dve", self.qnT[:, h, 0:T], self.ps[b][:, 0:T], (pk,), ("qnT",))
        wr, kr = B.wload("uqr")
        for tt in range(NTT):
            for cb in range(2):
                b = B.bank()
                pk = "ps%d" % b
                for rc in range(4):
                    B.mm(self.ps[b][0:TP, :], self.cqnT[:, rc, tt * TP:(tt + 1) * TP], wr[:, rc, cb * 512:(cb + 1) * 512],
                         rc == 0, rc == 3, ("cqnT", kr), (pk,))
                pv = self.ps[b][0:TP, :].rearrange("p (h d) -> p h d", h=8)
                x1, x2 = pv[:, :, 0:32], pv[:, :, 32:64]
                c_ = self.cs[0:TP, 0, tt, :].unsqueeze(1).broadcast_to([TP, 8, 32])
                s_ = self.cs[0:TP, 1, tt, :].unsqueeze(1).broadcast_to([TP, 8, 32])
                r = self.rtmp
                rv = [r[0:TP, k, :].rearrange("p (h d) -> p h d", h=8) for k in range(4)]
                B.tt("dve", rv[0], x1, c_, ALU.mult, (pk, "cs"), ("rtmp",))
                B.tt("dve", rv[1], x2, s_, ALU.mult, (pk, "cs"), ("rtmp",))
                B.tt("dve", rv[2], x1, s_, ALU.mult, (pk, "cs"), ("rtmp",))
                B.tt("dve", rv[3], x2, c_, ALU.mult, (pk, "cs"), ("rtmp",))
                qv = self.qpeb[0:TP, tt, cb * 512:(cb + 1) * 512].rearrange("p (h d) -> p h d", h=8)
                B.tt("pool", qv[:, :, 0:32], rv[0], rv[1], ALU.subtract, ("rtmp",), ("qpeb",))
                B.tt("pool", qv[:, :, 32:64], rv[2], rv[3], ALU.add, ("rtmp",), ("qpeb",))
        for hp in range(8):
            b = B.bank()
            pk = "ps%d" % b
            for tt in range(NTT):
                B.tr(B.psb(b)[:, tt * TP:(tt + 1) * TP], self.qpeb[0:TP, tt, hp * 128:(hp + 1) * 128],
                     self.identb[0:TP, 0:TP], ("qpeb", "cst2"), (pk,))
            B.cp("act" if hp % 2 else "dve", self.qpeT[:, hp, 0:T], B.psb(b)[:, 0:T], (pk,), ("qpeT",))

    def _s4_expand(self, c, key0=None, T=None, TP=None):
        B = self
        T = T or c["T"]
        TP = TP or c["TP"]
        key0 = c["key0"] if key0 is None else key0
        tag = c["tag"]
        NTT = T // TP
        wk, kk = B.wload("ukn")
        for h in range(H):
            b = B.bank()
            pk = "ps%d" % b
            for rc in range(4):
                B.mm(self.ps[b][:, 0:T], wk[:, rc, h * 128:(h + 1) * 128], self.ckvT[:, rc, 0:T], rc == 0, rc == 3,
                     ("ckvT", kk), (pk,))
            i = B.rot("kst", 4)
            B.cp("act" if h % 2 else "dve", self.kst[i][:, 0:T], self.ps[b][:, 0:T], (pk,), ("kst%d" % i,))
            B.dma("pool", c["kT"][h, :, key0:key0 + T], self.kst[i][:, 0:T], "kst%d" % i, ("kst%d" % i,), ("kT" + tag,))
        wv, kv = B.wload("ukv")
        for tt in range(NTT):
            kt = (key0 + tt * TP) // 128
            for cb in range(4):
                b = B.bank()
                pk = "ps%d" % b
                for rc in range(4):
                    B.mm(self.ps[b][0:TP, :], self.ckvT[:, rc, tt * TP:(tt + 1) * TP], wv[:, rc, cb * 512:(cb + 1) * 512],
                         rc == 0, rc == 3, ("ckvT", kv), (pk,))
                i = B.rot("vst", 4)
                B.cp("act" if cb % 2 else "dve", self.vst[i][0:TP, :], self.ps[b][0:TP, :], (pk,), ("vst%d" % i,))
                B.dma("pool", c["vS"][4 * cb:4 * cb + 4, 0:TP, kt, :].rearrange("h p d -> p h d"),
                      self.vst[i][0:TP, :].rearrange("p (h d) -> p h d", h=4), "vst%d" % i, ("vst%d" % i,), ("vS" + tag,))

    def _s5_attn(self, c):
        B = self
        T, tag = c["T"], c["tag"]
        nk = c["key0"] + T
        tiles = []
        k = 0
        while k < nk:
            n = min(128, nk - k)
            mi = (k - c["key0"]) // 128 if (c["masked"] and k >= c["key0"]) else None
            tiles.append((k, n, mi))
            k += n
        B.dma("sp", self.kpeK[:, 0:nk], c["kpS"][:, 0:nk], "kpeK", ("kpS" + tag,), ("kpeK",))
        steps = [(h, ti) for h in range(H) for ti in range(len(tiles))]
        state = {}

        def qk(h, ti):
            k0, n, mi = tiles[ti]
            if k0 % 1024 == 0:
                pi = B.rot("kvp", 4)
                npk = min(1024, nk - k0)
                B.dma("sp", self.kp_[pi][:, 0:npk], c["kT"][h, :, k0:k0 + npk], "kp%d" % pi, ("kT" + tag,), ("kp%d" % pi,))
                nkt = (npk + 127) // 128
                pv = min(128, npk)
                B.dma("sp", self.vp_[pi][0:pv, 0:nkt, :], c["vS"][h, 0:pv, k0 // 128:k0 // 128 + nkt, :], "vp%d" % pi,
                      ("vS" + tag,), ("vp%d" % pi,))
                state["pi"] = pi
            pi = state["pi"]
            hb = 64 * (h % 2)
            kl = k0 % 1024
            b = B.rot("abank", 4)
            pk = "ps%d" % b
            B.mm(self.ps[b][0:n, 0:T], self.kp_[pi][:, kl:kl + n], self.qnT[:, h, 0:T], True, False, ("kp%d" % pi, "qnT"), (pk,))
            B.mm(self.ps[b][0:n, 0:T], self.kpeK[hb:hb + 64, k0:k0 + n], self.qpeT[hb:hb + 64, h // 2, 0:T], False, True,
                 ("kpeK", "qpeT"), (pk,))
            return (b, pk, pi, kl)

        DEPTH = 2
        pend = [qk(*steps[i]) for i in range(min(DEPTH, len(steps)))]
        for si, (h, ti) in enumerate(steps):
            b, pk, pi, kl = pend.pop(0)
            if si + DEPTH < len(steps):
                pend.append(qk(*steps[si + DEPTH]))
            k0, n, mi = tiles[ti]
            bo, br = (4, 5) if h % 2 == 0 else (6, 7)
            po, pr = "ps%d" % bo, "ps%d" % br
            i = B.rot("pT", 3)
            pT, ptk = self.pT[i], "pT%d" % i
            B.act(pT[0:n, 0:T], self.ps[b][0:n, 0:T], AF.Exp, (pk,), (ptk,), scale=SCALE)
            if mi is not None:
                B.tt("pool", pT[0:n, 0:T], pT[0:n, 0:T], self.amask[0:n, mi, 0:T], ALU.mult, (ptk, "cst"), (ptk,))
            first, last = ti == 0, ti == len(tiles) - 1
            B.mm(self.ps[bo][:, 0:T], self.vp_[pi][0:n, kl // 128, :], pT[0:n, 0:T], first, last, ("vp%d" % pi, ptk), (po,))
            B.mm(self.ps[br][:, 0:T], self.onesb[0:n, :], pT[0:n, 0:T], first, last, (ptk, "cst2"), (pr,))
            if last:
                ri = B.rot("rcp", 2)
                rc_ = self.rcp[ri]
                B.recip(rc_[:, 0:T], self.ps[br][:, 0:T], (pr,), ("rcp%d" % ri,))
                B.tt("dve", self.attT[:, h, 0:T], self.ps[bo][:, 0:T], rc_[:, 0:T], ALU.mult, (po, "rcp%d" % ri), ("attT",))
        if "attT" in self.dbg_out and c["last"]:
            B.dma("pool", self.dbg_out["attT"], self.attT[:], "dbg_attT", ("attT",), ())

    def _conv(self, b, T, K, wts, bias, hist, fc, pre, prek, acc, acck, eng):
        B = self
        pk = "ps%d" % b
        H_ = K - 1
        B.cp("act", pre[:, H_:H_ + T], self.ps[b][:, 0:T], (pk,), (prek,))
        B.cp("pool", pre[:, 0:H_], hist[:, fc, :], (hist_key(hist, self),), (prek,))
        B.ts(eng, acc[:, 0:T], pre[:, H_:H_ + T], wts[:, K - 1, fc:fc + 1], bias[:, fc:fc + 1], ALU.mult, ALU.add,
             (prek, "cst"), (acck,))
        for k in range(K - 1):
            B.stt(eng, acc[:, 0:T], pre[:, k:k + T], wts[:, k, fc:fc + 1], acc[:, 0:T], ALU.mult, ALU.add,
                  (prek, acck, "cst"), (acck,))
        B.cp("pool", hist[:, fc, :], pre[:, T:T + H_], (prek,), (hist_key(hist, self),))

    def _s6_ssd(self, c):
        B = self
        T = c["T"]
        NCH = T // 64
        uT = self.uT
        dtt, atok, acum, etot, decs, dtdec = self.dtt, self.atok, self.acum, self.etot, self.decs, self.dtdec
        wdt, kdt = B.wload("dt")
        b = B.bank()
        pk = "ps%d" % b
        for ch in range(NCH):
            for kc in range(16):
                B.mm(self.ps[b][0:64, ch * 64:(ch + 1) * 64], uT[:, kc, ch * 64:(ch + 1) * 64], wdt[:, kc, :], kc == 0, kc == 15,
                     ("uT", kdt), (pk,))
        pv = self.ps[b][0:64, 0:NCH * 64].rearrange("p (c h) -> p c h", c=NCH)
        B.tt("dve", dtt[:, 0:NCH, :], pv, self.dtb_r[:, :].unsqueeze(1).broadcast_to([64, NCH, 64]), ALU.add, (pk, "cst"), ("dtt",))
        B.ts("dve", dtt[:, 0:NCH, :], dtt[:, 0:NCH, :], 30.0, None, ALU.min, None, ("dtt",), ("dtt",))
        B.act(dtt[:, 0:NCH, :], dtt[:, 0:NCH, :], AF.Exp, ("dtt",), ("dtt",))
        B.act(dtt[:, 0:NCH, :], dtt[:, 0:NCH, :], AF.Ln, ("dtt",), ("dtt",), bias=self.onesf[0:64, 0:1])
        B.tt("dve", atok[:, 0:NCH, :], dtt[:, 0:NCH, :], self.a_r[:, :].unsqueeze(1).broadcast_to([64, NCH, 64]), ALU.mult,
             ("dtt", "cst2"), ("atok",))
        b = B.bank()
        pk = "ps%d" % b
        for ch in range(NCH):
            B.mm(self.ps[b][0:64, ch * 64:(ch + 1) * 64], self.tri, atok[:, ch, :], True, True, ("atok", "cst"), (pk,))
        B.cp("dve", acum[:, 0:NCH, :], self.ps[b][0:64, 0:NCH * 64].rearrange("p (c h) -> p c h", c=NCH), (pk,), ("acum",))
        b = B.bank()
        pk = "ps%d" % b
        for ch in range(NCH):
            B.mm(self.ps[b][:, ch * 64:(ch + 1) * 64], self.onesf[0:64, :], atok[:, ch, :], True, True, ("atok", "cst2"), (pk,))
        pv = self.ps[b][:, 0:NCH * 64].rearrange("p (c h) -> p c h", c=NCH)
        B.act(etot[:, 0:NCH, :], pv, AF.Exp, (pk,), ("etot",))
        B.tt("dve", decs[:, 0:NCH, :], pv[0:64], acum[:, 0:NCH, :], ALU.subtract, (pk, "acum"), ("decs",))
        B.act(decs[:, 0:NCH, :], decs[:, 0:NCH, :], AF.Exp, ("decs",), ("decs",))
        B.tt("dve", dtdec[:, 0:NCH, :], decs[:, 0:NCH, :], dtt[:, 0:NCH, :], ALU.mult, ("decs", "dtt"), ("dtdec",))
        b = B.bank()
        pk = "ps%d" % b
        for ch in range(NCH):
            B.tr(self.ps[b][0:64, ch * 64:(ch + 1) * 64], acum[:, ch, :], self.identf[0:64, 0:64], ("acum", "cst"), (pk,))
        B.cp("act", atok[:, 0:NCH, :], self.ps[b][0:64, 0:NCH * 64].rearrange("p (c h) -> p c h", c=NCH), (pk,), ("atok",))
        B.dma("pool", self.acT_scr[0:NCH].rearrange("c h i -> h c i"), atok[:, 0:NCH, :], "acT_w", ("atok",), ("acT",))

        xk_all = tuple("xsT%d" % ch for ch in range(8))
        for g in range(SG):
            wx, kx = B.wload("xs", g)
            wb_, kb_ = B.wload("bm", g)
            wc_, kc_ = B.wload("cm", g)
            plan = [(wx, kx, j * 128, 4 * g + j, ("xs", j)) for j in range(4)]
            plan += [(wb_, kb_, 0, 32 + g, ("B", 0)), (wc_, kc_, 0, 40 + g, ("C", 0))]
            for (wt_, wk_, cl, fc, (kind, j)) in plan:
                b = B.bank()
                pk = "ps%d" % b
                for kc in range(16):
                    B.mm(self.ps[b][:, 0:T], wt_[:, kc, cl:cl + 128], uT[:, kc, 0:T], kc == 0, kc == 15, ("uT", wk_), (pk,))
                i = B.rot("pre", 2)
                eng = "dve"
                B._conv(b, T, 4, self.sconv_w, self.sconv_b, self.shist, fc, self.pre[i], "pre%d" % i, self.acc[i], "acc%d" % i, eng)
                if kind == "xs":
                    B.act(self.xsT[:, j, 0:T], self.acc[i][:, 0:T], AF.Silu, ("acc%d" % i,), xk_all)
                elif kind == "B":
                    B.act(self.BT[:, 0:T], self.acc[i][:, 0:T], AF.Silu, ("acc%d" % i,), ("BT",))
                else:
                    B.act(self.CT[:, 0:T], self.acc[i][:, 0:T], AF.Silu, ("acc%d" % i,), ("CT",))
            hs = slice(g * 512, (g + 1) * 512)
            hk, hbk = "hst%d" % g, "hbf%d" % g

            def prep(ch, g=g):
                q = ch % 2
                cs_ = slice(ch * 64, (ch + 1) * 64)
                K_ = lambda n: n + str(q)
                b = B.bank()
                pk = "ps%d" % b
                B.mm(self.ps[b][0:64, 0:64], self.BT[:, cs_], self.CT[:, cs_], True, True, ("BT", "CT"), (pk,))
                B.tt("dve", self.cbm[q][:, :], self.ps[b][0:64, 0:64], self.tri, ALU.mult, (pk, "cst"), (K_("cbm"),))
                ab, abk = self.abcs[q], K_("abcs")
                B.dma("sp", ab[:, :], self.acT_scr[ch, 8 * g:8 * g + 8, :].rearrange("h i -> (h i)").rearrange("(o n) -> o n", o=1)
                      .broadcast_to([128, 512]), abk, ("acT",), (abk,))
                B.act(self.Ebc[q][:, :], ab[:, :], AF.Exp, (abk,), (K_("Ebc"),))
                B.tt("dve", self.seg[q][:, :].rearrange("p (h i) -> p h i", h=8), ab[0:64, :].rearrange("p (h i) -> p h i", h=8),
                     acum[:, ch, 8 * g:8 * g + 8].unsqueeze(2).broadcast_to([64, 8, 64]), ALU.subtract, (abk, "acum"), (K_("seg"),))
                B.ts("dve", self.seg[q][:, :], self.seg[q][:, :], 0.0, None, ALU.min, None, (K_("seg"),), (K_("seg"),))
                B.act(self.seg[q][:, :], self.seg[q][:, :], AF.Exp, (K_("seg"),), (K_("seg"),))
                B.tt("dve", self.MT[q][:, :, :], self.seg[q][:, :].rearrange("p (h i) -> p h i", h=8),
                     self.cbm[q][:, :].unsqueeze(1).broadcast_to([64, 8, 64]), ALU.mult, (K_("seg"), K_("cbm")), (K_("MT"),))
                B.tt("pool", self.Ce[q][:, :, :], self.Ebc[q][:, :].rearrange("p (h i) -> p h i", h=8),
                     self.CT[:, cs_].unsqueeze(1).broadcast_to([128, 8, 64]), ALU.mult, (K_("Ebc"), "CT"), (K_("Ce"),))
                bt = B.bank()
                pt = "ps%d" % bt
                for j in range(4):
                    B.tr(self.ps[bt][0:64, j * 128:(j + 1) * 128], self.xsT[:, j, cs_], self.identf, ("xsT%d" % ch, "cst"), (pt,))
                pv = self.ps[bt][0:64, :].rearrange("p (h d) -> p h d", h=8)
                B.tt("dve", self.xdt[q][:, :].rearrange("p (h d) -> p h d", h=8), pv,
                     dtt[:, ch, 8 * g:8 * g + 8].unsqueeze(2).broadcast_to([64, 8, 64]), ALU.mult, (pt, "dtt"), (K_("xdt"),))
                B.tt("dve", self.xdtd[q][:, :].rearrange("p (h d) -> p h d", h=8), pv,
                     dtdec[:, ch, 8 * g:8 * g + 8].unsqueeze(2).broadcast_to([64, 8, 64]), ALU.mult, (pt, "dtdec"), (K_("xdtd"),))
                b2 = B.bank()
                p2 = "ps%d" % b2
                B.tr(B.psb(b2)[0:64, 0:128], self.BT[:, cs_], self.identb[:, :], ("BT", "cst2"), (p2,))
                B.cp("act", self.Btok[q][:, :], B.psb(b2)[0:64, 0:128], (p2,), (K_("Btok"),))

            def yst(ch, g=g):
                q = ch % 2
                cs_ = slice(ch * 64, (ch + 1) * 64)
                K_ = lambda n: n + str(q)
                by = B.bank()
                py = "ps%d" % by
                for hh in range(8):
                    h = 8 * g + hh
                    o_ = self.ps[by][64 * (hh % 2):64 * (hh % 2) + 64, (hh // 2) * 64:(hh // 2 + 1) * 64]
                    B.mm(o_, self.xdt[q][:, hh * 64:(hh + 1) * 64], self.MT[q][:, hh, :], True, False, (K_("xdt"), K_("MT")), (py,))
                    B.mm(o_, self.hbf[:, h * 64:(h + 1) * 64], self.Ce[q][:, hh, :], False, True, (hbk, K_("Ce")), (py,))
                B.tt("pool", self.xsT[:, :, cs_], self.xsT[:, :, cs_], self.dcol[:, 4 * g:4 * g + 4].unsqueeze(2).broadcast_to([128, 4, 64]),
                     ALU.mult, ("xsT%d" % ch, "cst"), ("xsT%d" % ch,))
                B.tt("dve", self.xsT[:, :, cs_], self.xsT[:, :, cs_], self.ps[by][:, 0:256].rearrange("p (j i) -> p j i", j=4),
                     ALU.add, ("xsT%d" % ch, py), ("xsT%d" % ch,))
                bs_ = B.bank()
                ps_ = "ps%d" % bs_
                B.mm(self.ps[bs_][:, :], self.Btok[q][:, :], self.xdtd[q][:, :], True, True, (K_("Btok"), K_("xdtd")), (ps_,))
                B.tt("pool", self.htmp[:, :].rearrange("p (h d) -> p h d", h=8), self.hst[:, hs].rearrange("p (h d) -> p h d", h=8),
                     etot[:, ch, 8 * g:8 * g + 8].unsqueeze(2).broadcast_to([128, 8, 64]), ALU.mult, (hk, "etot"), ("htmp",))
                B.tt("dve", self.hst[:, hs], self.htmp[:, :], self.ps[bs_][:, :], ALU.add, ("htmp", ps_), (hk,))
                B.cp("act", self.hbf[:, hs], self.hst[:, hs], (hk,), (hbk,))

            prep(0)
            for ch in range(NCH):
                if ch + 1 < NCH:
                    prep(ch + 1)
                yst(ch)
            if "yscan" in self.dbg_out and c["last"]:
                B.dma("pool", self.dbg_out["yscan"][:, 4 * g:4 * g + 4, :], self.xsT[:, :, :], "dbg_yscan", xk_all, ())
            wz, kz = B.wload("z", g)
            bn = B.bank()
            pn = "ps%d" % bn
            for j in range(4):
                b = B.bank()
                pk = "ps%d" % b
                for kc in range(16):
                    B.mm(self.ps[b][:, 0:T], wz[:, kc, j * 128:(j + 1) * 128], uT[:, kc, 0:T], kc == 0, kc == 15, ("uT", kz), (pk,))
                i = B.rot("zs", 2)
                B.act(self.zs[i][:, 0:T], self.ps[b][:, 0:T], AF.Silu, (pk,), ("zs%d" % i,))
                B.tt("dve", self.xsT[:, j, 0:T], self.xsT[:, j, 0:T], self.zs[i][:, 0:T], ALU.mult, xk_all + ("zs%d" % i,), xk_all)
                B.act(self.sq[i][:, 0:T], self.xsT[:, j, 0:T], AF.Square, xk_all, ("sq%d" % i,))
                B.mm(self.ps[bn][:, 0:T], self.onesf[:, :], self.sq[i][:, 0:T], j == 0, j == 3, ("sq%d" % i, "cst2"), (pn,))
            B.act(self.rt[:, 0:T], self.ps[bn][:, 0:T], AF.Sqrt, (pn,), ("rt",), bias=self.epsc[:, 0:1], scale=1.0 / 512)
            B.recip(self.rt[:, 0:T], self.rt[:, 0:T], ("rt",), ("rt",))
            for j in range(4):
                pc = 4 * g + j
                B.stt("dve", self.ynT[:, pc, 0:T], self.xsT[:, j, 0:T], self.g_ssdn[:, pc:pc + 1], self.rt[:, 0:T],
                      ALU.mult, ALU.mult, xk_all + ("rt", "cst"), ("ynT",))
        if "ynT" in self.dbg_out and c["last"]:
            B.dma("pool", self.dbg_out["ynT"], self.ynT[:], "dbg_ynT", ("ynT",), ())

    def _s7_merge(self, c):
        B = self
        T, TP = c["T"], c["TP"]
        NTT = T // TP
        uT, st = self.uT, self.st
        t1buf = self.mixsb[:, :].rearrange("p (j t) -> p j t", j=4)
        sga = self.xr[:, :].rearrange("p (j t) -> p j t", j=4)
        sgb = self.grow[:, :].rearrange("p (j t) -> p j t", j=4)
        for cb in range(4):
            wga, kga = B.wload("gate", cb)
            for j in range(4):
                cl = j * 128
                b3 = B.bank()
                p3 = "ps%d" % b3
                for kc in range(16):
                    B.mm(self.ps[b3][:, 0:T], wga[:, kc, cl:cl + 128], uT[:, kc, 0:T], kc == 0, kc == 15, ("uT", kga), (p3,))
                B.act(sga[:, j, 0:T], self.ps[b3][:, 0:T], AF.Sigmoid, (p3,), ("xr",))
            wm, km = B.wload("omla", cb)
            for j in range(4):
                cl = j * 128
                b1 = B.bank()
                p1 = "ps%d" % b1
                for kc in range(16):
                    B.mm(self.ps[b1][:, 0:T], wm[:, kc, cl:cl + 128], self.attT[:, kc, 0:T], kc == 0, kc == 15, ("attT", km), (p1,))
                B.tt("dve", t1buf[:, j, 0:T], self.ps[b1][:, 0:T], sga[:, j, 0:T], ALU.mult, (p1, "xr"), ("mixsb",))
            wgb, kgb = B.wload("gate", 4 + cb)
            for j in range(4):
                cl = j * 128
                b4 = B.bank()
                p4 = "ps%d" % b4
                for kc in range(16):
                    B.mm(self.ps[b4][:, 0:T], wgb[:, kc, cl:cl + 128], uT[:, kc, 0:T], kc == 0, kc == 15, ("uT", kgb), (p4,))
                B.act(sgb[:, j, 0:T], self.ps[b4][:, 0:T], AF.Sigmoid, (p4,), ("grow",))
            banks = [B.bank() for _ in range(4)]
            for kp in range(2):
                ws, ks = B.wload("ossd", cb, kp)
                for j in range(4):
                    cl = j * 128
                    b2 = banks[j]
                    for kc in range(16):
                        B.mm(self.ps[b2][:, 0:T], ws[:, kc, cl:cl + 128], self.ynT[:, kp * 16 + kc, 0:T],
                             kp == 0 and kc == 0, kp == 1 and kc == 15, ("ynT", ks), ("ps%d" % b2,))
            for j in range(4):
                dc = 4 * cb + j
                b2 = banks[j]
                i = B.rot("t12", 2)
                B.tt("dve", self.t12[i][:, 0:T], self.ps[b2][:, 0:T], sgb[:, j, 0:T], ALU.mult, ("ps%d" % b2, "grow"), ("t12%d" % i,))
                B.tt("pool", self.smT[:, dc, 0:T], t1buf[:, j, 0:T], self.t12[i][:, 0:T], ALU.add, ("mixsb", "t12%d" % i), ("smT",))
        if "smT" in self.dbg_out and c["last"]:
            B.dma("pool", self.dbg_out["smT"], self.smT[:], "dbg_smT", ("smT",), ())
        B.dma("sp", self.grow[:, :], self.w["post_mix_g"].rearrange("(o n) -> o n", o=1).broadcast_to([128, D]), "grow", (), ("grow",))
        for tt in range(NTT):
            tsl = slice(tt * TP, (tt + 1) * TP)
            B.dma("sp", self.xr[0:TP, :], c["xsrc"][tsl, :], "xr", (), ("xr",))
            for cb in range(4):
                wo, ko = B.wload("out", cb)
                b = B.bank()
                pk = "ps%d" % b
                for kc in range(16):
                    B.mm(self.ps[b][0:TP, :], self.smT[:, kc, tsl], wo[:, kc, :], kc == 0, kc == 15, ("smT", ko), (pk,))
                B.cp("act", self.mixsb[0:TP, cb * 512:(cb + 1) * 512], self.ps[b][0:TP, :], (pk,), ("mixsb",))
            c3, c4 = 12 + tt, 16 + tt
            B.act(self.xnb1[0:TP, :], self.mixsb[0:TP, :], AF.Square, ("mixsb",), ("sgk", "st%d" % c3), accum=st[0:TP, c3:c3 + 1])
            B.rstd(c3, D, TP)
            B.stt("dve", self.mixsb[0:TP, :], self.mixsb[0:TP, :], st[0:TP, c3:c3 + 1], self.grow[0:TP, :], ALU.mult, ALU.mult,
                  ("mixsb", "st%d" % c3, "grow"), ("mixsb",))
            B.tt("dve", self.xr[0:TP, :], self.xr[0:TP, :], self.mixsb[0:TP, :], ALU.add, ("xr", "mixsb"), ("xr",))
            B.dma("pool", c["xmid"][tsl, :], self.xr[0:TP, :], "xmid_w", ("xr",), ("xmid" + c["tag"],))
            B.act(self.xnb1[0:TP, :], self.xr[0:TP, :], AF.Square, ("xr",), ("sgk", "st%d" % c4), accum=st[0:TP, c4:c4 + 1])
            B.rstd(c4, D, TP)
            B.ts("dve", self.xnb1[0:TP, :], self.xr[0:TP, :], st[0:TP, c4:c4 + 1], None, ALU.mult, None, ("xr", "st%d" % c4), ("sgk",))
            for half in range(2):
                b = B.bank()
                pk = "ps%d" % b
                for k8 in range(8):
                    kc = half * 8 + k8
                    B.tr(B.psb(b)[:, k8 * TP:(k8 + 1) * TP], self.xnb1[0:TP, kc * 128:(kc + 1) * 128], self.identb[0:TP, 0:TP],
                         ("sgk", "cst2"), (pk,))
                B.tt("dve", uT[:, half * 8:half * 8 + 8, tsl], B.psb(b)[:, 0:8 * TP].rearrange("p (k t) -> p k t", k=8),
                     self.g_preffn[:, half * 8:half * 8 + 8].unsqueeze(2).broadcast_to([128, 8, TP]), ALU.mult, (pk, "cst"), ("uT",))
        if "xnT" in self.dbg_out and c["last"]:
            B.dma("pool", self.dbg_out["xnT"], self.uT[:], "dbg_xnT", ("uT",), ())

    def _s8_ffn_up(self, c):
        B = self
        T = c["T"]
        uT = self.uT
        for cb1 in range(11):
            for half in range(2):
                wt_, wk_ = B.wload("up", cb1 + 11 * half)
                for j in range(4):
                    fc = 4 * cb1 + j
                    cl = j * 128
                    b = B.bank()
                    pk = "ps%d" % b
                    for kc in range(16):
                        B.mm(self.ps[b][:, 0:T], wt_[:, kc, cl:cl + 128], uT[:, kc, 0:T], kc == 0, kc == 15, ("uT", wk_), (pk,))
                    i = B.rot("fpre", 4)
                    B._conv(b, T, 3, self.fconv_w, self.fconv_b, self.fhist, fc + 44 * half, self.fpre[i], "fpre%d" % i,
                            self.facc[i], "facc%d" % i, "dve")
                    if half == 0:
                        B.act(self.asil[j][:, 0:T], self.facc[i][:, 0:T], AF.Silu, ("facc%d" % i,), ("asil%d" % j,))
                    else:
                        B.tt("pool", self.hdnT[:, fc, 0:T], self.asil[j][:, 0:T], self.facc[i][:, 0:T], ALU.mult,
                             ("asil%d" % j, "facc%d" % i), ("hdnT",))
        if "hdnT" in self.dbg_out and c["last"]:
            B.dma("pool", self.dbg_out["hdnT"], self.hdnT[:], "dbg_hdnT", ("hdnT",), ())

    def _s9_ffn_down(self, c):
        B = self
        T, TP, tag = c["T"], c["TP"], c["tag"]
        NTT = T // TP
        st = self.st
        B.dma("sp", self.grow2[:, :], self.w["post_ffn_g"].rearrange("(o n) -> o n", o=1).broadcast_to([128, D]), "grow2", (), ("grow2",))
        for cb in range(4):
            banks = [B.bank() for _ in range(NTT)]
            for kp in range(4):
                wd, kd = B.wload("down", cb, kp)
                for tt in range(NTT):
                    b = banks[tt]
                    for kcl in range(11):
                        fc = kp * 11 + kcl
                        B.mm(self.ps[b][0:TP, :], self.hdnT[:, fc, tt * TP:(tt + 1) * TP], wd[:, kcl, :], fc == 0, fc == 43,
                             ("hdnT", kd), ("ps%d" % b,))
            for tt in range(NTT):
                b = banks[tt]
                B.cp("act" if tt % 2 else "dve", self.dnsb[0:TP, tt, cb * 512:(cb + 1) * 512], self.ps[b][0:TP, :], ("ps%d" % b,), ("dnsb",))
        for tt in range(NTT):
            tsl = slice(tt * TP, (tt + 1) * TP)
            B.dma("sp", self.xr2[0:TP, :], c["xmid"][tsl, :], "xr2", ("xmid" + tag,), ("xr2",))
            c5 = 20 + tt
            B.act(self.hdnT[0:TP, 0:4, :].rearrange("p a b -> p (a b)"), self.dnsb[0:TP, tt, :], AF.Square, ("dnsb",), ("hdnT", "st%d" % c5),
                  accum=st[0:TP, c5:c5 + 1])
            B.rstd(c5, D, TP)
            B.stt("dve", self.dnsb[0:TP, tt, :], self.dnsb[0:TP, tt, :], st[0:TP, c5:c5 + 1], self.grow2[0:TP, :], ALU.mult, ALU.mult,
                  ("dnsb", "st%d" % c5, "grow2"), ("dnsb",))
            B.tt("dve", self.xr2[0:TP, :], self.xr2[0:TP, :], self.dnsb[0:TP, tt, :], ALU.add, ("xr2", "dnsb"), ("xr2",))
            B.dma("pool", c["o_y"][tsl, :], self.xr2[0:TP, :], "o_y" + tag, ("xr2",), ())


def hist_key(hist, B):
    return "shist" if hist is B.shist else "fhist"


def build_nc(**kw):
    return Builder(**kw).build()


def _tables():
    half = 32
    inv = (np.float32(10000.0) ** (-np.arange(half, dtype=np.float32) / np.float32(half))).astype(np.float32)
    pos = np.concatenate([np.arange(SEQ), PAST + np.arange(DEC_SEQ), np.zeros(64)]).astype(np.float32)
    ang = pos[:, None] * inv[None, :]
    cos = np.cos(ang).astype(np.float32).reshape(65, 128, 32).transpose(1, 0, 2)
    sin = np.sin(ang).astype(np.float32).reshape(65, 128, 32).transpose(1, 0, 2)
    k = np.arange(512)[:, None] // 64
    q = np.arange(512)[None, :] // 64
    am = (k <= q).astype(np.float32).reshape(4, 128, 512).transpose(1, 0, 2)
    misc = np.zeros((128, 1024), np.float32)
    misc[:, 0:128] = np.eye(128, dtype=np.float32)
    j = np.arange(64)[:, None]
    i = np.arange(64)[None, :]
    misc[0:64, 128:192] = (j <= i)
    misc[0:64, 192:704] = np.tile((j <= i).astype(np.float32), (1, 8))
    misc[0:64, 704:768] = np.where(j > i, NEG, 0.0)
    return dict(t_cos=np.ascontiguousarray(cos), t_sin=np.ascontiguousarray(sin),
                t_amask=np.ascontiguousarray(am), t_misc=misc)


def make_in_maps(inputs):
    tabs = _tables()
    f = lambda a: np.ascontiguousarray(np.asarray(a, dtype=np.float32))
    wnames = ["pre_mix_g", "w_in", "q_norm_g", "w_uq", "kv_norm_g", "w_ukv", "ssd_conv_w", "ssd_conv_b", "ssd_dt_bias",
              "ssd_A_log", "ssd_D", "ssd_norm_g", "w_o_mla", "w_o_ssd", "w_out", "post_mix_g", "pre_ffn_g", "w_up",
              "ffn_conv_w", "ffn_conv_b", "w_down", "post_ffn_g"]
    shared = {n: f(inputs[n][0]) for n in wnames}
    shared.update(tabs)
    maps = []
    zero_prompt = np.zeros((SEQ, D), np.float32)
    for c in range(8):
        m = dict(shared)
        m["xp"] = f(inputs["x_prompt"][PROMPT_CORES.index(c)]) if c in PROMPT_CORES else zero_prompt
        m["xs"] = f(inputs["x_sample"][c])
        m["c_ckv"] = f(inputs["cache_mla_ckv"][0, c])
        m["c_kpe"] = f(inputs["cache_mla_kpe"][0, c])
        m["c_sconv"] = f(inputs["state_ssd_conv"][0, c])
        m["c_ssd"] = f(inputs["state_ssd"][0, c]).reshape(SH * SP_, SN)
        m["c_fconv"] = f(inputs["state_ffn_conv"][0, c])
        maps.append(m)
    return maps


def kernel(**inputs):
    nc = build_nc()
    res = run_bass_kernel_spmd(nc, make_in_maps(inputs), core_ids=list(range(8)))
    r = res.results
    st = lambda name, cores: np.stack([r[c][name] for c in cores])[None]
    P4, S8 = PROMPT_CORES, range(8)
    return (st("o_yp", P4)[0], st("o_ys", S8)[0],
            st("o_ckvp", P4), st("o_kpep", P4), st("o_sconvp", P4),
            st("o_ssdp", P4).reshape(1, 4, SH, SP_, SN), st("o_fconvp", P4),
            st("o_ckvs", S8), st("o_kpes", S8), st("o_sconvs", S8),
            st("o_ssds", S8).reshape(1, 8, SH, SP_, SN), st("o_fconvs", S8))
```

```python
from contextlib import ExitStack

import numpy as np
import concourse.bass as bass
import concourse.mybir as mybir
from concourse.bass_utils import run_bass_kernel_spmd

F32 = mybir.dt.float32
BF16 = mybir.dt.bfloat16
AF = mybir.ActivationFunctionType
ALU = mybir.AluOpType

D = 2048
SEQ = 8192
DEC_SEQ = 64
PAST = 2048
H = 16
QL = 512
KVL = 512
ROPE = 64
SI = 4096
SH = 64
SP_ = 64
SG = 8
SN = 128
CONV_DIM = 6144
DFF = 5632
EPS = 1e-6
OFF_Q, OFF_KV, OFF_Z, OFF_XBC, OFF_DT, OFF_GATE = 0, 512, 1088, 5184, 11328, 11392
IN_DIM = 15488
SCALE = 192 ** -0.5
NEG = -30000.0
PROMPT_CORES = (0, 1, 4, 5)


class Op:
    __slots__ = ("eng", "fn", "deps", "sem", "val", "needed", "idx")


class Prog:
    ENGS = ("pe", "act", "dve", "pool", "sp")

    def __init__(self, nc):
        self.nc = nc
        self.ops = {e: [] for e in self.ENGS}
        self.res = {}
        self.dma_cnt = {}

    def add(self, eng, fn, reads=(), writes=(), dma=None):
        op = Op()
        op.eng, op.fn, op.sem, op.val, op.needed, op.idx = eng, fn, None, 0, False, 0
        if dma is not None:
            self.dma_cnt[dma] = self.dma_cnt.get(dma, 0) + 1
            op.sem, op.val = dma, 16 * self.dma_cnt[dma]
        stream = ("dma:" + dma) if dma is not None else eng
        deps = []
        for k in reads:
            st = self.res.get(k)
            if st is not None and st[0] is not None:
                deps.append(st[0])
            if st is not None and k.startswith("ps"):
                deps.extend(v for s_, v in st[1].items() if s_ != stream)
        for k in writes:
            st = self.res.get(k)
            if st is not None:
                same = lambda d: dma is None and d.sem is None and d.eng == eng
                if st[0] is not None and not same(st[0]):
                    deps.append(st[0])
                deps.extend(v for v in st[1].values() if not same(v))
        out = []
        for d in deps:
            if d is op or d in out:
                continue
            if d.sem is None and d.eng == eng == "pe":
                continue
            d.needed = True
            out.append(d)
        op.deps = out
        for k in reads:
            st = self.res.get(k)
            if st is None:
                st = self.res[k] = [None, {}]
            st[1][stream] = op
        for k in writes:
            self.res[k] = [op, {}]
        self.ops[eng].append(op)
        return op

    def emit(self, es):
        nc = self.nc
        esem = {e: es.enter_context(nc.semaphore("sem_" + e)) for e in self.ENGS}
        dsem = {k: es.enter_context(nc.semaphore("dsem_" + k)) for k in self.dma_cnt}
        for e in self.ENGS:
            c = 0
            for op in self.ops[e]:
                if op.sem is None and op.needed:
                    c += 1
                    op.idx = c
        final = [(dsem[k], 16 * n) for k, n in self.dma_cnt.items()]

        def run(e, eng):
            waited = {}
            for op in self.ops[e]:
                need = {}
                for d in op.deps:
                    if d.sem is not None:
                        s, v, key = dsem[d.sem], d.val, "d" + d.sem
                    else:
                        s, v, key = esem[d.eng], d.idx, d.eng
                    if key not in need or need[key][1] < v:
                        need[key] = (s, v)
                for key, (s, v) in need.items():
                    if waited.get(key, 0) < v:
                        eng.wait_ge(s, v)
                        waited[key] = v
                ins = op.fn(eng)
                if op.sem is not None:
                    ins.then_inc(dsem[op.sem], 16)
                elif op.needed:
                    ins.then_inc(esem[e], 1)
            if e == "sp":
                for s, v in final:
                    eng.wait_ge(s, v)

        block = es.enter_context(nc.Block())

        @block.tensor
        def _(eng):
            run("pe", eng)

        @block.scalar
        def _(eng):
            run("act", eng)

        @block.vector
        def _(eng):
            run("dve", eng)

        @block.gpsimd
        def _(eng):
            run("pool", eng)

        @block.sync
        def _(eng):
            run("sp", eng)


class Arena:
    def __init__(self, B, name, nbytes):
        self.t = B.sb(name, [128, nbytes // 4], F32)
        self.nbytes = nbytes
        self.off = 0

    def reset(self, off=0):
        self.off = off

    def take(self, shape, dt):
        n = 1
        for s in shape[1:]:
            n *= s
        nb = n * (4 if dt == F32 else 2)
        nb = (nb + 31) // 32 * 32
        assert self.off + nb <= self.nbytes, ("arena overflow", shape, self.off, nb, self.nbytes)
        o4 = self.off // 4
        self.off += nb
        ap = self.t[0:shape[0], o4:o4 + nb // 4]
        if dt == BF16:
            ap = ap.bitcast(BF16)
        ap = ap[:, 0:n]
        if len(shape) == 3:
            ap = ap.rearrange("p (a b) -> p a b", a=shape[1])
        elif len(shape) == 4:
            ap = ap.rearrange("p (a b c) -> p a b c", a=shape[1], b=shape[2])
        return ap


class Builder:
    def __init__(self, nslot=16, do_sample=True, stage=99, dbg=()):
        self.nslot = nslot
        self.do_sample = do_sample
        self.stage = stage
        self.dbg = dbg
        self.nc = bass.Bass("TRN2", target_bir_lowering=False)
        self.P = Prog(self.nc)
        self.es = ExitStack()
        self.psrr = 0
        self.wrr = 0
        self.rr = {}

    def din(self, name, shape, dt=F32):
        return self.nc.dram_tensor(name, list(shape), dt, kind="ExternalInput").ap()

    def dout(self, name, shape, dt=F32):
        return self.nc.dram_tensor(name, list(shape), dt, kind="ExternalOutput").ap()

    def dscr(self, name, shape, dt=BF16):
        return self.nc.dram_tensor(name, list(shape), dt, kind="Internal").ap()

    def sb(self, name, shape, dt=F32):
        return self.es.enter_context(self.nc.sbuf_tensor(name, list(shape), dt))

    def mm(self, out, lhsT, rhs, start, stop, r, w):
        self.P.add("pe", lambda e: e.matmul(out, lhsT=lhsT, rhs=rhs, start=start, stop=stop), r, w)

    def tr(self, out, in_, ident, r, w):
        self.P.add("pe", lambda e: e.transpose(out, in_, ident), r, w)

    def act(self, out, in_, func, r, w, bias=None, scale=None, accum=None):
        kw = {}
        if bias is not None:
            kw["bias"] = bias
        if scale is not None:
            kw["scale"] = scale
        if accum is not None:
            kw["accum_out"] = accum
        self.P.add("act", lambda e: e.activation(out, in_, func, **kw), r, w)

    def tt(self, eng, out, in0, in1, op, r, w):
        self.P.add(eng, lambda e: e.tensor_tensor(out, in0, in1, op), r, w)

    def ts(self, eng, out, in0, s1, s2, op0, op1, r, w):
        if op1 is None:
            self.P.add(eng, lambda e: e.tensor_scalar(out, in0, s1, None, op0), r, w)
        else:
            self.P.add(eng, lambda e: e.tensor_scalar(out, in0, s1, s2, op0, op1), r, w)

    def stt(self, eng, out, in0, scalar, in1, op0, op1, r, w):
        self.P.add(eng, lambda e: e.scalar_tensor_tensor(out, in0, scalar, in1, op0, op1), r, w)

    def cp(self, eng, out, in_, r, w):
        if eng == "act":
            self.P.add("act", lambda e: e.copy(out, in_), r, w)
        else:
            self.P.add(eng, lambda e: e.tensor_copy(out, in_), r, w)

    def recip(self, out, in_, r, w):
        self.P.add("dve", lambda e: e.reciprocal(out, in_), r, w)

    def memset(self, eng, ap, val, w):
        self.P.add(eng, lambda e: e.memset(ap, val), (), w)

    def dma(self, q, out, in_, sem, r, w, **kw):
        self.P.add(q, lambda e: e.dma_start(out=out, in_=in_, **kw), r, w, dma=sem)

    def barrier(self, rkeys=(), wkeys=()):
        t = self.bar_t
        self.P.add("pool", lambda e: e.memset(t[:, 0:1], 0.0), (), tuple(self.KALL) + tuple(rkeys) + tuple(wkeys) + ("bar_t",))

    def bank(self):
        i = self.psrr
        self.psrr = (self.psrr + 1) % 8
        return i

    def rot(self, name, n):
        i = self.rr.get(name, 0)
        self.rr[name] = (i + 1) % n
        return i

    def psb(self, i):
        return self.ps[i][:].bitcast(BF16)

    def build(self):
        with self.es:
            self._declare()
            self._consts()
            self._layout()
            self._convert_weights()
            self._prompt()
            if self.do_sample:
                self._sample()
            self.P.emit(self.es)
        return self.nc

    def _declare(self):
        B = self
        self.xp = B.din("xp", [SEQ, D])
        self.xs = B.din("xs", [DEC_SEQ, D])
        self.c_ckv = B.din("c_ckv", [PAST, KVL])
        self.c_kpe = B.din("c_kpe", [PAST, ROPE])
        self.c_sconv = B.din("c_sconv", [3, CONV_DIM])
        self.c_ssd = B.din("c_ssd", [SH * SP_, SN])
        self.c_fconv = B.din("c_fconv", [2, 2 * DFF])
        self.w = {}
        for n, shp in [("pre_mix_g", [D]), ("w_in", [D, IN_DIM]), ("q_norm_g", [QL]), ("w_uq", [QL, H, 192]),
                       ("kv_norm_g", [KVL]), ("w_ukv", [KVL, H, 256]), ("ssd_conv_w", [4, CONV_DIM]),
                       ("ssd_conv_b", [CONV_DIM]), ("ssd_dt_bias", [SH]), ("ssd_A_log", [SH]), ("ssd_D", [SH]),
                       ("ssd_norm_g", [SI]), ("w_o_mla", [D, D]), ("w_o_ssd", [SI, D]), ("w_out", [D, D]),
                       ("post_mix_g", [D]), ("pre_ffn_g", [D]), ("w_up", [D, 2 * DFF]), ("ffn_conv_w", [3, 2 * DFF]),
                       ("ffn_conv_b", [2 * DFF]), ("w_down", [DFF, D]), ("post_ffn_g", [D])]:
            self.w[n] = B.din(n, shp)
        self.t_cos = B.din("t_cos", [128, 65, 32])
        self.t_sin = B.din("t_sin", [128, 65, 32])
        self.t_amask = B.din("t_amask", [128, 4, 512])
        self.t_misc = B.din("t_misc", [128, 1024])
        self.o_yp = B.dout("o_yp", [SEQ, D])
        self.o_ckvp = B.dout("o_ckvp", [SEQ, KVL])
        self.o_kpep = B.dout("o_kpep", [SEQ, ROPE])
        self.o_sconvp = B.dout("o_sconvp", [3, CONV_DIM])
        self.o_ssdp = B.dout("o_ssdp", [SH * SP_, SN])
        self.o_fconvp = B.dout("o_fconvp", [2, 2 * DFF])
        self.o_ys = B.dout("o_ys", [DEC_SEQ, D])
        self.o_ckvs = B.dout("o_ckvs", [DEC_SEQ, KVL])
        self.o_kpes = B.dout("o_kpes", [DEC_SEQ, ROPE])
        self.o_sconvs = B.dout("o_sconvs", [3, CONV_DIM])
        self.o_ssds = B.dout("o_ssds", [SH * SP_, SN])
        self.o_fconvs = B.dout("o_fconvs", [2, 2 * DFF])
        self.dbg_out = {}
        for name, shp in self.dbg:
            self.dbg_out[name] = B.dout("dbg_" + name, shp)
        self.ps = [self.es.enter_context(self.nc.psum_tensor("ps%d" % i, [128, 512], F32)) for i in range(8)]

    def _wdecl(self, name, src2d, K, N, nb, kpart=None):
        kc = K // 128
        kpart = kpart or kc
        scr = self.dscr("wb_" + name, [N // nb, kc // kpart, 128, kpart, nb])
        self.wt[name] = (scr, src2d, kc, kpart, nb, N // nb)

    def _convert_weights(self):
        B = self
        w = self.w
        self.wt = {}
        self.convkeys = {}
        win = w["w_in"]
        small = {}
        for name, src, lo, hi in (("uqn", w["w_uq"], 0, 128), ("uqr", w["w_uq"], 128, 192),
                                  ("ukn", w["w_ukv"], 0, 128), ("ukv", w["w_ukv"], 128, 256)):
            small[name] = (src, lo, hi)
        order = [("ckv", win[:, OFF_KV:OFF_KV + 512], D, 512, 512, None), ("kpe", win[:, OFF_KV + 512:OFF_Z], D, 64, 64, None),
                 ("q", win[:, OFF_Q:OFF_Q + 512], D, 512, 512, None), "uqn", "uqr", "ukn", "ukv",
                 ("dt", win[:, OFF_DT:OFF_GATE], D, 64, 64, None),
                 ("xs", win[:, OFF_XBC:OFF_XBC + SI], D, SI, 512, None),
                 ("bm", win[:, OFF_XBC + SI:OFF_XBC + SI + 1024], D, 1024, 128, None),
                 ("cm", win[:, OFF_XBC + SI + 1024:OFF_DT], D, 1024, 128, None),
                 ("z", win[:, OFF_Z:OFF_XBC], D, SI, 512, None),
                 ("gate", win[:, OFF_GATE:IN_DIM], D, 2 * D, 512, None),
                 ("omla", w["w_o_mla"], D, D, 512, None), ("ossd", w["w_o_ssd"], SI, D, 512, 16),
                 ("out", w["w_out"], D, D, 512, None), ("up", w["w_up"], D, 2 * DFF, 512, None),
                 ("down", w["w_down"], DFF, D, 512, 11)]
        for item in order:
            if not isinstance(item, str):
                name, src, K, N, nb, kpart = item
                B._wdecl(name, src, K, N, nb, kpart)
        self._conv_items = order
        self._conv_small = small
        self._emit_conversions(0, 12)

    def _emit_conversions(self, lo, hi):
        B = self
        small = self._conv_small
        for item in self._conv_items[lo:hi]:
            if isinstance(item, str):
                name = item
                src, lo, hi = small[name]
                wd = hi - lo
                scr = self.dscr("wb_" + name, [1, 1, 128, 4, H * wd])
                self.wt[name] = (scr, None, 4, 4, H * wd, 1)
                self.convkeys[name] = []
                for rc in range(4):
                    ck = "wc_%s_%d" % (name, rc)
                    self.convkeys[name].append(ck)
                    B.dma("pool", scr[0, 0, :, rc, :].rearrange("p (h d) -> p h d", h=H),
                          src[rc * 128:(rc + 1) * 128, :, lo:hi], "wc_" + name, (), (ck,))
                continue
            name = item[0]
            scr, src, kc, kpart, nb, ncb = self.wt[name]
            self.convkeys[name] = []
            for cb in range(ncb):
                for kp in range(kc // kpart):
                    s_ = src[kp * kpart * 128:(kp + 1) * kpart * 128, cb * nb:(cb + 1) * nb]
                    s_ = s_.rearrange("(kc p) n -> p kc n", p=128)
                    ck = "wc_%s_%d_%d" % (name, cb, kp)
                    self.convkeys[name].append(ck)
                    B.dma("pool", scr[cb, kp], s_, "wc_" + name, (), (ck,))

    def wload(self, name, cb=0, kp=0):
        scr, _, kc, kpart, nb, ncb = self.wt[name]
        i = self.wrr
        self.wrr = (self.wrr + 1) % len(self.wbuf)
        buf = self.wbuf[i]
        key = "wbuf%d" % i
        ap = buf[:, 0:kpart * nb].rearrange("p (k n) -> p k n", k=kpart)
        assert self.convkeys.get(name), ("weight used before its conversion was issued", name)
        self.dma("sp", ap, scr[cb, kp], key, tuple(self.convkeys[name]), (key,))
        return ap, key

    def _consts(self):
        B = self
        w = self.w
        sb = B.sb
        self.bar_t = sb("bar_t", [128, 8], F32)
        self.wbuf = [sb("wbuf%d" % i, [128, 8192], BF16) for i in range(3)]
        self.misc = sb("misc", [128, 1024], F32)
        B.dma("sp", self.misc[:], self.t_misc[:, :], "cst", (), ("cst",))
        self.identf = self.misc[:, 0:128]
        self.tri = self.misc[0:64, 128:192]
        self.amask = sb("amask", [128, 4, 512], BF16)
        B.dma("pool", self.amask[:], self.t_amask[:, :, :], "cstp", (), ("cst",))
        self.identb = sb("identb", [128, 128], BF16)
        B.cp("dve", self.identb[:], self.identf, ("cst",), ("cst2",))
        self.onesb = sb("onesb", [128, 128], BF16)
        B.memset("dve", self.onesb[:], 1.0, ("cst2",))
        self.onesf = sb("onesf", [128, 128], F32)
        B.memset("dve", self.onesf[:], 1.0, ("cst2",))
        self.epsc = sb("epsc", [128, 1], F32)
        B.memset("dve", self.epsc[:], EPS, ("cst2",))
        self.st = sb("st", [128, 24], F32)

        def col(name, src, n):
            t = sb(name, [128, n // 128], F32)
            with self.nc.allow_non_contiguous_dma(reason="tiny one-time gain/bias column loads"):
                B.dma("sp", t[:], src.rearrange("(c p) -> p c", p=128), "cst", (), ("cst",), allow_slow_non_contiguous=True)
            return t
        self.g_premix = col("g_premix", w["pre_mix_g"], D)
        self.g_preffn = col("g_preffn", w["pre_ffn_g"], D)
        self.g_qn = col("g_qn", w["q_norm_g"], QL)
        self.g_ssdn = col("g_ssdn", w["ssd_norm_g"], SI)
        self.sconv_b = col("sconv_b", w["ssd_conv_b"], CONV_DIM)
        self.fconv_b = col("fconv_b", w["ffn_conv_b"], 2 * DFF)
        self.sconv_w = sb("sconv_w", [128, 4, CONV_DIM // 128], F32)
        self.fconv_w = sb("fconv_w", [128, 3, 2 * DFF // 128], F32)
        self.dcol = sb("dcol", [128, 32], F32)
        with self.nc.allow_non_contiguous_dma(reason="tiny one-time conv tap loads"):
            for k in range(4):
                B.dma("sp", self.sconv_w[:, k, :], w["ssd_conv_w"][k].rearrange("(c p) -> p c", p=128), "cst", (), ("cst",), allow_slow_non_contiguous=True)
            for k in range(3):
                B.dma("sp", self.fconv_w[:, k, :], w["ffn_conv_w"][k].rearrange("(c p) -> p c", p=128), "cst", (), ("cst",), allow_slow_non_contiguous=True)
            dsrc = w["ssd_D"].rearrange("(c two) -> two c", two=2)
            for half in range(2):
                B.dma("sp", self.dcol[half * 64:(half + 1) * 64, :], dsrc[half:half + 1, :].broadcast_to([64, 32]),
                      "cst", (), ("cst",), allow_slow_non_contiguous=True)

        def row(name, src, n, parts=128):
            t = sb(name, [parts, n], F32)
            B.dma("sp", t[:], src.rearrange("(o n) -> o n", o=1).broadcast_to([parts, n]), "cst", (), ("cst",))
            return t
        self.g_kvn_r = row("g_kvn_r", w["kv_norm_g"], KVL)
        self.dtb_r = row("dtb_r", w["ssd_dt_bias"], SH, 64)
        self.alog_r = row("alog_r", w["ssd_A_log"], SH, 64)
        self.a_r = sb("a_r", [64, SH], F32)
        B.act(self.a_r[:], self.alog_r[:], AF.Exp, ("cst",), ("cst2",))
        B.ts("dve", self.a_r[:], self.a_r[:], -1.0, None, ALU.mult, None, ("cst2",), ("cst2",))
        self.hst = sb("hst", [128, SI], F32)
        self.hbf = sb("hbf", [128, SI], BF16)
        self.shist = sb("shist", [128, 48, 3], F32)
        self.fhist = sb("fhist", [128, 88, 2], F32)
        self.uT = sb("uT", [128, 16, 512], BF16)

    def _layout(self):
        X = self.arX = Arena(self, "arenaX", 48 * 1024)
        Y = self.arY = Arena(self, "arenaY", 56 * 1024 - 64)
        X.reset()
        self.attT = X.take([128, H, 512], BF16)
        o = X.off
        self.qnT = X.take([128, H, 512], BF16)
        self.qpeT = X.take([128, 8, 512], BF16)
        self.qpeb = X.take([128, 4, H * ROPE], BF16)
        X.reset(o)
        self.ynT = X.take([128, 32, 512], BF16)
        X.reset()
        self.hdnT = X.take([128, 44, 512], BF16)
        Y.reset()
        self.xin = [Y.take([128, D], F32) for _ in range(2)]
        self.xnb = Y.take([128, 4, D], BF16)
        self.junk = Y.take([128, D], BF16)
        self.KA1 = ("xin0", "xin1", "xnb", "junk")
        Y.reset()
        self.ckvo = [Y.take([128, KVL], F32) for _ in range(2)]
        self.ckvb = Y.take([128, 4, KVL], BF16)
        self.ckvT = Y.take([128, 4, 512], BF16)
        self.kpeo = Y.take([128, 4, ROPE], F32)
        self.kpeb = Y.take([128, 4, 128], BF16)
        self.kpeT = Y.take([128, 512], BF16)
        self.rtmp = Y.take([128, 4, 256], F32)
        self.cqnb = Y.take([128, 4, QL], BF16)
        self.cqnT = Y.take([128, 4, 512], BF16)
        self.kst = [Y.take([128, 512], BF16) for _ in range(4)]
        self.vst = [Y.take([128, 512], BF16) for _ in range(4)]
        self.junk2 = Y.take([128, 512], BF16)
        self.cs = Y.take([128, 2, 4, 32], F32)
        self.KA2 = ("ckvo0", "ckvo1", "ckvb", "ckvT", "kpeo", "kpeb", "kpeT", "rtmp", "cqnb", "cqnT",
                    "kst0", "kst1", "kst2", "kst3", "vst0", "vst1", "vst2", "vst3", "junk2", "cs")
        Y.reset()
        self.kpeK = Y.take([128, SEQ], BF16)
        self.kp_ = [Y.take([128, 1024], BF16) for _ in range(4)]
        self.vp_ = [Y.take([128, 8, 128], BF16) for _ in range(4)]
        self.pT = [Y.take([128, 512], BF16) for _ in range(3)]
        self.rcp = [Y.take([128, 512], F32) for _ in range(2)]
        self.KB = ("kpeK", "kp0", "kp1", "kp2", "kp3", "vp0", "vp1", "vp2", "vp3", "pT0", "pT1", "pT2", "rcp0", "rcp1")
        Y.reset()
        self.dtt = Y.take([64, 8, 64], F32)
        self.atok = Y.take([64, 8, 64], F32)
        self.acum = Y.take([64, 8, 64], F32)
        self.etot = Y.take([128, 8, 64], F32)
        self.decs = Y.take([64, 8, 64], F32)
        self.dtdec = Y.take([64, 8, 64], F32)
        self.xsT = Y.take([128, 4, 512], F32)
        self.BT = Y.take([128, 512], BF16)
        self.CT = Y.take([128, 512], BF16)
        self.pre = [Y.take([128, 516], F32) for _ in range(2)]
        self.acc = [Y.take([128, 512], F32) for _ in range(2)]
        o = Y.off
        self.Btok = [Y.take([64, 128], BF16) for _ in range(2)]
        self.xdt = [Y.take([64, 512], BF16) for _ in range(2)]
        self.xdtd = [Y.take([64, 512], BF16) for _ in range(2)]
        self.cbm = [Y.take([64, 64], F32) for _ in range(2)]
        self.Ebc = [Y.take([128, 512], F32) for _ in range(2)]
        self.seg = [Y.take([64, 512], F32) for _ in range(2)]
        self.MT = [Y.take([64, 8, 64], BF16) for _ in range(2)]
        self.Ce = [Y.take([128, 8, 64], BF16) for _ in range(2)]
        self.abcs = [Y.take([128, 512], F32) for _ in range(2)]
        self.htmp = Y.take([128, 512], F32)
        Y.reset(o)
        self.zs = [Y.take([128, 512], F32) for _ in range(2)]
        self.sq = [Y.take([128, 512], F32) for _ in range(2)]
        self.rt = Y.take([128, 512], F32)
        self.KC = ("dtt", "atok", "acum", "etot", "decs", "dtdec", "xsT0", "xsT1", "xsT2", "xsT3", "xsT4", "xsT5", "xsT6", "xsT7", "BT", "CT", "pre0", "pre1", "acc0", "acc1",
                   "htmp", "zs0", "zs1", "sq0", "sq1", "rt") + tuple(
                       n + str(i) for n in ("Btok", "xdt", "xdtd", "cbm", "Ebc", "seg", "MT", "Ce", "abcs") for i in range(2))
        Y.reset()
        self.smT = Y.take([128, 16, 512], BF16)
        o = Y.off
        self.sg = [Y.take([128, 512], F32) for _ in range(2)]
        Y.reset(o)
        self.xnb1 = Y.take([128, D], BF16)
        self.t12 = [Y.take([128, 512], F32) for _ in range(2)]
        self.mixsb = Y.take([128, D], F32)
        self.xr = Y.take([128, D], F32)
        self.grow = Y.take([128, D], F32)
        self.KD1 = ("smT", "sgk", "t120", "t121", "mixsb", "xr", "grow")
        Y.reset()
        self.fpre = [Y.take([128, 516], F32) for _ in range(4)]
        self.facc = [Y.take([128, 512], F32) for _ in range(4)]
        self.asil = [Y.take([128, 512], F32) for _ in range(4)]
        self.KD2 = ("fpre0", "fpre1", "fpre2", "fpre3", "facc0", "facc1", "facc2", "facc3", "asil0", "asil1", "asil2", "asil3")
        Y.reset()
        self.dnsb = Y.take([128, 4, D], F32)
        self.xr2 = Y.take([128, D], F32)
        self.grow2 = Y.take([128, D], F32)
        self.KD3 = ("dnsb", "xr2", "grow2")
        Y.reset()
        self.stg = Y.take([128, 32, 128], F32)
        self.KALL = tuple(set(self.KA1 + self.KA2 + self.KB + self.KC + self.KD1 + self.KD2 + self.KD3
                              + ("stg", "attT", "qnT", "qpeT", "qpeb", "ynT", "hdnT")))

    def rstd(self, col, n, TP=128):
        st = self.st
        key = "st%d" % col
        self.act(st[0:TP, col:col + 1], st[0:TP, col:col + 1], AF.Sqrt, (key,), (key,), bias=self.epsc[0:TP, 0:1], scale=1.0 / n)
        self.recip(st[0:TP, col:col + 1], st[0:TP, col:col + 1], (key,), (key,))

    def _prompt(self):
        B = self
        self.kT_scr = self.dscr("kT_scr", [H, 128, SEQ])
        self.v_scr = self.dscr("v_scr", [H, 128, SEQ // 128, 128])
        self.kpeT_scr = self.dscr("kpeT_scr", [128, SEQ])
        self.xmid_scr = self.dscr("xmid_scr", [SEQ, D], F32)
        self.acT_scr = self.dscr("acT_scr", [8, SH, 64], F32)
        B.memset("dve", self.hst[:], 0.0, tuple("hst%d" % g for g in range(SG)) + ("hst",))
        B.memset("pool", self.hbf[:], 0.0, tuple("hbf%d" % g for g in range(SG)))
        B.memset("dve", self.shist[:], 0.0, ("shist",))
        B.memset("pool", self.fhist[:], 0.0, ("fhist",))
        for s in range(self.nslot):
            ctx = dict(s=s, T=512, TP=128, xsrc=self.xp[s * 512:(s + 1) * 512, :], pos_tile0=4 * s,
                       o_ckv=self.o_ckvp[s * 512:(s + 1) * 512, :], o_kpe=self.o_kpep[s * 512:(s + 1) * 512, :],
                       o_y=self.o_yp[s * 512:(s + 1) * 512, :], xmid=self.xmid_scr[s * 512:(s + 1) * 512, :],
                       key0=s * 512, tag="p", kT=self.kT_scr, vS=self.v_scr, kpS=self.kpeT_scr, masked=True,
                       last=(s == self.nslot - 1), o_sconv=self.o_sconvp, o_ssd=self.o_ssdp, o_fconv=self.o_fconvp)
            self._slot(ctx)

            if self.stage < 9:
                return
        self._dump_states(self.o_sconvp, self.o_ssdp, self.o_fconvp, "p")

    def _dump_states(self, o_sconv, o_ssd, o_fconv, tag):
        B = self
        B.barrier()
        for q in range(4):
            for k in range(3):
                B.dma("sp", o_sconv[k, q * 1536:(q + 1) * 1536].rearrange("(c p) -> p c", p=128),
                      self.shist[:, q * 12:(q + 1) * 12, k], "o_sconv" + tag, ("shist",), (), allow_slow_non_contiguous=True)
        for q in range(8):
            for k in range(2):
                B.dma("sp", o_fconv[k, q * 1408:(q + 1) * 1408].rearrange("(c p) -> p c", p=128),
                      self.fhist[:, q * 11:(q + 1) * 11, k], "o_fconv" + tag, ("fhist",), (), allow_slow_non_contiguous=True)
        for pc in range(32):
            if pc % 4 == 0:
                b = B.bank()
                pk = "ps%d" % b
            B.tr(self.ps[b][:, (pc % 4) * 128:(pc % 4 + 1) * 128], self.hst[:, pc * 128:(pc + 1) * 128], self.identf,
                 tuple("hst%d" % g for g in range(SG)) + ("hst", "cst"), (pk,))
            if pc % 4 == 3:
                B.cp("act" if (pc // 4) % 2 else "dve", self.stg[:, pc - 3:pc + 1, :],
                     self.ps[b][:, :].rearrange("p (c n) -> p c n", c=4), (pk,), ("stg",))
        B.dma("sp", o_ssd.rearrange("(c p) n -> p c n", p=128), self.stg[:, :, :], "o_ssd" + tag, ("stg",), ())

    def _sample(self):
        B = self
        kT_s = self.dscr("kT_s", [H, 128, PAST + DEC_SEQ])
        v_s = self.dscr("v_s", [H, 128, PAST // 128 + 1, 128])
        kpeT_s = self.dscr("kpeT_s", [128, PAST + DEC_SEQ])
        xmid_s = self.dscr("xmid_s", [DEC_SEQ, D], F32)
        c = dict(s=0, T=DEC_SEQ, TP=64, xsrc=self.xs, pos_tile0=64, o_ckv=self.o_ckvs, o_kpe=self.o_kpes, o_y=self.o_ys,
                 xmid=xmid_s, key0=PAST, tag="s", kT=kT_s, vS=v_s, kpS=kpeT_s, masked=False, last=False)
        B.barrier()
        hk = tuple("hst%d" % g for g in range(SG)) + ("hst",)
        B.dma("sp", self.stg[:, :, :], self.c_ssd.rearrange("(c p) n -> p c n", p=128), "stg_in", (), ("stg",))
        for pc in range(32):
            if pc % 4 == 0:
                b = B.bank()
                pk = "ps%d" % b
            B.tr(self.ps[b][:, (pc % 4) * 128:(pc % 4 + 1) * 128], self.stg[:, pc, :], self.identf, ("stg", "cst"), (pk,))
            if pc % 4 == 3:
                B.cp("act" if (pc // 4) % 2 else "dve", self.hst[:, (pc - 3) * 128:(pc + 1) * 128], self.ps[b][:, :], (pk,), hk)
        B.cp("act", self.hbf[:, :], self.hst[:, :], hk, tuple("hbf%d" % g for g in range(SG)))
        for q in range(4):
            for k in range(3):
                B.dma("sp", self.shist[:, q * 12:(q + 1) * 12, k],
                      self.c_sconv[k, q * 1536:(q + 1) * 1536].rearrange("(c p) -> p c", p=128), "shist_in", (), ("shist",),
                      allow_slow_non_contiguous=True)
        for q in range(8):
            for k in range(2):
                B.dma("sp", self.fhist[:, q * 11:(q + 1) * 11, k],
                      self.c_fconv[k, q * 1408:(q + 1) * 1408].rearrange("(c p) -> p c", p=128), "fhist_in", (), ("fhist",),
                      allow_slow_non_contiguous=True)
        for blk in range(PAST // 512):
            B.barrier()
            for tt in range(4):
                r0 = blk * 512 + tt * 128
                i = B.rot("ckvo", 2)
                co, cok = self.ckvo[i], "ckvo%d" % i
                B.dma("sp", co[:, :], self.c_ckv[r0:r0 + 128, :], "ld_" + cok, (), (cok,))
                B.cp("act" if tt % 2 else "dve", self.ckvb[:, tt, :], co[:, :], (cok,), ("ckvb",))
            B.dma("sp", self.kpeo[:, :, :], self.c_kpe[blk * 512:(blk + 1) * 512, :].rearrange("(t p) d -> p t d", p=128),
                  "ld_kpeo", (), ("kpeo",))
            B.cp("act", self.kpeb[:, :, 0:64], self.kpeo[:, :, :], ("kpeo",), ("kpeb",))
            B.cp("dve", self.kpeb[:, :, 64:128], self.kpeo[:, :, :], ("kpeo",), ("kpeb",))
            for rc in range(4):
                b = B.bank()
                pk = "ps%d" % b
                for tt in range(4):
                    B.tr(B.psb(b)[:, tt * 128:(tt + 1) * 128], self.ckvb[:, tt, rc * 128:(rc + 1) * 128],
                         self.identb[:, :], ("ckvb", "cst2"), (pk,))
                B.cp("act" if rc % 2 else "dve", self.ckvT[:, rc, :], B.psb(b)[:, 0:512], (pk,), ("ckvT",))
            b = B.bank()
            pk = "ps%d" % b
            for tt in range(4):
                B.tr(B.psb(b)[:, tt * 128:(tt + 1) * 128], self.kpeb[:, tt, :], self.identb[:, :], ("kpeb", "cst2"), (pk,))
            B.cp("dve", self.kpeT[:, :], B.psb(b)[:, 0:512], (pk,), ("kpeT",))
            B.dma("pool", kpeT_s[:, blk * 512:(blk + 1) * 512], self.kpeT[:, :], "kpSs", ("kpeT",), ("kpSs",))
            self._s4_expand(c, key0=blk * 512, T=512, TP=128)
        self._slot(c)
        if self.stage < 9:
            return
        self._dump_states(self.o_sconvs, self.o_ssds, self.o_fconvs, "s")

    def _slot(self, c):
        B = self
        B.barrier(self.KD3 + ("hdnT",), self.KA1)
        self._s1_norm(c)
        if self.stage < 2:
            return
        B.barrier(self.KA1, self.KA2 + ("qnT", "qpeT", "qpeb", "attT"))
        self._s2_kv(c)
        if c["tag"] == "p" and c["s"] == 0:
            self._emit_conversions(12, len(self._conv_items))
        if self.stage < 3:
            return
        self._s3_q(c)
        self._s4_expand(c)
        if self.stage < 5:
            return
        B.barrier(self.KA2, self.KB)
        self._s5_attn(c)
        if self.stage < 6:
            return
        B.barrier(self.KB + ("qnT", "qpeT", "qpeb"), self.KC + ("ynT",))
        self._s6_ssd(c)
        if self.stage < 7:
            return
        B.barrier(self.KC, self.KD1)
        self._s7_merge(c)
        if self.stage < 8:
            return
        B.barrier(self.KD1 + ("attT", "ynT"), self.KD2 + ("hdnT",))
        self._s8_ffn_up(c)
        B.barrier(self.KD2, self.KD3)
        self._s9_ffn_down(c)

    def _s1_norm(self, c):
        B = self
        T, TP = c["T"], c["TP"]
        NTT = T // TP
        st, xnb, uT = self.st, self.xnb, self.uT
        for tt in range(NTT):
            i = B.rot("xin", 2)
            xin, xk = self.xin[i], "xin%d" % i
            B.dma("sp", xin[0:TP, :], c["xsrc"][tt * TP:(tt + 1) * TP, :], xk, (), (xk,))
            c0 = tt
            B.act(self.junk[0:TP, :], xin[0:TP, :], AF.Square, (xk,), ("junk", "st%d" % c0), accum=st[0:TP, c0:c0 + 1])
            B.rstd(c0, D, TP)
            B.ts("dve", xnb[0:TP, tt, :], xin[0:TP, :], st[0:TP, c0:c0 + 1], None, ALU.mult, None, (xk, "st%d" % c0), ("xnb",))
        for kc in range(16):
            b = B.bank()
            pk = "ps%d" % b
            for tt in range(NTT):
                B.tr(B.psb(b)[:, tt * TP:(tt + 1) * TP], xnb[0:TP, tt, kc * 128:(kc + 1) * 128], self.identb[0:TP, 0:TP],
                     ("xnb", "cst2"), (pk,))
            if kc % 2 == 0:
                B.ts("dve", uT[:, kc, 0:T], B.psb(b)[:, 0:T], self.g_premix[:, kc:kc + 1], None, ALU.mult, None,
                     (pk, "cst"), ("uT",))
            else:
                B.act(uT[:, kc, 0:T], B.psb(b)[:, 0:T], AF.Copy, (pk, "cst"), ("uT",), scale=self.g_premix[:, kc:kc + 1])
        if "uT" in self.dbg_out and c["last"]:
            B.dma("pool", self.dbg_out["uT"], self.uT[:], "dbg_uT", ("uT",), ())

    def _s2_kv(self, c):
        B = self
        T, TP, tag = c["T"], c["TP"], c["tag"]
        NTT = T // TP
        st, uT = self.st, self.uT
        B.dma("sp", self.cs[0:TP, 0, 0:NTT, :], self.t_cos[0:TP, c["pos_tile0"]:c["pos_tile0"] + NTT, :], "cs", (), ("cs",))
        B.dma("sp", self.cs[0:TP, 1, 0:NTT, :], self.t_sin[0:TP, c["pos_tile0"]:c["pos_tile0"] + NTT, :], "cs", (), ("cs",))
        wck, kck = B.wload("ckv")
        wkp, kkp = B.wload("kpe")
        for tt in range(NTT):
            ba, bb = B.bank(), B.bank()
            pa, pb = "ps%d" % ba, "ps%d" % bb
            psa, psk = self.ps[ba], self.ps[bb]
            for kc in range(16):
                B.mm(psa[0:TP, :], uT[:, kc, tt * TP:(tt + 1) * TP], wck[:, kc, :], kc == 0, kc == 15, ("uT", kck), (pa,))
            for kc in range(16):
                B.mm(psk[0:TP, 0:ROPE], uT[:, kc, tt * TP:(tt + 1) * TP], wkp[:, kc, :], kc == 0, kc == 15, ("uT", kkp), (pb,))
            c1 = 4 + tt
            B.act(self.junk2[0:TP, 0:KVL], psa[0:TP, :], AF.Square, (pa,), ("junk2", "st%d" % c1), accum=st[0:TP, c1:c1 + 1])
            B.rstd(c1, KVL, TP)
            i = B.rot("ckvo", 2)
            co, cok = self.ckvo[i], "ckvo%d" % i
            B.stt("dve", co[0:TP, :], psa[0:TP, :], st[0:TP, c1:c1 + 1], self.g_kvn_r[0:TP, :], ALU.mult, ALU.mult,
                  (pa, "st%d" % c1, "cst"), (cok,))
            B.cp("act", self.ckvb[0:TP, tt, :], co[0:TP, :], (cok,), ("ckvb",))
            B.dma("pool", c["o_ckv"][tt * TP:(tt + 1) * TP, :], co[0:TP, :], "o_" + cok, (cok,), ())
            c_, s_ = self.cs[0:TP, 0, tt, :], self.cs[0:TP, 1, tt, :]
            x1, x2 = psk[0:TP, 0:32], psk[0:TP, 32:64]
            r = self.rtmp
            B.tt("dve", r[0:TP, 0, 0:32], x1, c_, ALU.mult, (pb, "cs"), ("rtmp",))
            B.tt("dve", r[0:TP, 1, 0:32], x2, s_, ALU.mult, (pb, "cs"), ("rtmp",))
            B.tt("dve", r[0:TP, 2, 0:32], x1, s_, ALU.mult, (pb, "cs"), ("rtmp",))
            B.tt("dve", r[0:TP, 3, 0:32], x2, c_, ALU.mult, (pb, "cs"), ("rtmp",))
            B.tt("dve", self.kpeo[0:TP, tt, 0:32], r[0:TP, 0, 0:32], r[0:TP, 1, 0:32], ALU.subtract, ("rtmp",), ("kpeo",))
            B.tt("dve", self.kpeo[0:TP, tt, 32:64], r[0:TP, 2, 0:32], r[0:TP, 3, 0:32], ALU.add, ("rtmp",), ("kpeo",))
            B.cp("act", self.kpeb[0:TP, tt, 0:64], self.kpeo[0:TP, tt, :], ("kpeo",), ("kpeb",))
            B.cp("act", self.kpeb[0:TP, tt, 64:128], self.kpeo[0:TP, tt, :], ("kpeo",), ("kpeb",))
        B.dma("pool", c["o_kpe"].rearrange("(t p) d -> p t d", p=TP), self.kpeo[0:TP, 0:NTT, :], "o_kpe" + tag, ("kpeo",), ())
        for rc in range(4):
            b = B.bank()
            pk = "ps%d" % b
            for tt in range(NTT):
                B.tr(B.psb(b)[:, tt * TP:(tt + 1) * TP], self.ckvb[0:TP, tt, rc * 128:(rc + 1) * 128],
                     self.identb[0:TP, 0:TP], ("ckvb", "cst2"), (pk,))
            B.cp("act" if rc % 2 else "dve", self.ckvT[:, rc, 0:T], B.psb(b)[:, 0:T], (pk,), ("ckvT",))
        b = B.bank()
        pk = "ps%d" % b
        for tt in range(NTT):
            B.tr(B.psb(b)[:, tt * TP:(tt + 1) * TP], self.kpeb[0:TP, tt, :], self.identb[0:TP, 0:TP], ("kpeb", "cst2"), (pk,))
        B.cp("dve", self.kpeT[:, 0:T], B.psb(b)[:, 0:T], (pk,), ("kpeT",))
        B.dma("pool", c["kpS"][:, c["key0"]:c["key0"] + T], self.kpeT[:, 0:T], "kpS" + tag, ("kpeT",), ("kpS" + tag,))
        if "ckvT" in self.dbg_out and c["last"]:
            B.dma("pool", self.dbg_out["ckvT"], self.ckvT[:], "dbg_ckvT", ("ckvT",), ())

    def _s3_q(self, c):
        B = self
        T, TP = c["T"], c["TP"]
        NTT = T // TP
        st, uT = self.st, self.uT
        wq, kq = B.wload("q")
        for tt in range(NTT):
            b = B.bank()
            pk = "ps%d" % b
            for kc in range(16):
                B.mm(self.ps[b][0:TP, :], uT[:, kc, tt * TP:(tt + 1) * TP], wq[:, kc, :], kc == 0, kc == 15, ("uT", kq), (pk,))
            c2 = 8 + tt
            B.act(self.junk2[0:TP, 0:QL], self.ps[b][0:TP, :], AF.Square, (pk,), ("junk2", "st%d" % c2), accum=st[0:TP, c2:c2 + 1])
            B.rstd(c2, QL, TP)
            B.ts("dve", self.cqnb[0:TP, tt, :], self.ps[b][0:TP, :], st[0:TP, c2:c2 + 1], None, ALU.mult, None, (pk, "st%d" % c2), ("cqnb",))
        for rc in range(4):
            b = B.bank()
            pk = "ps%d" % b
            for tt in range(NTT):
                B.tr(B.psb(b)[:, tt * TP:(tt + 1) * TP], self.cqnb[0:TP, tt, rc * 128:(rc + 1) * 128],
                     self.identb[0:TP, 0:TP], ("cqnb", "cst2"), (pk,))
            B.ts("dve", self.cqnT[:, rc, 0:T], B.psb(b)[:, 0:T], self.g_qn[:, rc:rc + 1], None, ALU.mult, None,
                 (pk, "cst"), ("cqnT",))
        wn, kn = B.wload("uqn")
        for h in range(H):
            b = B.bank()
            pk = "ps%d" % b
            for rc in range(4):
                B.mm(self.ps[b][:, 0:T], wn[:, rc, h * 128:(h + 1) * 128], self.cqnT[:, rc, 0:T], rc == 0, rc == 3,
                     ("cqnT", kn), (pk,))
            B.cp("act" if h % 2 else "dve", self.qnT[:, h, 0:T], self.ps[b][:, 0:T], (pk,), ("qnT",))
        wr, kr = B.wload("uqr")
        for tt in range(NTT):
            for cb in range(2):
                b = B.bank()
                pk = "ps%d" % b
                for rc in range(4):
                    B.mm(self.ps[b][0:TP, :], self.cqnT[:, rc, tt * TP:(tt + 1) * TP], wr[:, rc, cb * 512:(cb + 1) * 512],
                         rc == 0, rc == 3, ("cqnT", kr), (pk,))
                pv = self.ps[b][0:TP, :].rearrange("p (h d) -> p h d", h=8)
                x1, x2 = pv[:, :, 0:32], pv[:, :, 32:64]
                c_ = self.cs[0:TP, 0, tt, :].unsqueeze(1).broadcast_to([TP, 8, 32])
                s_ = self.cs[0:TP, 1, tt, :].unsqueeze(1).broadcast_to([TP, 8, 32])
                r = self.rtmp
                rv = [r[0:TP, k, :].rearrange("p (h d) -> p h d", h=8) for k in range(4)]
                B.tt("dve", rv[0], x1, c_, ALU.mult, (pk, "cs"), ("rtmp",))
                B.tt("dve", rv[1], x2, s_, ALU.mult, (pk, "cs"), ("rtmp",))
                B.tt("dve", rv[2], x1, s_, ALU.mult, (pk, "cs"), ("rtmp",))
                B.tt("dve", rv[3], x2, c_, ALU.mult, (pk, "cs"), ("rtmp",))
                qv = self.qpeb[0:TP, tt, cb * 512:(cb + 1) * 512].rearrange("p (h d) -> p h d", h=8)
                B.tt("pool", qv[:, :, 0:32], rv[0], rv[1], ALU.subtract, ("rtmp",), ("qpeb",))
                B.tt("pool", qv[:, :, 32:64], rv[2], rv[3], ALU.add, ("rtmp",), ("qpeb",))
        for hp in range(8):
            b = B.bank()
            pk = "ps%d" % b
            for tt in range(NTT):
                B.tr(B.psb(b)[:, tt * TP:(tt + 1) * TP], self.qpeb[0:TP, tt, hp * 128:(hp + 1) * 128],
                     self.identb[0:TP, 0:TP], ("qpeb", "cst2"), (pk,))
            B.cp("act" if hp % 2 else "dve", self.qpeT[:, hp, 0:T], B.psb(b)[:, 0:T], (pk,), ("qpeT",))

    def _s4_expand(self, c, key0=None, T=None, TP=None):
        B = self
        T = T or c["T"]
        TP = TP or c["TP"]
        key0 = c["key0"] if key0 is None else key0
        tag = c["tag"]
        NTT = T // TP
        wk, kk = B.wload("ukn")
        for h in range(H):
            b = B.bank()
            pk = "ps%d" % b
            for rc in range(4):
                B.mm(self.ps[b][:, 0:T], wk[:, rc, h * 128:(h + 1) * 128], self.ckvT[:, rc, 0:T], rc == 0, rc == 3,
                     ("ckvT", kk), (pk,))
            i = B.rot("kst", 4)
            B.cp("act" if h % 2 else "dve", self.kst[i][:, 0:T], self.ps[b][:, 0:T], (pk,), ("kst%d" % i,))
            B.dma("pool", c["kT"][h, :, key0:key0 + T], self.kst[i][:, 0:T], "kst%d" % i, ("kst%d" % i,), ("kT" + tag,))
        wv, kv = B.wload("ukv")
        for tt in range(NTT):
            kt = (key0 + tt * TP) // 128
            for cb in range(4):
                b = B.bank()
                pk = "ps%d" % b
                for rc in range(4):
                    B.mm(self.ps[b][0:TP, :], self.ckvT[:, rc, tt * TP:(tt + 1) * TP], wv[:, rc, cb * 512:(cb + 1) * 512],
                         rc == 0, rc == 3, ("ckvT", kv), (pk,))
                i = B.rot("vst", 4)
                B.cp("act" if cb % 2 else "dve", self.vst[i][0:TP, :], self.ps[b][0:TP, :], (pk,), ("vst%d" % i,))
                B.dma("pool", c["vS"][4 * cb:4 * cb + 4, 0:TP, kt, :].rearrange("h p d -> p h d"),
                      self.vst[i][0:TP, :].rearrange("p (h d) -> p h d", h=4), "vst%d" % i, ("vst%d" % i,), ("vS" + tag,))

    def _s5_attn(self, c):
        B = self
        T, tag = c["T"], c["tag"]
        nk = c["key0"] + T
        tiles = []
        k = 0
        while k < nk:
            n = min(128, nk - k)
            mi = (k - c["key0"]) // 128 if (c["masked"] and k >= c["key0"]) else None
            tiles.append((k, n, mi))
            k += n
        B.dma("sp", self.kpeK[:, 0:nk], c["kpS"][:, 0:nk], "kpeK", ("kpS" + tag,), ("kpeK",))
        steps = [(h, ti) for h in range(H) for ti in range(len(tiles))]
        state = {}

        def qk(h, ti):
            k0, n, mi = tiles[ti]
            if k0 % 1024 == 0:
                pi = B.rot("kvp", 4)
                npk = min(1024, nk - k0)
                B.dma("sp", self.kp_[pi][:, 0:npk], c["kT"][h, :, k0:k0 + npk], "kp%d" % pi, ("kT" + tag,), ("kp%d" % pi,))
                nkt = (npk + 127) // 128
                pv = min(128, npk)
                B.dma("sp", self.vp_[pi][0:pv, 0:nkt, :], c["vS"][h, 0:pv, k0 // 128:k0 // 128 + nkt, :], "vp%d" % pi,
                      ("vS" + tag,), ("vp%d" % pi,))
                state["pi"] = pi
            pi = state["pi"]
            hb = 64 * (h % 2)
            kl = k0 % 1024
            b = B.rot("abank", 4)
            pk = "ps%d" % b
            B.mm(self.ps[b][0:n, 0:T], self.kp_[pi][:, kl:kl + n], self.qnT[:, h, 0:T], True, False, ("kp%d" % pi, "qnT"), (pk,))
            B.mm(self.ps[b][0:n, 0:T], self.kpeK[hb:hb + 64, k0:k0 + n], self.qpeT[hb:hb + 64, h // 2, 0:T], False, True,
                 ("kpeK", "qpeT"), (pk,))
            return (b, pk, pi, kl)

        DEPTH = 2
        pend = [qk(*steps[i]) for i in range(min(DEPTH, len(steps)))]
        for si, (h, ti) in enumerate(steps):
            b, pk, pi, kl = pend.pop(0)
            if si + DEPTH < len(steps):
                pend.append(qk(*steps[si + DEPTH]))
            k0, n, mi = tiles[ti]
            bo, br = (4, 5) if h % 2 == 0 else (6, 7)
            po, pr = "ps%d" % bo, "ps%d" % br
            i = B.rot("pT", 3)
            pT, ptk = self.pT[i], "pT%d" % i
            B.act(pT[0:n, 0:T], self.ps[b][0:n, 0:T], AF.Exp, (pk,), (ptk,), scale=SCALE)
            if mi is not None:
                B.tt("pool", pT[0:n, 0:T], pT[0:n, 0:T], self.amask[0:n, mi, 0:T], ALU.mult, (ptk, "cst"), (ptk,))
            first, last = ti == 0, ti == len(tiles) - 1
            B.mm(self.ps[bo][:, 0:T], self.vp_[pi][0:n, kl // 128, :], pT[0:n, 0:T], first, last, ("vp%d" % pi, ptk), (po,))
            B.mm(self.ps[br][:, 0:T], self.onesb[0:n, :], pT[0:n, 0:T], first, last, (ptk, "cst2"), (pr,))
            if last:
                ri = B.rot("rcp", 2)
                rc_ = self.rcp[ri]
                B.recip(rc_[:, 0:T], self.ps[br][:, 0:T], (pr,), ("rcp%d" % ri,))
                B.tt("dve", self.attT[:, h, 0:T], self.ps[bo][:, 0:T], rc_[:, 0:T], ALU.mult, (po, "rcp%d" % ri), ("attT",))
        if "attT" in self.dbg_out and c["last"]:
            B.dma("pool", self.dbg_out["attT"], self.attT[:], "dbg_attT", ("attT",), ())

    def _conv(self, b, T, K, wts, bias, hist, fc, pre, prek, acc, acck, eng):
        B = self
        pk = "ps%d" % b
        H_ = K - 1
        B.cp("act", pre[:, H_:H_ + T], self.ps[b][:, 0:T], (pk,), (prek,))
        B.cp("pool", pre[:, 0:H_], hist[:, fc, :], (hist_key(hist, self),), (prek,))
        B.ts(eng, acc[:, 0:T], pre[:, H_:H_ + T], wts[:, K - 1, fc:fc + 1], bias[:, fc:fc + 1], ALU.mult, ALU.add,
             (prek, "cst"), (acck,))
        for k in range(K - 1):
            B.stt(eng, acc[:, 0:T], pre[:, k:k + T], wts[:, k, fc:fc + 1], acc[:, 0:T], ALU.mult, ALU.add,
                  (prek, acck, "cst"), (acck,))
        B.cp("pool", hist[:, fc, :], pre[:, T:T + H_], (prek,), (hist_key(hist, self),))

    def _s6_ssd(self, c):
        B = self
        T = c["T"]
        NCH = T // 64
        uT = self.uT
        dtt, atok, acum, etot, decs, dtdec = self.dtt, self.atok, self.acum, self.etot, self.decs, self.dtdec
        wdt, kdt = B.wload("dt")
        b = B.bank()
        pk = "ps%d" % b
        for ch in range(NCH):
            for kc in range(16):
                B.mm(self.ps[b][0:64, ch * 64:(ch + 1) * 64], uT[:, kc, ch * 64:(ch + 1) * 64], wdt[:, kc, :], kc == 0, kc == 15,
                     ("uT", kdt), (pk,))
        pv = self.ps[b][0:64, 0:NCH * 64].rearrange("p (c h) -> p c h", c=NCH)
        B.tt("dve", dtt[:, 0:NCH, :], pv, self.dtb_r[:, :].unsqueeze(1).broadcast_to([64, NCH, 64]), ALU.add, (pk, "cst"), ("dtt",))
        B.ts("dve", dtt[:, 0:NCH, :], dtt[:, 0:NCH, :], 30.0, None, ALU.min, None, ("dtt",), ("dtt",))
        B.act(dtt[:, 0:NCH, :], dtt[:, 0:NCH, :], AF.Exp, ("dtt",), ("dtt",))
        B.act(dtt[:, 0:NCH, :], dtt[:, 0:NCH, :], AF.Ln, ("dtt",), ("dtt",), bias=self.onesf[0:64, 0:1])
        B.tt("dve", atok[:, 0:NCH, :], dtt[:, 0:NCH, :], self.a_r[:, :].unsqueeze(1).broadcast_to([64, NCH, 64]), ALU.mult,
             ("dtt", "cst2"), ("atok",))
        b = B.bank()
        pk = "ps%d" % b
        for ch in range(NCH):
            B.mm(self.ps[b][0:64, ch * 64:(ch + 1) * 64], self.tri, atok[:, ch, :], True, True, ("atok", "cst"), (pk,))
        B.cp("dve", acum[:, 0:NCH, :], self.ps[b][0:64, 0:NCH * 64].rearrange("p (c h) -> p c h", c=NCH), (pk,), ("acum",))
        b = B.bank()
        pk = "ps%d" % b
        for ch in range(NCH):
            B.mm(self.ps[b][:, ch * 64:(ch + 1) * 64], self.onesf[0:64, :], atok[:, ch, :], True, True, ("atok", "cst2"), (pk,))
        pv = self.ps[b][:, 0:NCH * 64].rearrange("p (c h) -> p c h", c=NCH)
        B.act(etot[:, 0:NCH, :], pv, AF.Exp, (pk,), ("etot",))
        B.tt("dve", decs[:, 0:NCH, :], pv[0:64], acum[:, 0:NCH, :], ALU.subtract, (pk, "acum"), ("decs",))
        B.act(decs[:, 0:NCH, :], decs[:, 0:NCH, :], AF.Exp, ("decs",), ("decs",))
        B.tt("dve", dtdec[:, 0:NCH, :], decs[:, 0:NCH, :], dtt[:, 0:NCH, :], ALU.mult, ("decs", "dtt"), ("dtdec",))
        b = B.bank()
        pk = "ps%d" % b
        for ch in range(NCH):
            B.tr(self.ps[b][0:64, ch * 64:(ch + 1) * 64], acum[:, ch, :], self.identf[0:64, 0:64], ("acum", "cst"), (pk,))
        B.cp("act", atok[:, 0:NCH, :], self.ps[b][0:64, 0:NCH * 64].rearrange("p (c h) -> p c h", c=NCH), (pk,), ("atok",))
        B.dma("pool", self.acT_scr[0:NCH].rearrange("c h i -> h c i"), atok[:, 0:NCH, :], "acT_w", ("atok",), ("acT",))

        xk_all = tuple("xsT%d" % ch for ch in range(8))
        for g in range(SG):
            wx, kx = B.wload("xs", g)
            wb_, kb_ = B.wload("bm", g)
            wc_, kc_ = B.wload("cm", g)
            plan = [(wx, kx, j * 128, 4 * g + j, ("xs", j)) for j in range(4)]
            plan += [(wb_, kb_, 0, 32 + g, ("B", 0)), (wc_, kc_, 0, 40 + g, ("C", 0))]
            for (wt_, wk_, cl, fc, (kind, j)) in plan:
                b = B.bank()
                pk = "ps%d" % b
                for kc in range(16):
                    B.mm(self.ps[b][:, 0:T], wt_[:, kc, cl:cl + 128], uT[:, kc, 0:T], kc == 0, kc == 15, ("uT", wk_), (pk,))
                i = B.rot("pre", 2)
                eng = "dve"
                B._conv(b, T, 4, self.sconv_w, self.sconv_b, self.shist, fc, self.pre[i], "pre%d" % i, self.acc[i], "acc%d" % i, eng)
                if kind == "xs":
                    B.act(self.xsT[:, j, 0:T], self.acc[i][:, 0:T], AF.Silu, ("acc%d" % i,), xk_all)
                elif kind == "B":
                    B.act(self.BT[:, 0:T], self.acc[i][:, 0:T], AF.Silu, ("acc%d" % i,), ("BT",))
                else:
                    B.act(self.CT[:, 0:T], self.acc[i][:, 0:T], AF.Silu, ("acc%d" % i,), ("CT",))
            hs = slice(g * 512, (g + 1) * 512)
            hk, hbk = "hst%d" % g, "hbf%d" % g

            def prep(ch, g=g):
                q = ch % 2
                cs_ = slice(ch * 64, (ch + 1) * 64)
                K_ = lambda n: n + str(q)
                b = B.bank()
                pk = "ps%d" % b
                B.mm(self.ps[b][0:64, 0:64], self.BT[:, cs_], self.CT[:, cs_], True, True, ("BT", "CT"), (pk,))
                B.tt("dve", self.cbm[q][:, :], self.ps[b][0:64, 0:64], self.tri, ALU.mult, (pk, "cst"), (K_("cbm"),))
                ab, abk = self.abcs[q], K_("abcs")
                B.dma("sp", ab[:, :], self.acT_scr[ch, 8 * g:8 * g + 8, :].rearrange("h i -> (h i)").rearrange("(o n) -> o n", o=1)
                      .broadcast_to([128, 512]), abk, ("acT",), (abk,))
                B.act(self.Ebc[q][:, :], ab[:, :], AF.Exp, (abk,), (K_("Ebc"),))
                B.tt("dve", self.seg[q][:, :].rearrange("p (h i) -> p h i", h=8), ab[0:64, :].rearrange("p (h i) -> p h i", h=8),
                     acum[:, ch, 8 * g:8 * g + 8].unsqueeze(2).broadcast_to([64, 8, 64]), ALU.subtract, (abk, "acum"), (K_("seg"),))
                B.ts("dve", self.seg[q][:, :], self.seg[q][:, :], 0.0, None, ALU.min, None, (K_("seg"),), (K_("seg"),))
                B.act(self.seg[q][:, :], self.seg[q][:, :], AF.Exp, (K_("seg"),), (K_("seg"),))
                B.tt("dve", self.MT[q][:, :, :], self.seg[q][:, :].rearrange("p (h i) -> p h i", h=8),
                     self.cbm[q][:, :].unsqueeze(1).broadcast_to([64, 8, 64]), ALU.mult, (K_("seg"), K_("cbm")), (K_("MT"),))
                B.tt("pool", self.Ce[q][:, :, :], self.Ebc[q][:, :].rearrange("p (h i) -> p h i", h=8),
                     self.CT[:, cs_].unsqueeze(1).broadcast_to([128, 8, 64]), ALU.mult, (K_("Ebc"), "CT"), (K_("Ce"),))
                bt = B.bank()
                pt = "ps%d" % bt
                for j in range(4):
                    B.tr(self.ps[bt][0:64, j * 128:(j + 1) * 128], self.xsT[:, j, cs_], self.identf, ("xsT%d" % ch, "cst"), (pt,))
                pv = self.ps[bt][0:64, :].rearrange("p (h d) -> p h d", h=8)
                B.tt("dve", self.xdt[q][:, :].rearrange("p (h d) -> p h d", h=8), pv,
                     dtt[:, ch, 8 * g:8 * g + 8].unsqueeze(2).broadcast_to([64, 8, 64]), ALU.mult, (pt, "dtt"), (K_("xdt"),))
                B.tt("dve", self.xdtd[q][:, :].rearrange("p (h d) -> p h d", h=8), pv,
                     dtdec[:, ch, 8 * g:8 * g + 8].unsqueeze(2).broadcast_to([64, 8, 64]), ALU.mult, (pt, "dtdec"), (K_("xdtd"),))
                b2 = B.bank()
                p2 = "ps%d" % b2
                B.tr(B.psb(b2)[0:64, 0:128], self.BT[:, cs_], self.identb[:, :], ("BT", "cst2"), (p2,))
                B.cp("act", self.Btok[q][:, :], B.psb(b2)[0:64, 0:128], (p2,), (K_("Btok"),))

            def yst(ch, g=g):
                q = ch % 2
                cs_ = slice(ch * 64, (ch + 1) * 64)
                K_ = lambda n: n + str(q)
                by = B.bank()
                py = "ps%d" % by
                for hh in range(8):
                    h = 8 * g + hh
                    o_ = self.ps[by][64 * (hh % 2):64 * (hh % 2) + 64, (hh // 2) * 64:(hh // 2 + 1) * 64]
                    B.mm(o_, self.xdt[q][:, hh * 64:(hh + 1) * 64], self.MT[q][:, hh, :], True, False, (K_("xdt"), K_("MT")), (py,))
                    B.mm(o_, self.hbf[:, h * 64:(h + 1) * 64], self.Ce[q][:, hh, :], False, True, (hbk, K_("Ce")), (py,))
                B.tt("pool", self.xsT[:, :, cs_], self.xsT[:, :, cs_], self.dcol[:, 4 * g:4 * g + 4].unsqueeze(2).broadcast_to([128, 4, 64]),
                     ALU.mult, ("xsT%d" % ch, "cst"), ("xsT%d" % ch,))
                B.tt("dve", self.xsT[:, :, cs_], self.xsT[:, :, cs_], self.ps[by][:, 0:256].rearrange("p (j i) -> p j i", j=4),
                     ALU.add, ("xsT%d" % ch, py), ("xsT%d" % ch,))
                bs_ = B.bank()
                ps_ = "ps%d" % bs_
                B.mm(self.ps[bs_][:, :], self.Btok[q][:, :], self.xdtd[q][:, :], True, True, (K_("Btok"), K_("xdtd")), (ps_,))
                B.tt("pool", self.htmp[:, :].rearrange("p (h d) -> p h d", h=8), self.hst[:, hs].rearrange("p (h d) -> p h d", h=8),
                     etot[:, ch, 8 * g:8 * g + 8].unsqueeze(2).broadcast_to([128, 8, 64]), ALU.mult, (hk, "etot"), ("htmp",))
                B.tt("dve", self.hst[:, hs], self.htmp[:, :], self.ps[bs_][:, :], ALU.add, ("htmp", ps_), (hk,))
                B.cp("act", self.hbf[:, hs], self.hst[:, hs], (hk,), (hbk,))

            prep(0)
            for ch in range(NCH):
                if ch + 1 < NCH:
                    prep(ch + 1)
                yst(ch)
            if "yscan" in self.dbg_out and c["last"]:
                B.dma("pool", self.dbg_out["yscan"][:, 4 * g:4 * g + 4, :], self.xsT[:, :, :], "dbg_yscan", xk_all, ())
            wz, kz = B.wload("z", g)
            bn = B.bank()
            pn = "ps%d" % bn
            for j in range(4):
                b = B.bank()
                pk = "ps%d" % b
                for kc in range(16):
                    B.mm(self.ps[b][:, 0:T], wz[:, kc, j * 128:(j + 1) * 128], uT[:, kc, 0:T], kc == 0, kc == 15, ("uT", kz), (pk,))
                i = B.rot("zs", 2)
                B.act(self.zs[i][:, 0:T], self.ps[b][:, 0:T], AF.Silu, (pk,), ("zs%d" % i,))
                B.tt("dve", self.xsT[:, j, 0:T], self.xsT[:, j, 0:T], self.zs[i][:, 0:T], ALU.mult, xk_all + ("zs%d" % i,), xk_all)
                B.act(self.sq[i][:, 0:T], self.xsT[:, j, 0:T], AF.Square, xk_all, ("sq%d" % i,))
                B.mm(self.ps[bn][:, 0:T], self.onesf[:, :], self.sq[i][:, 0:T], j == 0, j == 3, ("sq%d" % i, "cst2"), (pn,))
            B.act(self.rt[:, 0:T], self.ps[bn][:, 0:T], AF.Sqrt, (pn,), ("rt",), bias=self.epsc[:, 0:1], scale=1.0 / 512)
            B.recip(self.rt[:, 0:T], self.rt[:, 0:T], ("rt",), ("rt",))
            for j in range(4):
                pc = 4 * g + j
                B.stt("dve", self.ynT[:, pc, 0:T], self.xsT[:, j, 0:T], self.g_ssdn[:, pc:pc + 1], self.rt[:, 0:T],
                      ALU.mult, ALU.mult, xk_all + ("rt", "cst"), ("ynT",))
        if "ynT" in self.dbg_out and c["last"]:
            B.dma("pool", self.dbg_out["ynT"], self.ynT[:], "dbg_ynT", ("ynT",), ())

    def _s7_merge(self, c):
        B = self
        T, TP = c["T"], c["TP"]
        NTT = T // TP
        uT, st = self.uT, self.st
        t1buf = self.mixsb[:, :].rearrange("p (j t) -> p j t", j=4)
        sga = self.xr[:, :].rearrange("p (j t) -> p j t", j=4)
        sgb = self.grow[:, :].rearrange("p (j t) -> p j t", j=4)
        for cb in range(4):
            wga, kga = B.wload("gate", cb)
            for j in range(4):
                cl = j * 128
                b3 = B.bank()
                p3 = "ps%d" % b3
                for kc in range(16):
                    B.mm(self.ps[b3][:, 0:T], wga[:, kc, cl:cl + 128], uT[:, kc, 0:T], kc == 0, kc == 15, ("uT", kga), (p3,))
                B.act(sga[:, j, 0:T], self.ps[b3][:, 0:T], AF.Sigmoid, (p3,), ("xr",))
            wm, km = B.wload("omla", cb)
            for j in range(4):
                cl = j * 128
                b1 = B.bank()
                p1 = "ps%d" % b1
                for kc in range(16):
                    B.mm(self.ps[b1][:, 0:T], wm[:, kc, cl:cl + 128], self.attT[:, kc, 0:T], kc == 0, kc == 15, ("attT", km), (p1,))
                B.tt("dve", t1buf[:, j, 0:T], self.ps[b1][:, 0:T], sga[:, j, 0:T], ALU.mult, (p1, "xr"), ("mixsb",))
            wgb, kgb = B.wload("gate", 4 + cb)
            for j in range(4):
                cl = j * 128
                b4 = B.bank()
                p4 = "ps%d" % b4
                for kc in range(16):
                    B.mm(self.ps[b4][:, 0:T], wgb[:, kc, cl:cl + 128], uT[:, kc, 0:T], kc == 0, kc == 15, ("uT", kgb), (p4,))
                B.act(sgb[:, j, 0:T], self.ps[b4][:, 0:T], AF.Sigmoid, (p4,), ("grow",))
            banks = [B.bank() for _ in range(4)]
            for kp in range(2):
                ws, ks = B.wload("ossd", cb, kp)
                for j in range(4):
                    cl = j * 128
                    b2 = banks[j]
                    for kc in range(16):
                        B.mm(self.ps[b2][:, 0:T], ws[:, kc, cl:cl + 128], self.ynT[:, kp * 16 + kc, 0:T],
                             kp == 0 and kc == 0, kp == 1 and kc == 15, ("ynT", ks), ("ps%d" % b2,))
            for j in range(4):
                dc = 4 * cb + j
                b2 = banks[j]
                i = B.rot("t12", 2)
                B.tt("dve", self.t12[i][:, 0:T], self.ps[b2][:, 0:T], sgb[:, j, 0:T], ALU.mult, ("ps%d" % b2, "grow"), ("t12%d" % i,))
                B.tt("pool", self.smT[:, dc, 0:T], t1buf[:, j, 0:T], self.t12[i][:, 0:T], ALU.add, ("mixsb", "t12%d" % i), ("smT",))
        if "smT" in self.dbg_out and c["last"]:
            B.dma("pool", self.dbg_out["smT"], self.smT[:], "dbg_smT", ("smT",), ())
        B.dma("sp", self.grow[:, :], self.w["post_mix_g"].rearrange("(o n) -> o n", o=1).broadcast_to([128, D]), "grow", (), ("grow",))
        for tt in range(NTT):
            tsl = slice(tt * TP, (tt + 1) * TP)
            B.dma("sp", self.xr[0:TP, :], c["xsrc"][tsl, :], "xr", (), ("xr",))
            for cb in range(4):
                wo, ko = B.wload("out", cb)
                b = B.bank()
                pk = "ps%d" % b
                for kc in range(16):
                    B.mm(self.ps[b][0:TP, :], self.smT[:, kc, tsl], wo[:, kc, :], kc == 0, kc == 15, ("smT", ko), (pk,))
                B.cp("act", self.mixsb[0:TP, cb * 512:(cb + 1) * 512], self.ps[b][0:TP, :], (pk,), ("mixsb",))
            c3, c4 = 12 + tt, 16 + tt
            B.act(self.xnb1[0:TP, :], self.mixsb[0:TP, :], AF.Square, ("mixsb",), ("sgk", "st%d" % c3), accum=st[0:TP, c3:c3 + 1])
            B.rstd(c3, D, TP)
            B.stt("dve", self.mixsb[0:TP, :], self.mixsb[0:TP, :], st[0:TP, c3:c3 + 1], self.grow[0:TP, :], ALU.mult, ALU.mult,
                  ("mixsb", "st%d" % c3, "grow"), ("mixsb",))
            B.tt("dve", self.xr[0:TP, :], self.xr[0:TP, :], self.mixsb[0:TP, :], ALU.add, ("xr", "mixsb"), ("xr",))
            B.dma("pool", c["xmid"][tsl, :], self.xr[0:TP, :], "xmid_w", ("xr",), ("xmid" + c["tag"],))
            B.act(self.xnb1[0:TP, :], self.xr[0:TP, :], AF.Square, ("xr",), ("sgk", "st%d" % c4), accum=st[0:TP, c4:c4 + 1])
            B.rstd(c4, D, TP)
            B.ts("dve", self.xnb1[0:TP, :], self.xr[0:TP, :], st[0:TP, c4:c4 + 1], None, ALU.mult, None, ("xr", "st%d" % c4), ("sgk",))
            for half in range(2):
                b = B.bank()
                pk = "ps%d" % b
                for k8 in range(8):
                    kc = half * 8 + k8
                    B.tr(B.psb(b)[:, k8 * TP:(k8 + 1) * TP], self.xnb1[0:TP, kc * 128:(kc + 1) * 128], self.identb[0:TP, 0:TP],
                         ("sgk", "cst2"), (pk,))
                B.tt("dve", uT[:, half * 8:half * 8 + 8, tsl], B.psb(b)[:, 0:8 * TP].rearrange("p (k t) -> p k t", k=8),
                     self.g_preffn[:, half * 8:half * 8 + 8].unsqueeze(2).broadcast_to([128, 8, TP]), ALU.mult, (pk, "cst"), ("uT",))
        if "xnT" in self.dbg_out and c["last"]:
            B.dma("pool", self.dbg_out["xnT"], self.uT[:], "dbg_xnT", ("uT",), ())

    def _s8_ffn_up(self, c):
        B = self
        T = c["T"]
        uT = self.uT
        for cb1 in range(11):
            for half in range(2):
                wt_, wk_ = B.wload("up", cb1 + 11 * half)
                for j in range(4):
                    fc = 4 * cb1 + j
                    cl = j * 128
                    b = B.bank()
                    pk = "ps%d" % b
                    for kc in range(16):
                        B.mm(self.ps[b][:, 0:T], wt_[:, kc, cl:cl + 128], uT[:, kc, 0:T], kc == 0, kc == 15, ("uT", wk_), (pk,))
                    i = B.rot("fpre", 4)
                    B._conv(b, T, 3, self.fconv_w, self.fconv_b, self.fhist, fc + 44 * half, self.fpre[i], "fpre%d" % i,
                            self.facc[i], "facc%d" % i, "dve")
                    if half == 0:
                        B.act(self.asil[j][:, 0:T], self.facc[i][:, 0:T], AF.Silu, ("facc%d" % i,), ("asil%d" % j,))
                    else:
                        B.tt("pool", self.hdnT[:, fc, 0:T], self.asil[j][:, 0:T], self.facc[i][:, 0:T], ALU.mult,
                             ("asil%d" % j, "facc%d" % i), ("hdnT",))
        if "hdnT" in self.dbg_out and c["last"]:
            B.dma("pool", self.dbg_out["hdnT"], self.hdnT[:], "dbg_hdnT", ("hdnT",), ())

    def _s9_ffn_down(self, c):
        B = self
        T, TP, tag = c["T"], c["TP"], c["tag"]
        NTT = T // TP
        st = self.st
        B.dma("sp", self.grow2[:, :], self.w["post_ffn_g"].rearrange("(o n) -> o n", o=1).broadcast_to([128, D]), "grow2", (), ("grow2",))
        for cb in range(4):
            banks = [B.bank() for _ in range(NTT)]
            for kp in range(4):
                wd, kd = B.wload("down", cb, kp)
                for tt in range(NTT):
                    b = banks[tt]
                    for kcl in range(11):
                        fc = kp * 11 + kcl
                        B.mm(self.ps[b][0:TP, :], self.hdnT[:, fc, tt * TP:(tt + 1) * TP], wd[:, kcl, :], fc == 0, fc == 43,
                             ("hdnT", kd), ("ps%d" % b,))
            for tt in range(NTT):
                b = banks[tt]
                B.cp("act" if tt % 2 else "dve", self.dnsb[0:TP, tt, cb * 512:(cb + 1) * 512], self.ps[b][0:TP, :], ("ps%d" % b,), ("dnsb",))
        for tt in range(NTT):
            tsl = slice(tt * TP, (tt + 1) * TP)
            B.dma("sp", self.xr2[0:TP, :], c["xmid"][tsl, :], "xr2", ("xmid" + tag,), ("xr2",))
            c5 = 20 + tt
            B.act(self.hdnT[0:TP, 0:4, :].rearrange("p a b -> p (a b)"), self.dnsb[0:TP, tt, :], AF.Square, ("dnsb",), ("hdnT", "st%d" % c5),
                  accum=st[0:TP, c5:c5 + 1])
            B.rstd(c5, D, TP)
            B.stt("dve", self.dnsb[0:TP, tt, :], self.dnsb[0:TP, tt, :], st[0:TP, c5:c5 + 1], self.grow2[0:TP, :], ALU.mult, ALU.mult,
                  ("dnsb", "st%d" % c5, "grow2"), ("dnsb",))
            B.tt("dve", self.xr2[0:TP, :], self.xr2[0:TP, :], self.dnsb[0:TP, tt, :], ALU.add, ("xr2", "dnsb"), ("xr2",))
            B.dma("pool", c["o_y"][tsl, :], self.xr2[0:TP, :], "o_y" + tag, ("xr2",), ())


def hist_key(hist, B):
    return "shist" if hist is B.shist else "fhist"


def build_nc(**kw):
    return Builder(**kw).build()


def _tables():
    half = 32
    inv = (np.float32(10000.0) ** (-np.arange(half, dtype=np.float32) / np.float32(half))).astype(np.float32)
    pos = np.concatenate([np.arange(SEQ), PAST + np.arange(DEC_SEQ), np.zeros(64)]).astype(np.float32)
    ang = pos[:, None] * inv[None, :]
    cos = np.cos(ang).astype(np.float32).reshape(65, 128, 32).transpose(1, 0, 2)
    sin = np.sin(ang).astype(np.float32).reshape(65, 128, 32).transpose(1, 0, 2)
    k = np.arange(512)[:, None] // 64
    q = np.arange(512)[None, :] // 64
    am = (k <= q).astype(np.float32).reshape(4, 128, 512).transpose(1, 0, 2)
    misc = np.zeros((128, 1024), np.float32)
    misc[:, 0:128] = np.eye(128, dtype=np.float32)
    j = np.arange(64)[:, None]
    i = np.arange(64)[None, :]
    misc[0:64, 128:192] = (j <= i)
    misc[0:64, 192:704] = np.tile((j <= i).astype(np.float32), (1, 8))
    misc[0:64, 704:768] = np.where(j > i, NEG, 0.0)
    return dict(t_cos=np.ascontiguousarray(cos), t_sin=np.ascontiguousarray(sin),
                t_amask=np.ascontiguousarray(am), t_misc=misc)


def make_in_maps(inputs):
    tabs = _tables()
    f = lambda a: np.ascontiguousarray(np.asarray(a, dtype=np.float32))
    wnames = ["pre_mix_g", "w_in", "q_norm_g", "w_uq", "kv_norm_g", "w_ukv", "ssd_conv_w", "ssd_conv_b", "ssd_dt_bias",
              "ssd_A_log", "ssd_D", "ssd_norm_g", "w_o_mla", "w_o_ssd", "w_out", "post_mix_g", "pre_ffn_g", "w_up",
              "ffn_conv_w", "ffn_conv_b", "w_down", "post_ffn_g"]
    shared = {n: f(inputs[n][0]) for n in wnames}
    shared.update(tabs)
    maps = []
    zero_prompt = np.zeros((SEQ, D), np.float32)
    for c in range(8):
        m = dict(shared)
        m["xp"] = f(inputs["x_prompt"][PROMPT_CORES.index(c)]) if c in PROMPT_CORES else zero_prompt
        m["xs"] = f(inputs["x_sample"][c])
        m["c_ckv"] = f(inputs["cache_mla_ckv"][0, c])
        m["c_kpe"] = f(inputs["cache_mla_kpe"][0, c])
        m["c_sconv"] = f(inputs["state_ssd_conv"][0, c])
        m["c_ssd"] = f(inputs["state_ssd"][0, c]).reshape(SH * SP_, SN)
        m["c_fconv"] = f(inputs["state_ffn_conv"][0, c])
        maps.append(m)
    return maps


def kernel(**inputs):
    nc = build_nc()
    res = run_bass_kernel_spmd(nc, make_in_maps(inputs), core_ids=list(range(8)))
    r = res.results
    st = lambda name, cores: np.stack([r[c][name] for c in cores])[None]
    P4, S8 = PROMPT_CORES, range(8)
    return (st("o_yp", P4)[0], st("o_ys", S8)[0],
            st("o_ckvp", P4), st("o_kpep", P4), st("o_sconvp", P4),
            st("o_ssdp", P4).reshape(1, 4, SH, SP_, SN), st("o_fconvp", P4),
            st("o_ckvs", S8), st("o_kpes", S8), st("o_sconvs", S8),
            st("o_ssds", S8).reshape(1, 8, SH, SP_, SN), st("o_fconvs", S8))
```
